# Optimizing a Trainium2 kernel written in Bass

```python
import math
import jax, jax.numpy as jnp
from jax import lax
import numpy as np

D_MODEL = 2048
BATCH = 4
SEQ = 2048
DEPTH = 4

GRID_W = 64
CTX_LEN = 256
HEAD_DIM = 128
NA_HEADS = 6
CV_GROUPS = 4
GD_HEADS = 6
NA_D = NA_HEADS * HEAD_DIM
CV_D = CV_GROUPS * HEAD_DIM
GD_D = GD_HEADS * HEAD_DIM
D_MIX = NA_D + CV_D + GD_D
WIN_R = 8
WIN_C = 16
CV_KSIZE = 3
GD_KSIZE = 5
GD_CHUNK = 64
ROPE_BASE = 10000.0
D_FF = 5632
N_ADA = 9
SPLIT_SIZES = (3 * NA_D, 3 * CV_D, 3 * GD_D, GD_D, 2 * GD_HEADS, 2 * GD_HEADS)
N_IN = sum(SPLIT_SIZES)
ALPHA = (2 * DEPTH) ** 0.25
BETA_INIT = (8 * DEPTH) ** -0.25
LN_EPS = 1e-6
NEG_INF = -1e30

kernel_name = 'hybrid_natten_shortconv_gdn_dit'


def _ln_plain(x):
    xf = x.astype(jnp.float32)
    mu = xf.mean(-1, keepdims=True)
    var = jnp.square(xf - mu).mean(-1, keepdims=True)
    return ((xf - mu) * lax.rsqrt(var + LN_EPS)).astype(x.dtype)


def _post_norm(x, y, g, b):
    return _ln_plain(ALPHA * x + y) * g + b


def _modulate(x, shift, scale):
    return _ln_plain(x) * (1.0 + scale) + shift


def _rms(x):
    xf = x.astype(jnp.float32)
    return (xf * lax.rsqrt(jnp.mean(xf * xf, -1, keepdims=True) + LN_EPS)).astype(x.dtype)


def _l2norm(x):
    xf = x.astype(jnp.float32)
    return (xf * lax.rsqrt(jnp.sum(xf * xf, -1, keepdims=True) + LN_EPS)).astype(x.dtype)


def _swiglu(h, w_gu, w_down):
    gate, up = jnp.split(h @ w_gu, 2, axis=-1)
    return (jax.nn.silu(gate) * up) @ w_down


def _dwconv(x, w):
    k = w.shape[0]
    return lax.conv_general_dilated(x, w[:, None, :], window_strides=(1,), padding=[(k // 2, k // 2)],
                                    dimension_numbers=('NWC', 'WIO', 'NWC'), feature_group_count=x.shape[-1])


def _heads(a, n):
    b, t, _ = a.shape
    return a.reshape(b, t, n, HEAD_DIM).transpose(0, 2, 1, 3)


def _merge(a):
    b, h, t, d = a.shape
    return a.transpose(0, 2, 1, 3).reshape(b, t, h * d)


def _axial_angles(n_tok):
    t = jnp.arange(n_tok, dtype=jnp.int32)
    row = (t // GRID_W).astype(jnp.float32)
    col = (t % GRID_W).astype(jnp.float32)
    n_freq = HEAD_DIM // 4
    inv_freq = ROPE_BASE ** (-jnp.arange(n_freq, dtype=jnp.float32) / n_freq)
    return row[:, None] * inv_freq, col[:, None] * inv_freq


def _rotate(x, ang):
    x1, x2 = jnp.split(x, 2, axis=-1)
    cos, sin = jnp.cos(ang).astype(x.dtype), jnp.sin(ang).astype(x.dtype)
    return jnp.concatenate([x1 * cos - x2 * sin, x1 * sin + x2 * cos], axis=-1)


def _axial_rope(x, ang_r, ang_c):
    xr, xc = jnp.split(x, 2, axis=-1)
    return jnp.concatenate([_rotate(xr, ang_r), _rotate(xc, ang_c)], axis=-1)


def _dense_attention(q, k, v):
    s = jnp.einsum('bhqd,bhkd->bhqk', q, k).astype(jnp.float32)
    p = jax.nn.softmax(s, axis=-1).astype(v.dtype)
    return jnp.einsum('bhqk,bhkd->bhqd', p, v)


def _neighbourhood_attention(q, k, v, k_ctx, v_ctx, rpb):
    b, h, t, dh = q.shape
    rows = t // GRID_W
    wr = min(WIN_R, rows)
    r = jnp.arange(rows)
    row_idx = jnp.clip(r - wr // 2, 0, rows - wr)[:, None] + jnp.arange(wr)[None, :]
    cc = jnp.arange(GRID_W)
    col_start = jnp.clip(cc - WIN_C // 2, 0, GRID_W - WIN_C)
    col_ok = (cc[None, :] >= col_start[:, None]) & (cc[None, :] < col_start[:, None] + WIN_C)
    dr = row_idx - r[:, None] + WIN_R - 1
    dc = jnp.clip(cc[None, :] - cc[:, None] + WIN_C - 1, 0, 2 * WIN_C - 2)
    bias = rpb[:, dr[:, None, :, None], dc[None, :, None, :]].astype(jnp.float32)
    qg = q.reshape(b, h, rows, GRID_W, dh)
    kb = k.reshape(b, h, rows, GRID_W, dh)[:, :, row_idx]
    vb = v.reshape(b, h, rows, GRID_W, dh)[:, :, row_idx]
    s_loc = jnp.einsum('bhrqd,bhrwkd->bhrqwk', qg, kb).astype(jnp.float32) + bias
    s_loc = jnp.where(col_ok[:, None, :], s_loc, NEG_INF).reshape(b, h, rows, GRID_W, wr * GRID_W)
    s_ctx = jnp.einsum('bhrqd,bhld->bhrql', qg, k_ctx).astype(jnp.float32)
    p = jax.nn.softmax(jnp.concatenate([s_loc, s_ctx], axis=-1), axis=-1).astype(v.dtype)
    n_loc = wr * GRID_W
    o = (jnp.einsum('bhrqn,bhrnd->bhrqd', p[..., :n_loc], vb.reshape(b, h, rows, n_loc, dh))
         + jnp.einsum('bhrql,bhld->bhrqd', p[..., n_loc:], v_ctx))
    return o.reshape(b, h, t, dh)


def _gated_delta_chunked(q, k, v, g, beta, state):
    out_dtype = v.dtype
    q, k, v, g, beta, state = [a.astype(jnp.float32) for a in (q, k, v, g, beta, state)]
    b, h, t, dk = q.shape
    dv = v.shape[-1]
    c = GD_CHUNK
    n = t // c
    q, k, v = [a.reshape(b, h, n, c, a.shape[-1]) for a in (q, k, v)]
    g, beta = g.reshape(b, h, n, c), beta.reshape(b, h, n, c)
    gc = jnp.cumsum(g, axis=-1)
    incl = jnp.tril(jnp.ones((c, c), bool))
    strict = jnp.tril(jnp.ones((c, c), bool), -1)
    decay = jnp.where(incl, jnp.exp(jnp.where(incl, gc[..., :, None] - gc[..., None, :], 0.0)), 0.0)
    kb = k * beta[..., None]
    a_mat = jnp.where(strict, jnp.einsum('bhncd,bhnjd->bhncj', kb, k) * decay, 0.0) + jnp.eye(c, dtype=jnp.float32)
    rhs = jnp.concatenate([v * beta[..., None], kb * jnp.exp(gc)[..., None]], axis=-1)
    sol = lax.linalg.triangular_solve(a_mat, rhs, left_side=True, lower=True, unit_diagonal=True)
    u, w = sol[..., :dv], sol[..., dv:]
    attn = jnp.where(incl, jnp.einsum('bhncd,bhnjd->bhncj', q, k) * decay, 0.0)
    q_dec = q * jnp.exp(gc)[..., None]
    k_tail = k * jnp.exp(gc[..., -1:] - gc)[..., None]
    g_tot = jnp.exp(gc[..., -1])

    def step(s, xs):
        q_i, k_i, u_i, w_i, a_i, gt_i = xs
        v_new = u_i - jnp.einsum('bhcd,bhde->bhce', w_i, s)
        o = jnp.einsum('bhcd,bhde->bhce', q_i, s) + jnp.einsum('bhcj,bhje->bhce', a_i, v_new)
        s = s * gt_i[..., None, None] + jnp.einsum('bhcd,bhce->bhde', k_i, v_new)
        return s, o

    xs = [jnp.moveaxis(a, 2, 0) for a in (q_dec, k_tail, u, w, attn, g_tot)]
    state, o = lax.scan(step, state, xs)
    o = jnp.moveaxis(o, 0, 2).reshape(b, h, t, dv)
    return o.astype(out_dtype), state


def _gdn_inputs(qkv, a_lin, b_lin, conv_w, a_log, dt_bias, ang):
    qkv = jax.nn.silu(_dwconv(qkv, conv_w))
    q, k, v = [_heads(a, GD_HEADS) for a in jnp.split(qkv, 3, axis=-1)]
    q, k = _l2norm(q), _l2norm(k)
    if ang is not None:
        q, k = _axial_rope(q, *ang), _axial_rope(k, *ang)
    q = q * HEAD_DIM ** -0.5
    b, t, _ = a_lin.shape
    a_lin = a_lin.reshape(b, t, 2, GD_HEADS).transpose(2, 0, 3, 1)
    b_lin = b_lin.reshape(b, t, 2, GD_HEADS).transpose(2, 0, 3, 1)
    g = -jnp.exp(a_log.astype(jnp.float32))[:, None, :, None] * jax.nn.softplus(
        (a_lin + dt_bias[:, None, :, None]).astype(jnp.float32))
    beta = jax.nn.sigmoid(b_lin.astype(jnp.float32))
    return q, k, v, g, beta


def _bidir_gdn(q, k, v, g, beta, s_f, s_b):
    o_f, s_f = _gated_delta_chunked(q, k, v, g[0], beta[0], s_f)
    fl = lambda a: jnp.flip(a, axis=2)
    o_b, s_b = _gated_delta_chunked(fl(q), fl(k), fl(v), fl(g[1]), fl(beta[1]), s_b)
    return o_f + fl(o_b), s_f, s_b


def _gdn_out(o, z, norm_w):
    b, h, t, d = o.shape
    o = o.transpose(0, 2, 1, 3)
    o = _rms(o) * norm_w * jax.nn.silu(z.reshape(b, t, h, d))
    return o.reshape(b, t, h * d)


def _mixer(hx, hc, w_in, rpb, cv_w, gd_conv_w, gd_a_log, gd_dt_bias, gd_norm_w, w_out, ang, ctx_out):
    split_at = np.cumsum(SPLIT_SIZES)[:-1].tolist()
    na_x, cv_x, qkv_x, z_x, beta_x, dec_x = jnp.split(hx @ w_in, split_at, axis=-1)
    na_c, cv_c, qkv_c, z_c, beta_c, dec_c = jnp.split(hc @ w_in, split_at, axis=-1)
    scale = HEAD_DIM ** -0.5
    qx, kx, vx = [_heads(a, NA_HEADS) for a in jnp.split(na_x, 3, axis=-1)]
    qc, kc, vc = [_heads(a, NA_HEADS) for a in jnp.split(na_c, 3, axis=-1)]
    o_na_x = _merge(_neighbourhood_attention(qx * scale, kx, vx, kc, vc, rpb))
    bx, cx, ux = jnp.split(cv_x, 3, axis=-1)
    o_cv_x = bx * _dwconv(cx * ux, cv_w)
    gq_c, gk_c, gv_c, g_c, be_c = _gdn_inputs(qkv_c, dec_c, beta_c, gd_conv_w, gd_a_log, gd_dt_bias, None)
    zeros = jnp.zeros(gq_c.shape[:2] + (HEAD_DIM, HEAD_DIM), jnp.float32)
    o_gd_c, s_f, s_b = _bidir_gdn(gq_c, gk_c, gv_c, g_c, be_c, zeros, zeros)
    gq_x, gk_x, gv_x, g_x, be_x = _gdn_inputs(qkv_x, dec_x, beta_x, gd_conv_w, gd_a_log, gd_dt_bias, ang)
    o_gd_x, _, _ = _bidir_gdn(gq_x, gk_x, gv_x, g_x, be_x, s_f, s_b)
    yx = jnp.concatenate([o_na_x, o_cv_x, _gdn_out(o_gd_x, z_x, gd_norm_w)], axis=-1) @ w_out
    if not ctx_out:
        return yx, None
    o_na_c = _merge(_dense_attention(qc * scale, kc, vc))
    bc, ccg, uc = jnp.split(cv_c, 3, axis=-1)
    o_cv_c = bc * _dwconv(ccg * uc, cv_w)
    yc = jnp.concatenate([o_na_c, o_cv_c, _gdn_out(o_gd_c, z_c, gd_norm_w)], axis=-1) @ w_out
    return yx, yc


def setup_inputs(seed: int = 0) -> dict:
    key = jax.random.key(seed)
    ks = jax.random.split(key, 20)
    f32 = jnp.float32
    d = D_MODEL
    nrm = lambda k, shape, s: s * jax.random.normal(k, shape, f32)
    dt = jnp.exp(jax.random.uniform(ks[16], (DEPTH, 2, GD_HEADS), f32, math.log(1e-3), math.log(1e-1)))
    return {
        'x': nrm(ks[0], (BATCH, SEQ, d), 1.0),
        'c': nrm(ks[1], (BATCH, d), 1.0),
        'ctx': nrm(ks[2], (BATCH, CTX_LEN, d), 1.0),
        'c_ctx': nrm(ks[3], (d,), 1.0),
        'w_ada': nrm(ks[4], (DEPTH, d, N_ADA * d), 0.5 * d ** -0.5),
        'b_ada': nrm(ks[5], (DEPTH, N_ADA * d), 0.01),
        'ln_g': 1.0 + nrm(ks[6], (DEPTH, 3, d), 0.02),
        'ln_b': nrm(ks[7], (DEPTH, 3, d), 0.02),
        'ffn1_w_gu': nrm(ks[8], (DEPTH, d, 2 * D_FF), d ** -0.5),
        'ffn1_w_down': nrm(ks[9], (DEPTH, D_FF, d), BETA_INIT * D_FF ** -0.5),
        'w_in': nrm(ks[10], (DEPTH, d, N_IN), d ** -0.5),
        'na_rpb': nrm(ks[11], (DEPTH, NA_HEADS, 2 * WIN_R - 1, 2 * WIN_C - 1), 0.1),
        'cv_conv_w': nrm(ks[12], (DEPTH, CV_KSIZE, CV_D), CV_KSIZE ** -0.5),
        'gd_conv_w': nrm(ks[13], (DEPTH, GD_KSIZE, 3 * GD_D), GD_KSIZE ** -0.5),
        'gd_a_log': jnp.log(jax.random.uniform(ks[14], (DEPTH, 2, GD_HEADS), f32, 1.0, 16.0)),
        'gd_dt_bias': dt + jnp.log(-jnp.expm1(-dt)),
        'gd_norm_w': 1.0 + nrm(ks[15], (DEPTH, HEAD_DIM), 0.02),
        'w_out': nrm(ks[17], (DEPTH, D_MIX, d), BETA_INIT * D_MIX ** -0.5),
        'ffn2_w_gu': nrm(ks[18], (DEPTH, d, 2 * D_FF), d ** -0.5),
        'ffn2_w_down': nrm(ks[19], (DEPTH, D_FF, d), BETA_INIT * D_FF ** -0.5),
    }


def reference(x, c, ctx, c_ctx, w_ada, b_ada, ln_g, ln_b, ffn1_w_gu, ffn1_w_down, w_in, na_rpb, cv_conv_w,
              gd_conv_w, gd_a_log, gd_dt_bias, gd_norm_w, w_out, ffn2_w_gu, ffn2_w_down):
    ang = _axial_angles(x.shape[1])
    silu_c = jax.nn.silu(c)
    silu_cc = jax.nn.silu(c_ctx)
    for l in range(DEPTH):
        last = l == DEPTH - 1
        ada_x = [a[:, None, :] for a in jnp.split(silu_c @ w_ada[l] + b_ada[l], N_ADA, axis=-1)]
        ada_c = jnp.split(silu_cc @ w_ada[l] + b_ada[l], N_ADA, axis=-1)
        hx = _modulate(x, ada_x[0], ada_x[1])
        hc = _modulate(ctx, ada_c[0], ada_c[1])
        x = _post_norm(x, 0.5 * ada_x[2] * _swiglu(hx, ffn1_w_gu[l], ffn1_w_down[l]), ln_g[l, 0], ln_b[l, 0])
        ctx = _post_norm(ctx, 0.5 * ada_c[2] * _swiglu(hc, ffn1_w_gu[l], ffn1_w_down[l]), ln_g[l, 0], ln_b[l, 0])
        hx = _modulate(x, ada_x[3], ada_x[4])
        hc = _modulate(ctx, ada_c[3], ada_c[4])
        yx, yc = _mixer(hx, hc, w_in[l], na_rpb[l], cv_conv_w[l], gd_conv_w[l], gd_a_log[l], gd_dt_bias[l],
                        gd_norm_w[l], w_out[l], ang, not last)
        x = _post_norm(x, ada_x[5] * yx, ln_g[l, 1], ln_b[l, 1])
        hx = _modulate(x, ada_x[6], ada_x[7])
        x = _post_norm(x, 0.5 * ada_x[8] * _swiglu(hx, ffn2_w_gu[l], ffn2_w_down[l]), ln_g[l, 2], ln_b[l, 2])
        if not last:
            ctx = _post_norm(ctx, ada_c[5] * yc, ln_g[l, 1], ln_b[l, 1])
            hc = _modulate(ctx, ada_c[6], ada_c[7])
            ctx = _post_norm(ctx, 0.5 * ada_c[8] * _swiglu(hc, ffn2_w_gu[l], ffn2_w_down[l]), ln_g[l, 2], ln_b[l, 2])
    return x
```

```python
import numpy as np
from contextlib import ExitStack
import ml_dtypes
import concourse.bass as bass
import concourse.mybir as mybir
from concourse.bass_utils import run_bass_kernel_spmd

F32 = mybir.dt.float32
BF16 = mybir.dt.bfloat16
AF = mybir.ActivationFunctionType
ALU = mybir.AluOpType
AX = mybir.AxisListType

D = 2048
KC = 16
DFF = 5632
DEPTH = 4
ALPHA = (2 * DEPTH) ** 0.25
EPS = 1e-6
T = 1152
GROUPS = [(0, 512, 0), (512, 512, 0), (1024, 128, 1)]

ENGS = ('pe', 'act', 'dve', 'pool', 'sp')


class Buf:
    __slots__ = ('name', 'writers', 'readers', 'dsem', 'dcount')

    def __init__(self, name):
        self.name = name
        self.writers = []
        self.readers = []
        self.dsem = None
        self.dcount = 0


class Op:
    __slots__ = ('eng', 'fn', 'deps', 'marked', 'sem', 'val', 'is_dma', 'grp')

    def __init__(self, eng, fn, deps, is_dma=False):
        self.eng = eng
        self.fn = fn
        self.deps = deps
        self.marked = False
        self.sem = None
        self.val = None
        self.is_dma = is_dma
        self.grp = None


class Prog:
    def __init__(self, nc, stack):
        self.nc = nc
        self.stack = stack
        self.lists = {e: [] for e in ENGS}
        self.esem = {e: stack.enter_context(nc.semaphore('s_' + e)) for e in ENGS}
        self.nbuf = 0

    def buf(self, name=None):
        self.nbuf += 1
        return Buf((name or 'b') + str(self.nbuf))

    def sb(self, name, shape, dt):
        return self.stack.enter_context(self.nc.sbuf_tensor(name, list(shape), dt))

    def ps(self, name, shape, dt=F32):
        return self.stack.enter_context(self.nc.psum_tensor(name, list(shape), dt))

    def _deps(self, reads, writes):
        deps = []
        for b in reads:
            deps.extend(b.writers)
        for b in writes:
            deps.extend(b.readers)
            deps.extend(b.writers)
        return deps

    def _commit(self, op, reads, writes):
        for b in reads:
            b.readers.append(op)
        for b in writes:
            if b.readers:
                b.readers = []
                b.writers = [op]
            else:
                b.writers.append(op)

    def op(self, eng, fn, reads=(), writes=()):
        o = Op(eng, fn, self._deps(reads, writes))
        self._commit(o, reads, writes)
        self.lists[eng].append(o)
        return o

    def dma(self, q, fn, reads=(), writes=(), sembuf=None):
        if sembuf is None:
            sembuf = writes[0]
        if sembuf.dsem is None:
            sembuf.dsem = self.stack.enter_context(self.nc.semaphore('d_' + sembuf.name))
        o = Op(q, fn, self._deps(reads, writes), is_dma=True)
        sembuf.dcount += 1
        o.sem = sembuf.dsem
        o.val = 16 * sembuf.dcount
        o.marked = True
        self._commit(o, reads, writes)
        self.lists[q].append(o)
        return o

    @staticmethod
    def group(ops):
        v = max(o.val for o in ops)
        for o in ops:
            o.val = v
            o.grp = ops[0]

    def emit(self, final_waits=()):
        nc = self.nc
        for e in ENGS:
            for o in self.lists[e]:
                for d in o.deps:
                    if d.eng == 'pe' and o.eng == 'pe' and not d.is_dma and not o.is_dma:
                        continue
                    d.marked = True
        for e in ENGS:
            c = 0
            for o in self.lists[e]:
                if o.is_dma:
                    continue
                if o.marked:
                    c += 1
                    o.sem = self.esem[e]
                    o.val = c
        lists = self.lists
        fw = {}
        for o in final_waits:
            k = id(o.sem)
            if k not in fw or fw[k][1] < o.val:
                fw[k] = (o.sem, o.val)

        def run(e, eng):
            wm = {}
            for o in lists[e]:
                need = {}
                for d in o.deps:
                    if d.eng == 'pe' and e == 'pe' and not d.is_dma and not o.is_dma:
                        continue
                    if o.grp is not None and d.grp is o.grp:
                        continue
                    k = id(d.sem)
                    if wm.get(k, 0) >= d.val:
                        continue
                    if k not in need or need[k][1] < d.val:
                        need[k] = (d.sem, d.val)
                for k, (s, v) in need.items():
                    eng.wait_ge(s, v)
                    wm[k] = v
                ins = o.fn(eng)
                if o.is_dma:
                    ins.then_inc(o.sem, 16)
                elif o.marked:
                    ins.then_inc(o.sem, 1)
            if e == 'sp':
                for k, (s, v) in fw.items():
                    eng.wait_ge(s, v)

        with nc.Block() as block:
            @block.tensor
            def _(eng):
                run('pe', eng)

            @block.scalar
            def _(eng):
                run('act', eng)

            @block.vector
            def _(eng):
                run('dve', eng)

            @block.gpsimd
            def _(eng):
                run('pool', eng)

            @block.sync
            def _(eng):
                run('sp', eng)


class PsPool:
    def __init__(self, p, n=8):
        self.t = [p.ps('psb%d' % i, [128, 512]) for i in range(n)]
        self.b = [p.buf('psb') for _ in range(n)]
        self.i = 0
        self.n = n

    def next(self):
        i = self.i
        self.i = (i + 1) % self.n
        return self.t[i], self.b[i]


def build_ada():
    nc = bass.Bass("TRN2", target_bir_lowering=False)
    cT = nc.dram_tensor("cT", [128, KC * 5], F32, kind="ExternalInput").ap()
    w = nc.dram_tensor("w", [DEPTH, D, 2304], F32, kind="ExternalInput").ap()
    b = nc.dram_tensor("b", [128, 72], F32, kind="ExternalInput").ap()
    out = nc.dram_tensor("out", [128, 72 * 5], F32, kind="ExternalOutput").ap()
    with ExitStack() as st:
        p = Prog(nc, st)
        c_sb = p.sb("c_sb", [128, KC, 5], F32)
        s_sb = p.sb("s_sb", [128, KC, 5], F32)
        b_sb = p.sb("b_sb", [128, 72], F32)
        o_sb = p.sb("o_sb", [128, 72, 5], F32)
        NWB = 3
        wb = [p.sb("wb%d" % i, [128, KC, 128], F32) for i in range(NWB)]
        ps = p.ps("ps", [128, 72, 5])
        Bc, Bs, Bb, Bo, Bps, Bout = [p.buf(n) for n in ('c', 's', 'b', 'o', 'ps', 'out')]
        Bw = [p.buf('w') for _ in range(NWB)]
        p.dma('sp', lambda e: e.dma_start(out=c_sb[:].rearrange("p k r -> p (k r)"), in_=cT), writes=[Bc])
        p.dma('sp', lambda e: e.dma_start(out=b_sb[:], in_=b), writes=[Bb])
        p.op('act', lambda e: e.activation(out=s_sb[:], in_=c_sb[:], func=AF.Silu), reads=[Bc], writes=[Bs])
        idx = 0
        for l in range(DEPTH):
            for m in range(18):
                src = w[l, :, m * 128:(m + 1) * 128].rearrange("(k p) n -> p k n", p=128)
                t = wb[idx % NWB]
                q = 'sp' if idx % 2 == 0 else 'act'
                p.dma(q, lambda e, t=t, src=src: e.dma_start(out=t[:], in_=src), writes=[Bw[idx % NWB]])
                for kc in range(KC):
                    p.op('pe', lambda e, t=t, kc=kc, idx=idx: e.matmul(
                        ps[:, idx, :], lhsT=t[:, kc, :], rhs=s_sb[:, kc, :], start=(kc == 0), stop=(kc == KC - 1)),
                        reads=[Bw[idx % NWB], Bs], writes=[Bps])
                idx += 1
        for r in range(5):
            p.op('dve', lambda e, r=r: e.tensor_tensor(out=o_sb[:, :, r], in0=ps[:, :, r], in1=b_sb[:], op=ALU.add),
                 reads=[Bps, Bb], writes=[Bo])
        o = p.dma('sp', lambda e: e.dma_start(out=out, in_=o_sb[:].rearrange("p m r -> p (m r)")), reads=[Bo],
                  writes=[Bout])
        p.emit(final_waits=[o])
    return nc


class TokCtx:
    pass


def ln_stats(p, c, src_sb, srcB):
    for g, (s0, n, kind) in enumerate(GROUPS):
        sl = slice(s0, s0 + n)
        p.op('dve', lambda e, sl=sl, n=n: e.tensor_tensor(out=c.acc1[:, :n], in0=src_sb[:, 0, sl], in1=src_sb[:, 1, sl],
                                                           op=ALU.add),
             reads=[srcB[0][g], srcB[1][g]], writes=[c.Bacc1])
        for k in range(2, KC):
            p.op('dve', lambda e, sl=sl, n=n, k=k: e.tensor_tensor(out=c.acc1[:, :n], in0=c.acc1[:, :n],
                                                                    in1=src_sb[:, k, sl], op=ALU.add),
                 reads=[srcB[k][g], c.Bacc1], writes=[c.Bacc1])
        p.op('act', lambda e, sl=sl, n=n: e.activation(out=c.acc2[:, :n], in_=src_sb[:, 0, sl], func=AF.Square),
             reads=[srcB[0][g]], writes=[c.Bacc2])
        for k in range(1, KC):
            sq, Bsq = c.sq[k % 2], c.Bsq[k % 2]
            p.op('act', lambda e, sl=sl, n=n, k=k, sq=sq: e.activation(out=sq[:, :n], in_=src_sb[:, k, sl],
                                                                       func=AF.Square),
                 reads=[srcB[k][g]], writes=[Bsq])
            p.op('dve', lambda e, n=n, sq=sq: e.tensor_tensor(out=c.acc2[:, :n], in0=c.acc2[:, :n], in1=sq[:, :n],
                                                              op=ALU.add),
                 reads=[Bsq, c.Bacc2], writes=[c.Bacc2])
        ps1, B1 = c.pp.next()
        ps2, B2 = c.pp.next()
        p.op('pe', lambda e, n=n, ps1=ps1: e.matmul(ps1[:, :n], lhsT=c.ones[:], rhs=c.acc1[:, :n], start=True, stop=True),
             reads=[c.Bones, c.Bacc1], writes=[B1])
        p.op('pe', lambda e, n=n, ps2=ps2: e.matmul(ps2[:, :n], lhsT=c.ones[:], rhs=c.acc2[:, :n], start=True, stop=True),
             reads=[c.Bones, c.Bacc2], writes=[B2])
        p.op('act', lambda e, n=n, ps1=ps1: e.mul(out=c.mean[:, :n], in_=ps1[:, :n], mul=1.0 / D),
             reads=[B1], writes=[c.Bmean])
        p.op('dve', lambda e, n=n: e.tensor_tensor(out=c.msq[:, :n], in0=c.mean[:, :n], in1=c.mean[:, :n], op=ALU.mult),
             reads=[c.Bmean], writes=[c.Bmsq])
        p.op('dve', lambda e, n=n, ps2=ps2: e.scalar_tensor_tensor(out=c.msq[:, :n], in0=ps2[:, :n], scalar=1.0 / D,
                                                                   in1=c.msq[:, :n], op0=ALU.mult, op1=ALU.subtract),
             reads=[B2, c.Bmsq], writes=[c.Bmsq])
        p.op('act', lambda e, n=n: e.activation(out=c.msq[:, :n], in_=c.msq[:, :n], func=AF.Sqrt, bias=c.eps[:, 0:1],
                                                scale=1.0),
             reads=[c.Bmsq, c.Bones], writes=[c.Bmsq])
        p.op('dve', lambda e, sl=sl, n=n: e.reciprocal(out=c.rstd[:, sl], in_=c.msq[:, :n]),
             reads=[c.Bmsq], writes=[c.Brstd[g]])
        p.op('dve', lambda e, sl=sl, n=n: e.scalar_tensor_tensor(out=c.nmr[:, sl], in0=c.mean[:, :n], scalar=-1.0,
                                                                  in1=c.rstd[:, sl], op0=ALU.mult, op1=ALU.mult),
             reads=[c.Bmean, c.Brstd[g]], writes=[c.Bnmr[g]])


def ln_apply(p, c, src_sb, srcB, dst_sb, dstB, scale_ap, bias_ap, parB):
    i = 0
    for g, (s0, n, kind) in enumerate(GROUPS):
        sl = slice(s0, s0 + n)
        for k in range(KC):
            t1, Bt1 = c.t1[i % 2], c.Bt1[i % 2]
            i += 1
            p.op('dve', lambda e, sl=sl, n=n, k=k, t1=t1: e.tensor_tensor(out=t1[:, :n], in0=src_sb[:, k, sl],
                                                                           in1=c.rstd[:, sl], op=ALU.mult),
                 reads=[srcB[k][g], c.Brstd[g]], writes=[Bt1])
            p.op('dve', lambda e, sl=sl, n=n, t1=t1: e.tensor_tensor(out=t1[:, :n], in0=t1[:, :n], in1=c.nmr[:, sl],
                                                                      op=ALU.add),
                 reads=[Bt1, c.Bnmr[g]], writes=[Bt1])
            p.op('act', lambda e, sl=sl, n=n, k=k, t1=t1, kind=kind: e.activation(
                out=dst_sb[:, k, sl], in_=t1[:, :n], func=AF.Identity, scale=scale_ap(kind, k), bias=bias_ap(kind, k)),
                reads=[Bt1, parB], writes=[dstB[k][g]])


def tok_common(p, nc):
    c = TokCtx()
    c.pp = PsPool(p, 8)
    c.ones = p.sb("ones", [128, 128], F32)
    c.eps = p.sb("eps", [128, 1], F32)
    c.Bones = p.buf('ones')
    p.op('dve', lambda e: e.memset(c.ones[:], 1.0), writes=[c.Bones])
    p.op('dve', lambda e: e.memset(c.eps[:], EPS), writes=[c.Bones])
    for nm in ('acc1', 'acc2', 'mean', 'msq'):
        setattr(c, nm, p.sb(nm, [128, 512], F32))
        setattr(c, 'B' + nm, p.buf(nm))
    c.sq = [p.sb("sq%d" % i, [128, 512], F32) for i in range(2)]
    c.Bsq = [p.buf('sq') for _ in range(2)]
    c.t1 = [p.sb("t1_%d" % i, [128, 512], F32) for i in range(2)]
    c.Bt1 = [p.buf('t1') for _ in range(2)]
    c.rstd = p.sb("rstd", [128, T], F32)
    c.nmr = p.sb("nmr", [128, T], F32)
    c.Brstd = [p.buf('rstd') for _ in GROUPS]
    c.Bnmr = [p.buf('nmr') for _ in GROUPS]
    return c


NPH = 4
PHC = 11


def build_pf(prefix, nph=NPH, dbg=None, mixh=False):
    nc = bass.Bass("TRN2", target_bir_lowering=False)
    xT = nc.dram_tensor("xT", [D, T], F32, kind="ExternalInput").ap()
    ada = nc.dram_tensor("ada", [128, 2 * 3 * KC], F32, kind="ExternalInput").ap()
    lnp = nc.dram_tensor("lnp", [128, 2 * KC], F32, kind="ExternalInput").ap()
    if dbg is None:
        wgu = nc.dram_tensor("wgu", [D, 2 * DFF], F32, kind="ExternalInput").ap()
        wdn = nc.dram_tensor("wdn", [DFF, D], F32, kind="ExternalInput").ap()
    if prefix:
        omT = nc.dram_tensor("omT", [D, T], BF16, kind="ExternalInput").ap()
        wo = nc.dram_tensor("wo", [D, D], F32, kind="ExternalInput").ap()
        adam = nc.dram_tensor("adam", [128, 2 * KC], F32, kind="ExternalInput").ap()
        lnpm = nc.dram_tensor("lnpm", [128, 2 * KC], F32, kind="ExternalInput").ap()
    yT = nc.dram_tensor("yT", [D, T], F32, kind="ExternalOutput").ap()
    if mixh:
        adah = nc.dram_tensor("adah", [128, 2 * 2 * KC], F32, kind="ExternalInput").ap()
        hTo = nc.dram_tensor("hT", [D, T], BF16, kind="ExternalOutput").ap()
    NG = len(GROUPS)
    with ExitStack() as st:
        p = Prog(nc, st)
        c = tok_common(p, nc)
        pp = c.pp
        x_sb = p.sb("x_sb", [128, KC, T], F32)
        h_sb = p.sb("h_sb", [128, KC, T], BF16)
        a_sb = p.sb("a_sb", [128, PHC, T], BF16)
        xB = [[p.buf('x') for _ in range(NG)] for _ in range(KC)]
        hB = [[p.buf('h') for _ in range(NG)] for _ in range(KC)]
        aB = [[p.buf('a') for _ in range(NG)] for _ in range(PHC)]
        ada_sb = p.sb("ada_sb", [128, 2, 3, KC], F32)
        sc1p = p.sb("sc1p", [128, 2, KC], F32)
        hgate = p.sb("hgate", [128, 2, KC], F32)
        lnp_sb = p.sb("lnp_sb", [128, 2, KC], F32)
        Bpar = p.buf('par')
        NWG = 3
        wg_sb = [p.sb("wg%d" % i, [128, 2, KC, 128], BF16) for i in range(NWG)]
        Bwg = [p.buf('wg') for _ in range(NWG)]
        NWD = 3
        wd_sb = [p.sb("wd%d" % i, [128, PHC, 128], BF16) for i in range(NWD)]
        Bwd = [p.buf('wd') for _ in range(NWD)]
        sg_sb = [p.sb("sg%d" % i, [128, 512], F32) for i in range(2)]
        Bsg = [p.buf('sg') for _ in range(2)]
        Bout = p.buf('out')

        lo = []
        for q4 in range(4):
            src = xT[q4 * 512:(q4 + 1) * 512, :].rearrange("(k p) t -> p k t", p=128)
            lo.append(p.dma('sp', lambda e, q4=q4, src=src: e.dma_start(out=x_sb[:, 4 * q4:4 * q4 + 4, :], in_=src),
                            writes=[xB[k][g] for k in range(4 * q4, 4 * q4 + 4) for g in range(NG)], sembuf=xB[4 * q4][0]))
        po = [p.dma('sp', lambda e: e.dma_start(out=ada_sb[:].rearrange("p a b k -> p (a b k)"), in_=ada), writes=[Bpar]),
              p.dma('sp', lambda e: e.dma_start(out=lnp_sb[:].rearrange("p a k -> p (a k)"), in_=lnp), writes=[Bpar])]
        if prefix:
            adam_sb = p.sb("adam_sb", [128, 2, KC], F32)
            lnpm_sb = p.sb("lnpm_sb", [128, 2, KC], F32)
            po.append(p.dma('sp', lambda e: e.dma_start(out=adam_sb[:].rearrange("p a k -> p (a k)"), in_=adam),
                            writes=[Bpar]))
            po.append(p.dma('sp', lambda e: e.dma_start(out=lnpm_sb[:].rearrange("p a k -> p (a k)"), in_=lnpm),
                            writes=[Bpar]))
            for q4 in range(4):
                src = omT[q4 * 512:(q4 + 1) * 512, :].rearrange("(k p) t -> p k t", p=128)
                p.dma('sp', lambda e, q4=q4, src=src: e.dma_start(out=h_sb[:, 4 * q4:4 * q4 + 4, :], in_=src),
                      writes=[hB[k][g] for k in range(4 * q4, 4 * q4 + 4) for g in range(NG)], sembuf=hB[4 * q4][0])
        Prog.group(po)
        p.op('dve', lambda e: e.tensor_scalar_add(out=sc1p[:], in0=ada_sb[:, :, 1, :], scalar1=1.0),
             reads=[Bpar], writes=[Bpar])
        p.op('dve', lambda e: e.tensor_scalar_mul(out=hgate[:], in0=ada_sb[:, :, 2, :], scalar1=0.5),
             reads=[Bpar], writes=[Bpar])

        wq = []
        if prefix:
            for m in range(KC):
                wq.append(('wo', m))
        for ph in range(nph if dbg is None else 0):
            for j in range(PHC):
                wq.append(('gu', ph * PHC + j))
            for m in range(KC):
                wq.append(('dn', (ph, m)))
        cnt = {'g': 0, 'd': 0}
        slot = {}
        occ = {}
        cur = [0]

        def can_issue(i):
            kind, a = wq[i]
            if kind in ('wo', 'gu'):
                key = ('g', cnt['g'] % NWG)
            else:
                key = ('d', cnt['d'] % NWD)
            return key not in occ or occ[key] < cur[0]

        def issue(i):
            kind, a = wq[i]
            if kind == 'wo':
                s = cnt['g'] % NWG
                cnt['g'] += 1
                src = wo[:, a * 128:(a + 1) * 128].rearrange("(k p) n -> p k n", p=128)
                p.dma('pool', lambda e, s=s, src=src: e.dma_start(out=wg_sb[s][:, 0, :, :], in_=src), writes=[Bwg[s]])
            elif kind == 'gu':
                s = cnt['g'] % NWG
                cnt['g'] += 1
                sg_ = wgu[:, a * 128:(a + 1) * 128].rearrange("(k p) n -> p k n", p=128)
                su_ = wgu[:, DFF + a * 128:DFF + (a + 1) * 128].rearrange("(k p) n -> p k n", p=128)
                o1 = p.dma('pool', lambda e, s=s, src=sg_: e.dma_start(out=wg_sb[s][:, 0, :, :], in_=src), writes=[Bwg[s]])
                o2 = p.dma('pool', lambda e, s=s, src=su_: e.dma_start(out=wg_sb[s][:, 1, :, :], in_=src), writes=[Bwg[s]])
                Prog.group([o1, o2])
            else:
                ph, m = a
                s = cnt['d'] % NWD
                cnt['d'] += 1
                src = wdn[ph * PHC * 128:(ph + 1) * PHC * 128, m * 128:(m + 1) * 128].rearrange("(k p) n -> p k n", p=128)
                p.dma('pool', lambda e, s=s, src=src: e.dma_start(out=wd_sb[s][:], in_=src), writes=[Bwd[s]])
            slot[i] = s
            occ[('g' if kind in ('wo', 'gu') else 'd', s)] = i

        nxt = [0]

        def prefetch(upto):
            cur[0] = upto - 2
            while nxt[0] < len(wq) and nxt[0] <= upto and can_issue(nxt[0]):
                issue(nxt[0])
                nxt[0] += 1

        wi = 0

        def scale_x_alpha():
            for g, (s0, n, kind) in enumerate(GROUPS):
                for k in range(KC):
                    p.op('act', lambda e, k=k, s0=s0, n=n: e.mul(out=x_sb[:, k, s0:s0 + n], in_=x_sb[:, k, s0:s0 + n],
                                                                 mul=ALPHA),
                         reads=[xB[k][g]], writes=[xB[k][g]])

        if prefix:
            scale_x_alpha()
            for m in range(KC):
                prefetch(wi + 2)
                s = slot[wi]
                wi += 1
                for g, (s0, n, kind) in enumerate(GROUPS):
                    pt, pB = pp.next()
                    for kc in range(KC):
                        p.op('pe', lambda e, pt=pt, s=s, kc=kc, s0=s0, n=n: e.matmul(
                            pt[:, :n], lhsT=wg_sb[s][:, 0, kc, :], rhs=h_sb[:, kc, s0:s0 + n],
                            start=(kc == 0), stop=(kc == KC - 1)),
                            reads=[Bwg[s], hB[kc][g]], writes=[pB])
                    p.op('dve', lambda e, pt=pt, m=m, s0=s0, n=n, kind=kind: e.scalar_tensor_tensor(
                        out=x_sb[:, m, s0:s0 + n], in0=pt[:, :n], scalar=adam_sb[:, kind, m:m + 1],
                        in1=x_sb[:, m, s0:s0 + n], op0=ALU.mult, op1=ALU.add),
                        reads=[pB, Bpar, xB[m][g]], writes=[xB[m][g]])
            ln_stats(p, c, x_sb, xB)
            ln_apply(p, c, x_sb, xB, x_sb, xB, lambda kind, k: lnpm_sb[:, 0, k:k + 1],
                     lambda kind, k: lnpm_sb[:, 1, k:k + 1], Bpar)

        if dbg is None:
            ln_stats(p, c, x_sb, xB)
            ln_apply(p, c, x_sb, xB, h_sb, hB, lambda kind, k: sc1p[:, kind, k:k + 1],
                     lambda kind, k: ada_sb[:, kind, 0, k:k + 1], Bpar)
            scale_x_alpha()
        ev = 0
        for ph in range(nph if dbg is None else 0):
            for j in range(PHC):
                prefetch(wi + 2)
                s = slot[wi]
                wi += 1
                for g, (s0, n, kind) in enumerate(GROUPS):
                    pg, pgB = pp.next()
                    pu, puB = pp.next()
                    for kc in range(KC):
                        p.op('pe', lambda e, pg=pg, s=s, kc=kc, s0=s0, n=n: e.matmul(
                            pg[:, :n], lhsT=wg_sb[s][:, 0, kc, :], rhs=h_sb[:, kc, s0:s0 + n],
                            start=(kc == 0), stop=(kc == KC - 1)),
                            reads=[Bwg[s], hB[kc][g]], writes=[pgB])
                    for kc in range(KC):
                        p.op('pe', lambda e, pu=pu, s=s, kc=kc, s0=s0, n=n: e.matmul(
                            pu[:, :n], lhsT=wg_sb[s][:, 1, kc, :], rhs=h_sb[:, kc, s0:s0 + n],
                            start=(kc == 0), stop=(kc == KC - 1)),
                            reads=[Bwg[s], hB[kc][g]], writes=[puB])
                    sgt, sgB = sg_sb[ev % 2], Bsg[ev % 2]
                    ev += 1
                    p.op('act', lambda e, pg=pg, sgt=sgt, n=n: e.activation(out=sgt[:, :n], in_=pg[:, :n], func=AF.Silu),
                         reads=[pgB], writes=[sgB])
                    p.op('dve', lambda e, pu=pu, sgt=sgt, j=j, s0=s0, n=n: e.tensor_tensor(
                        out=a_sb[:, j, s0:s0 + n], in0=sgt[:, :n], in1=pu[:, :n], op=ALU.mult),
                        reads=[sgB, puB], writes=[aB[j][g]])
            for m in range(KC):
                prefetch(wi + 2)
                s = slot[wi]
                wi += 1
                for g, (s0, n, kind) in enumerate(GROUPS):
                    pt, pB = pp.next()
                    for j in range(PHC):
                        p.op('pe', lambda e, pt=pt, s=s, j=j, s0=s0, n=n: e.matmul(
                            pt[:, :n], lhsT=wd_sb[s][:, j, :], rhs=a_sb[:, j, s0:s0 + n],
                            start=(j == 0), stop=(j == PHC - 1)),
                            reads=[Bwd[s], aB[j][g]], writes=[pB])
                    p.op('dve', lambda e, pt=pt, m=m, s0=s0, n=n, kind=kind: e.scalar_tensor_tensor(
                        out=x_sb[:, m, s0:s0 + n], in0=pt[:, :n], scalar=hgate[:, kind, m:m + 1],
                        in1=x_sb[:, m, s0:s0 + n], op0=ALU.mult, op1=ALU.add),
                        reads=[pB, Bpar, xB[m][g]], writes=[xB[m][g]])
        if dbg != 'copy':
            ln_stats(p, c, x_sb, xB)
            ln_apply(p, c, x_sb, xB, x_sb, xB, lambda kind, k: lnp_sb[:, 0, k:k + 1],
                     lambda kind, k: lnp_sb[:, 1, k:k + 1], Bpar)
        outs = []
        if mixh:
            adah_sb = p.sb("adah_sb", [128, 2, 2, KC], F32)
            Bah = p.buf('adah')
            p.dma('sp', lambda e: e.dma_start(out=adah_sb[:].rearrange("p a b k -> p (a b k)"), in_=adah), writes=[Bah])
            p.op('dve', lambda e: e.tensor_scalar_add(out=adah_sb[:, :, 1, :], in0=adah_sb[:, :, 1, :], scalar1=1.0),
                 reads=[Bah], writes=[Bah])
            ln_stats(p, c, x_sb, xB)
            ln_apply(p, c, x_sb, xB, h_sb, hB, lambda kind, k: adah_sb[:, kind, 1, k:k + 1],
                     lambda kind, k: adah_sb[:, kind, 0, k:k + 1], Bah)
            for q4 in range(4):
                dst = hTo[q4 * 512:(q4 + 1) * 512, :].rearrange("(k p) t -> p k t", p=128)
                outs.append(p.dma('sp', lambda e, q4=q4, dst=dst: e.dma_start(out=dst, in_=h_sb[:, 4 * q4:4 * q4 + 4, :]),
                                  reads=[hB[k][g] for k in range(4 * q4, 4 * q4 + 4) for g in range(NG)], writes=[p.buf('o')],
                                  sembuf=Bout))
        for q4 in range(4):
            dst = yT[q4 * 512:(q4 + 1) * 512, :].rearrange("(k p) t -> p k t", p=128)
            outs.append(p.dma('sp', lambda e, q4=q4, dst=dst: e.dma_start(out=dst, in_=x_sb[:, 4 * q4:4 * q4 + 4, :]),
                              reads=[xB[k][g] for k in range(4 * q4, 4 * q4 + 4) for g in range(NG)], writes=[p.buf('o')],
                              sembuf=Bout))
        p.emit(final_waits=outs)
    return nc


_progs = {}


def get_prog(name):
    if name not in _progs:
        builders = {'ada': build_ada, 'pf0': lambda: build_pf(False), 'pf0h': lambda: build_pf(False, mixh=True),
                    'pf1': lambda: build_pf(True), 'ma': lambda: build_ma(), 'mc': lambda: build_mc(),
                    'mg1': lambda: build_mg1(), 'mg2': lambda: build_mg2()}
        _progs[name] = builders[name]()
    return _progs[name]


def run(name, in_maps):
    res = run_bass_kernel_spmd(get_prog(name), in_maps, core_ids=list(range(8)))
    return res.results


def fm(v):
    v = np.asarray(v)
    lead = v.shape[:-1]
    a = v.reshape(lead + (KC, 128))
    a = np.moveaxis(a, -1, 0)
    return np.ascontiguousarray(a)


def run_ada(c, c_ctx, w_ada, b_ada):
    cc = np.concatenate([c, c_ctx[None, :]], axis=0)
    cT = np.ascontiguousarray(cc.T.reshape(KC, 128, 5).transpose(1, 0, 2)).reshape(128, KC * 5)
    in_maps = []
    for i in range(8):
        wsl = np.ascontiguousarray(w_ada[:, :, i * 2304:(i + 1) * 2304])
        bsl = b_ada[:, i * 2304:(i + 1) * 2304].reshape(DEPTH, 18, 128).transpose(2, 0, 1).reshape(128, 72)
        in_maps.append({"cT": cT, "w": wsl, "b": np.ascontiguousarray(bsl)})
    res = run('ada', in_maps)
    ada = np.empty((DEPTH, 9 * D, 5), np.float32)
    for i in range(8):
        o = res[i]["out"].reshape(128, DEPTH, 18, 5)
        ada[:, i * 2304:(i + 1) * 2304, :] = o.transpose(1, 2, 0, 3).reshape(DEPTH, 2304, 5)
    return ada


MT = 2304
MG256 = [(g * 256, 256, 0) for g in range(8)] + [(2048, 256, 1)]
TG512 = [(0, 512, 0), (512, 512, 0), (1024, 512, 0), (1536, 512, 0), (2048, 256, 1)]


def m_common(p, nc):
    c = TokCtx()
    c.pp = PsPool(p, 8)
    c.ones = p.sb("ones", [128, 128], F32)
    c.eps = p.sb("eps", [128, 1], F32)
    c.Bones = p.buf('ones')
    p.op('dve', lambda e: e.memset(c.ones[:], 1.0), writes=[c.Bones])
    p.op('dve', lambda e: e.memset(c.eps[:], EPS), writes=[c.Bones])
    for nm in ('acc1', 'acc2', 'mean', 'msq'):
        setattr(c, nm, p.sb(nm, [128, 256], F32))
        setattr(c, 'B' + nm, p.buf(nm))
    c.sq = [p.sb("sq%d" % i, [128, 256], F32) for i in range(2)]
    c.Bsq = [p.buf('sq') for _ in range(2)]
    c.t1 = [p.sb("t1_%d" % i, [128, 256], F32) for i in range(2)]
    c.Bt1 = [p.buf('t1') for _ in range(2)]
    c.rs = [p.sb("rs%d" % i, [128, 256], F32) for i in range(2)]
    c.nm = [p.sb("nm%d" % i, [128, 256], F32) for i in range(2)]
    c.Brs = [p.buf('rs') for _ in range(2)]
    c.Bnm = [p.buf('nm') for _ in range(2)]
    return c


def frontend(p, c, xT, adam, h_sb, hB, nxs=2):
    am = p.sb("am_sb", [128, 2, 2, KC], F32)
    Bam = p.buf('am')
    p.dma('sp', lambda e: e.dma_start(out=am[:].rearrange("p a b k -> p (a b k)"), in_=adam), writes=[Bam])
    p.op('dve', lambda e: e.tensor_scalar_add(out=am[:, :, 1, :], in0=am[:, :, 1, :], scalar1=1.0),
         reads=[Bam], writes=[Bam])
    xs = [p.sb("xs%d" % i, [128, KC, 256], F32) for i in range(nxs)]
    Bxs = [p.buf('xs') for _ in range(nxs)]
    ti = 0
    for g, (s0, n, kind) in enumerate(MG256):
        x_t, Bx = xs[g % nxs], Bxs[g % nxs]
        src = xT[:, s0:s0 + n].rearrange("(k p) t -> p k t", p=128)
        p.dma('sp', lambda e, x_t=x_t, src=src: e.dma_start(out=x_t[:], in_=src), writes=[Bx])
        p.op('dve', lambda e, x_t=x_t: e.tensor_tensor(out=c.acc1[:], in0=x_t[:, 0, :], in1=x_t[:, 1, :], op=ALU.add),
             reads=[Bx], writes=[c.Bacc1])
        for k in range(2, KC):
            p.op('dve', lambda e, x_t=x_t, k=k: e.tensor_tensor(out=c.acc1[:], in0=c.acc1[:], in1=x_t[:, k, :],
                                                                  op=ALU.add),
                 reads=[Bx, c.Bacc1], writes=[c.Bacc1])
        p.op('act', lambda e, x_t=x_t: e.activation(out=c.acc2[:], in_=x_t[:, 0, :], func=AF.Square),
             reads=[Bx], writes=[c.Bacc2])
        for k in range(1, KC):
            sq, Bsq = c.sq[k % 2], c.Bsq[k % 2]
            p.op('act', lambda e, x_t=x_t, k=k, sq=sq: e.activation(out=sq[:], in_=x_t[:, k, :], func=AF.Square),
                 reads=[Bx], writes=[Bsq])
            p.op('dve', lambda e, sq=sq: e.tensor_tensor(out=c.acc2[:], in0=c.acc2[:], in1=sq[:], op=ALU.add),
                 reads=[Bsq, c.Bacc2], writes=[c.Bacc2])
        ps1, B1 = c.pp.next()
        ps2, B2 = c.pp.next()
        p.op('pe', lambda e, ps1=ps1: e.matmul(ps1[:, :n], lhsT=c.ones[:], rhs=c.acc1[:], start=True, stop=True),
             reads=[c.Bones, c.Bacc1], writes=[B1])
        p.op('pe', lambda e, ps2=ps2: e.matmul(ps2[:, :n], lhsT=c.ones[:], rhs=c.acc2[:], start=True, stop=True),
             reads=[c.Bones, c.Bacc2], writes=[B2])
        rs, nm, Brs, Bnm = c.rs[g % 2], c.nm[g % 2], c.Brs[g % 2], c.Bnm[g % 2]
        p.op('act', lambda e, ps1=ps1: e.mul(out=c.mean[:], in_=ps1[:, :n], mul=1.0 / D), reads=[B1], writes=[c.Bmean])
        p.op('dve', lambda e: e.tensor_tensor(out=c.msq[:], in0=c.mean[:], in1=c.mean[:], op=ALU.mult),
             reads=[c.Bmean], writes=[c.Bmsq])
        p.op('dve', lambda e, ps2=ps2: e.scalar_tensor_tensor(out=c.msq[:], in0=ps2[:, :n], scalar=1.0 / D,
                                                               in1=c.msq[:], op0=ALU.mult, op1=ALU.subtract),
             reads=[B2, c.Bmsq], writes=[c.Bmsq])
        p.op('act', lambda e: e.activation(out=c.msq[:], in_=c.msq[:], func=AF.Sqrt, bias=c.eps[:, 0:1], scale=1.0),
             reads=[c.Bmsq, c.Bones], writes=[c.Bmsq])
        p.op('dve', lambda e, rs=rs: e.reciprocal(out=rs[:], in_=c.msq[:]), reads=[c.Bmsq], writes=[Brs])
        p.op('dve', lambda e, rs=rs, nm=nm: e.scalar_tensor_tensor(out=nm[:], in0=c.mean[:], scalar=-1.0, in1=rs[:],
                                                                   op0=ALU.mult, op1=ALU.mult),
             reads=[c.Bmean, Brs], writes=[Bnm])
        for k in range(KC):
            t1, Bt1 = c.t1[ti % 2], c.Bt1[ti % 2]
            ti += 1
            p.op('dve', lambda e, x_t=x_t, k=k, t1=t1, rs=rs: e.tensor_tensor(out=t1[:], in0=x_t[:, k, :], in1=rs[:],
                                                                               op=ALU.mult),
                 reads=[Bx, Brs], writes=[Bt1])
            p.op('dve', lambda e, t1=t1, nm=nm: e.tensor_tensor(out=t1[:], in0=t1[:], in1=nm[:], op=ALU.add),
                 reads=[Bt1, Bnm], writes=[Bt1])
            p.op('act', lambda e, k=k, t1=t1, kind=kind, s0=s0: e.activation(
                out=h_sb[:, k, s0:s0 + 256], in_=t1[:], func=AF.Identity, scale=am[:, kind, 1, k:k + 1],
                bias=am[:, kind, 0, k:k + 1]),
                reads=[Bt1, Bam], writes=[hB[k][g]])


def h_bufs_for(hB, k, s0, n):
    return [hB[k][g] for g in range(s0 // 256, (s0 + n + 255) // 256)]


def proj_fm(p, c, w_sb, Bw, h_sb, hB, s0, n):
    pt, pB = c.pp.next()
    for kc in range(KC):
        p.op('pe', lambda e, pt=pt, kc=kc: e.matmul(pt[:, :n], lhsT=w_sb(kc), rhs=h_sb[:, kc, s0:s0 + n],
                                                    start=(kc == 0), stop=(kc == KC - 1)),
             reads=[Bw] + h_bufs_for(hB, kc, s0, n), writes=[pB])
    return pt, pB


def build_mc():
    nc = bass.Bass("TRN2", target_bir_lowering=False)
    hT = nc.dram_tensor("hT", [D, MT], BF16, kind="ExternalInput").ap()
    wcv = nc.dram_tensor("wcv", [D, 768], F32, kind="ExternalInput").ap()
    cvw = nc.dram_tensor("cvw", [128, 6], F32, kind="ExternalInput").ap()
    ocv = nc.dram_tensor("ocv", [256, MT], BF16, kind="ExternalOutput").ap()
    with ExitStack() as st:
        p = Prog(nc, st)
        c = TokCtx()
        c.pp = PsPool(p, 8)
        h_sb = p.sb("h_sb", [128, KC, MT], BF16)
        hB = [[p.buf('h') for _ in MG256] for _ in range(KC)]
        w_sb = p.sb("w_sb", [128, KC, 768], BF16)
        Bw = p.buf('w')
        wo = []
        for j in range(6):
            src = wcv[:, j * 128:(j + 1) * 128].rearrange("(k p) n -> p k n", p=128)
            wo.append(p.dma('pool', lambda e, j=j, src=src: e.dma_start(out=w_sb[:, :, j * 128:(j + 1) * 128], in_=src),
                            writes=[Bw]))
        Prog.group(wo)
        cw = p.sb("cw", [128, 2, 3], F32)
        Bcw = p.buf('cw')
        p.dma('sp', lambda e: e.dma_start(out=cw[:].rearrange("p a b -> p (a b)"), in_=cvw), writes=[Bcw])
        load_h(p, hT, h_sb, hB)
        CUW = 2308
        cu = p.sb("cu", [128, 2, CUW], F32)
        Bcu = [[p.buf('cu') for _ in TG512] for _ in range(2)]
        Bpad = p.buf('pad')
        for gi in range(2):
            for col in (0, 2049, 2307):
                p.op('dve', lambda e, gi=gi, col=col: e.memset(cu[:, gi, col:col + (2 if col == 2049 else 1)], 0.0),
                     writes=[Bpad])
        o_sb = p.sb("o_sb", [128, 2, MT], BF16)
        Bo = p.buf('o')
        tmp = [p.sb("tmp%d" % i, [128, 512], F32) for i in range(2)]
        Btmp = [p.buf('tmp') for _ in range(2)]
        col0 = lambda kind: 1 if kind == 0 else 2051 - 2048
        it = 0
        for gi in range(2):
            for tg, (s0, n, kind) in enumerate(TG512):
                pc, pcB = proj_fm(p, c, lambda kc, gi=gi: w_sb[:, kc, (2 + gi) * 128:(3 + gi) * 128], Bw, h_sb, hB, s0, n)
                pu, puB = proj_fm(p, c, lambda kc, gi=gi: w_sb[:, kc, (4 + gi) * 128:(5 + gi) * 128], Bw, h_sb, hB, s0, n)
                t, Bt = tmp[it % 2], Btmp[it % 2]
                it += 1
                p.op('act', lambda e, t=t, pc=pc, n=n: e.copy(out=t[:, :n], in_=pc[:, :n]), reads=[pcB], writes=[Bt])
                o0 = s0 + col0(kind)
                p.op('dve', lambda e, t=t, pu=pu, n=n, gi=gi, o0=o0: e.tensor_tensor(
                    out=cu[:, gi, o0:o0 + n], in0=t[:, :n], in1=pu[:, :n], op=ALU.mult),
                    reads=[Bt, puB], writes=[Bcu[gi][tg]])
        for gi in range(2):
            for tg, (s0, n, kind) in enumerate(TG512):
                pb, pbB = proj_fm(p, c, lambda kc, gi=gi: w_sb[:, kc, gi * 128:(gi + 1) * 128], Bw, h_sb, hB, s0, n)
                t, Bt = tmp[it % 2], Btmp[it % 2]
                it += 1
                o0 = s0 + col0(kind)
                nb = [Bcu[gi][x] for x in range(max(0, tg - 1), min(len(TG512), tg + 2))] + [Bpad, Bcw]
                p.op('dve', lambda e, t=t, n=n, gi=gi, o0=o0: e.tensor_scalar_mul(
                    out=t[:, :n], in0=cu[:, gi, o0 - 1:o0 - 1 + n], scalar1=cw[:, gi, 0:1]), reads=nb, writes=[Bt])
                for tap in (1, 2):
                    p.op('dve', lambda e, t=t, n=n, gi=gi, o0=o0, tap=tap: e.scalar_tensor_tensor(
                        out=t[:, :n], in0=cu[:, gi, o0 - 1 + tap:o0 - 1 + tap + n], scalar=cw[:, gi, tap:tap + 1],
                        in1=t[:, :n], op0=ALU.mult, op1=ALU.add), reads=nb + [Bt], writes=[Bt])
                p.op('dve', lambda e, t=t, pb=pb, n=n, gi=gi, s0=s0: e.tensor_tensor(
                    out=o_sb[:, gi, s0:s0 + n], in0=t[:, :n], in1=pb[:, :n], op=ALU.mult),
                    reads=[Bt, pbB], writes=[Bo])
        outs = []
        for gi in range(2):
            outs.append(p.dma('sp', lambda e, gi=gi: e.dma_start(out=ocv[gi * 128:(gi + 1) * 128, :], in_=o_sb[:, gi, :]),
                              reads=[Bo], writes=[p.buf('oo')], sembuf=Bo))
        p.emit(final_waits=outs)
    return nc


def na_tables(rpb3):
    kl = np.arange(128) // 64
    kc = np.arange(128) % 64
    m = np.arange(16)
    qc = np.arange(64)
    ri = m[None, :] + kl[:, None] - 1
    ci = kc[:, None] - qc[None, :] + 15
    row_ok = (ri >= 0) & (ri <= 14)
    cs = np.clip(qc - 8, 0, 48)
    col_ok = (kc[:, None] >= cs[None, :]) & (kc[:, None] < cs[None, :] + 16)
    bt = rpb3[:, np.clip(ri, 0, 14)[:, :, None], np.clip(ci, 0, 30)[:, None, :]]
    bt = np.ascontiguousarray(bt.transpose(1, 0, 2, 3)).reshape(128, 3 * 1024).astype(np.float32)
    mask = (row_ok[:, :, None] & col_ok[:, None, :]).astype(np.float32).reshape(128, 1024)
    return bt, np.ascontiguousarray(mask)


def build_ma():
    nc = bass.Bass("TRN2", target_bir_lowering=False)
    hT = nc.dram_tensor("hT", [D, MT], BF16, kind="ExternalInput").ap()
    wna = nc.dram_tensor("wna", [D, 1152], F32, kind="ExternalInput").ap()
    btab = nc.dram_tensor("btab", [128, 3 * 1024], F32, kind="ExternalInput").ap()
    mask = nc.dram_tensor("mask", [128, 1024], F32, kind="ExternalInput").ap()
    ona = nc.dram_tensor("ona", [384, MT], BF16, kind="ExternalOutput").ap()
    SCALE = 128 ** -0.5
    with ExitStack() as st:
        p = Prog(nc, st)
        c = TokCtx()
        c.pp = PsPool(p, 8)
        pp = c.pp
        h_sb = p.sb("h_sb", [128, KC, MT], BF16)
        hB = [[p.buf('h') for _ in MG256] for _ in range(KC)]
        G_sb = p.sb("G_sb", [128, 3, 16, 64], F32)
        mk_sb = p.sb("mk_sb", [128, 16, 64], F32)
        BG = p.buf('G')
        Bmk = p.buf('mk')
        p.dma('sp', lambda e: e.dma_start(out=G_sb[:].rearrange("p a m q -> p (a m q)"), in_=btab), writes=[BG])
        p.dma('sp', lambda e: e.dma_start(out=mk_sb[:].rearrange("p m q -> p (m q)"), in_=mask), writes=[Bmk])
        for i in range(3):
            p.op('act', lambda e, i=i: e.activation(out=G_sb[:, i], in_=G_sb[:, i], func=AF.Exp), reads=[BG], writes=[BG])
            p.op('dve', lambda e, i=i: e.tensor_tensor(out=G_sb[:, i], in0=G_sb[:, i], in1=mk_sb[:], op=ALU.mult),
                 reads=[BG, Bmk], writes=[BG])
        onesb = p.sb("onesb", [128, 128], BF16)
        Bob = p.buf('onesb')
        p.op('dve', lambda e: e.memset(onesb[:], 1.0), writes=[Bob])
        wv_sb = p.sb("wv_sb", [128, KC, 384], BF16)
        Bwv = p.buf('wv')
        wo = []
        for j in range(3):
            src = wna[:, (6 + j) * 128:(7 + j) * 128].rearrange("(k p) n -> p k n", p=128)
            wo.append(p.dma('pool', lambda e, j=j, src=src: e.dma_start(out=wv_sb[:, :, j * 128:(j + 1) * 128], in_=src),
                            writes=[Bwv]))
        Prog.group(wo)
        NW = 2
        wq_sb = [p.sb("wq%d" % i, [128, KC, 128], BF16) for i in range(NW)]
        Bwq = [p.buf('wq') for _ in range(NW)]

        def load_wq(j):
            src = wna[:, j * 128:(j + 1) * 128].rearrange("(k p) n -> p k n", p=128)
            p.dma('pool', lambda e, j=j, src=src: e.dma_start(out=wq_sb[j % NW][:], in_=src), writes=[Bwq[j % NW]])

        load_wq(0)
        load_h(p, hT, h_sb, hB)
        qT = p.sb("qT", [128, 3, MT], BF16)
        kT = p.sb("kT", [128, 3, MT], BF16)
        BqT = [[p.buf('qT') for _ in TG512] for _ in range(3)]
        BkT = [[p.buf('kT') for _ in TG512] for _ in range(3)]
        for j in range(6):
            if j + 1 < 6:
                load_wq(j + 1)
            for tg, (s0, n, kind) in enumerate(TG512):
                pt, pB = proj_fm(p, c, lambda kc, j=j: wq_sb[j % NW][:, kc, :], Bwq[j % NW], h_sb, hB, s0, n)
                if j < 3:
                    p.op('act', lambda e, pt=pt, j=j, s0=s0, n=n: e.mul(out=qT[:, j, s0:s0 + n], in_=pt[:, :n], mul=SCALE),
                         reads=[pB], writes=[BqT[j][tg]])
                else:
                    p.op('dve', lambda e, pt=pt, j=j, s0=s0, n=n: e.tensor_copy(out=kT[:, j - 3, s0:s0 + n], in_=pt[:, :n]),
                         reads=[pB], writes=[BkT[j - 3][tg]])
        vstarts = [128 * j for j in range(16)] + [64 + 128 * j for j in range(15)] + [2048, 2176]
        v_sb = p.sb("v_sb", [128, 33, 384], BF16)
        Bv = [p.buf('v') for _ in vstarts]
        for t_i, ts in enumerate(vstarts):
            pt, pB = pp.next()
            for kc in range(KC):
                p.op('pe', lambda e, pt=pt, kc=kc, ts=ts: e.matmul(pt[:, :384], lhsT=h_sb[:, kc, ts:ts + 128],
                                                                   rhs=wv_sb[:, kc, :], start=(kc == 0), stop=(kc == KC - 1)),
                     reads=[Bwv] + h_bufs_for(hB, kc, ts, 128), writes=[pB])
            eng = 'act' if t_i % 2 == 0 else 'dve'
            if eng == 'act':
                p.op('act', lambda e, pt=pt, t_i=t_i: e.copy(out=v_sb[:, t_i, :], in_=pt[:, :384]), reads=[pB], writes=[Bv[t_i]])
            else:
                p.op('dve', lambda e, pt=pt, t_i=t_i: e.tensor_copy(out=v_sb[:, t_i, :], in_=pt[:, :384]), reads=[pB],
                     writes=[Bv[t_i]])
        o_sbs = [p.sb("o_sb%d" % i, [128, MT], BF16) for i in range(2)]
        Bos = [p.buf('o') for _ in range(2)]
        Bout = p.buf('out')
        outs = []
        ex = [p.sb("ex%d" % i, [128, 256], F32) for i in range(2)]
        Bex = [p.buf('ex') for _ in range(2)]
        pr = [p.sb("pr%d" % i, [128, 512], BF16) for i in range(2)]
        Bpr = [p.buf('pr') for _ in range(2)]
        rd = [p.sb("rd%d" % i, [128, 256], F32) for i in range(2)]
        Brd = [p.buf('rd') for _ in range(2)]
        it = 0
        tgof = lambda tok: min(tok // 512, 4)
        for i in range(3):
            o_sb, Bo = o_sbs[i % 2], Bos[i % 2]
            for qr in range(32):
                start = min(max(qr - 4, 0), 24)
                q0 = qr * 64
                ps, psB = pp.next()
                ktoks = [64 * start + 128 * ti for ti in range(4)] + [2048, 2176]
                for ti, kt in enumerate(ktoks):
                    p.op('pe', lambda e, ps=ps, ti=ti, kt=kt, i=i, q0=q0: e.matmul(
                        ps[:, ti * 64:(ti + 1) * 64], lhsT=kT[:, i, kt:kt + 128], rhs=qT[:, i, q0:q0 + 64],
                        start=True, stop=True),
                        reads=[BkT[i][tgof(kt)], BkT[i][tgof(kt + 127)], BqT[i][tgof(q0)]], writes=[psB])
                e_t, Be = ex[it % 2], Bex[it % 2]
                p_t, Bp = pr[it % 2], Bpr[it % 2]
                r_t, Br = rd[it % 2], Brd[it % 2]
                it += 1
                p.op('act', lambda e, ps=ps, e_t=e_t: e.activation(out=e_t[:, :256], in_=ps[:, :256], func=AF.Exp),
                     reads=[psB], writes=[Be])
                p.op('act', lambda e, ps=ps, p_t=p_t: e.activation(out=p_t[:, 256:384], in_=ps[:, 256:384], func=AF.Exp),
                     reads=[psB], writes=[Bp])
                m0 = start - qr + 8
                p.op('dve', lambda e, e_t=e_t, p_t=p_t, i=i, m0=m0: e.tensor_tensor(
                    out=p_t[:, 0:256].rearrange("p (t q) -> p t q", q=64),
                    in0=e_t[:, :256].rearrange("p (t q) -> p t q", q=64),
                    in1=G_sb[:, i, m0:m0 + 7:2, :], op=ALU.mult), reads=[Be, BG], writes=[Bp])
                if start % 2 == 0:
                    vt = [start // 2 + ti for ti in range(4)]
                else:
                    vt = [16 + (start - 1) // 2 + ti for ti in range(4)]
                vt += [31, 32]
                po, poB = pp.next()
                pd, pdB = pp.next()
                for ti, v_i in enumerate(vt):
                    p.op('pe', lambda e, po=po, ti=ti, v_i=v_i, i=i, p_t=p_t: e.matmul(
                        po[:, :64], lhsT=v_sb[:, v_i, i * 128:(i + 1) * 128], rhs=p_t[:, ti * 64:(ti + 1) * 64],
                        start=(ti == 0), stop=(ti == 5)), reads=[Bv[v_i], Bp], writes=[poB])
                for ti in range(6):
                    p.op('pe', lambda e, pd=pd, ti=ti, p_t=p_t: e.matmul(
                        pd[:, :64], lhsT=onesb[:], rhs=p_t[:, ti * 64:(ti + 1) * 64],
                        start=(ti == 0), stop=(ti == 5)), reads=[Bob, Bp], writes=[pdB])
                p.op('dve', lambda e, pd=pd, r_t=r_t: e.reciprocal(out=r_t[:, :64], in_=pd[:, :64]), reads=[pdB], writes=[Br])
                p.op('dve', lambda e, po=po, r_t=r_t, i=i, q0=q0, o_sb=o_sb: e.tensor_tensor(
                    out=o_sb[:, q0:q0 + 64], in0=po[:, :64], in1=r_t[:, :64], op=ALU.mult),
                    reads=[poB, Br], writes=[Bo])
            ps, psB = pp.next()
            for ci in range(2):
                p.op('pe', lambda e, ps=ps, ci=ci, i=i: e.matmul(
                    ps[:, ci * 256:(ci + 1) * 256], lhsT=kT[:, i, 2048 + 128 * ci:2048 + 128 * (ci + 1)],
                    rhs=qT[:, i, 2048:2304], start=True, stop=True), reads=[BkT[i][4], BqT[i][4]], writes=[psB])
            p_t, Bp = pr[it % 2], Bpr[it % 2]
            r_t, Br = rd[it % 2], Brd[it % 2]
            it += 1
            p.op('act', lambda e, ps=ps, p_t=p_t: e.activation(out=p_t[:, :512], in_=ps[:, :512], func=AF.Exp),
                 reads=[psB], writes=[Bp])
            po, poB = pp.next()
            pd, pdB = pp.next()
            for ci in range(2):
                p.op('pe', lambda e, po=po, ci=ci, i=i, p_t=p_t: e.matmul(
                    po[:, :256], lhsT=v_sb[:, 31 + ci, i * 128:(i + 1) * 128], rhs=p_t[:, ci * 256:(ci + 1) * 256],
                    start=(ci == 0), stop=(ci == 1)), reads=[Bv[31 + ci], Bp], writes=[poB])
            for ci in range(2):
                p.op('pe', lambda e, pd=pd, ci=ci, p_t=p_t: e.matmul(
                    pd[:, :256], lhsT=onesb[:], rhs=p_t[:, ci * 256:(ci + 1) * 256],
                    start=(ci == 0), stop=(ci == 1)), reads=[Bob, Bp], writes=[pdB])
            p.op('dve', lambda e, pd=pd, r_t=r_t: e.reciprocal(out=r_t[:, :256], in_=pd[:, :256]), reads=[pdB], writes=[Br])
            p.op('dve', lambda e, po=po, r_t=r_t, i=i, o_sb=o_sb: e.tensor_tensor(
                out=o_sb[:, 2048:2304], in0=po[:, :256], in1=r_t[:, :256], op=ALU.mult),
                reads=[poB, Br], writes=[Bo])
            outs.append(p.dma('sp', lambda e, i=i, o_sb=o_sb: e.dma_start(out=ona[i * 128:(i + 1) * 128, :], in_=o_sb[:]),
                              reads=[Bo], writes=[p.buf('oo')], sembuf=Bout))
        p.emit(final_waits=outs)
    return nc


def load_h(p, hT, h_sb, hB):
    for q4 in range(4):
        src = hT[q4 * 512:(q4 + 1) * 512, :].rearrange("(k p) t -> p k t", p=128)
        p.dma('sp', lambda e, q4=q4, src=src: e.dma_start(out=h_sb[:, 4 * q4:4 * q4 + 4, :], in_=src),
              writes=[hB[k][g] for k in range(4 * q4, 4 * q4 + 4) for g in range(len(MG256))], sembuf=hB[4 * q4][0])


def gdn_consts():
    t = np.arange(2048)
    row = (t // 64).astype(np.float32)
    col = (t % 64).astype(np.float32)
    inv = (10000.0 ** (-np.arange(32, dtype=np.float32) / 32)).astype(np.float32)
    d = np.arange(128)
    ang = np.where((d < 64)[:, None], row[None, :] * inv[d % 32][:, None], col[None, :] * inv[d % 32][:, None]).astype(np.float32)
    cosT = np.cos(ang).astype(np.float32)
    sinT = np.sin(ang).astype(np.float32)
    perm = np.zeros((128, 128), np.float32)
    for dp in range(128):
        if dp % 64 < 32:
            perm[dp + 32, dp] = -1.0
        else:
            perm[dp - 32, dp] = 1.0
    return cosT, sinT, perm


def build_mg1():
    nc = bass.Bass("TRN2", target_bir_lowering=False)
    hT = nc.dram_tensor("hT", [D, MT], BF16, kind="ExternalInput").ap()
    wqkv = nc.dram_tensor("wqkv", [D, 1152], F32, kind="ExternalInput").ap()
    wzb = nc.dram_tensor("wzb", [D, 396], F32, kind="ExternalInput").ap()
    cwd = nc.dram_tensor("cw", [128, 45], F32, kind="ExternalInput").ap()
    gpar = nc.dram_tensor("gpar", [128, 12], F32, kind="ExternalInput").ap()
    nwd = nc.dram_tensor("nw", [128, 384], F32, kind="ExternalInput").ap()
    cosd = nc.dram_tensor("cosT", [128, 2048], F32, kind="ExternalInput").ap()
    sind = nc.dram_tensor("sinT", [128, 2048], F32, kind="ExternalInput").ap()
    permd = nc.dram_tensor("perm", [128, 128], F32, kind="ExternalInput").ap()
    qkvT = nc.dram_tensor("qkvT", [9 * 128, MT], F32, kind="ExternalOutput").ap()
    zs = nc.dram_tensor("zs", [MT, 384], F32, kind="ExternalOutput").ap()
    bg = nc.dram_tensor("bg", [MT, 12], F32, kind="ExternalOutput").ap()
    SCALE = 128 ** -0.5
    with ExitStack() as st:
        p = Prog(nc, st)
        pp = PsPool(p, 8)
        c = TokCtx()
        c.pp = pp
        h_sb = p.sb("h_sb", [128, KC, MT], BF16)
        hB = [[p.buf('h') for _ in MG256] for _ in range(KC)]
        load_h(p, hT, h_sb, hB)
        ones = p.sb("ones", [128, 128], F32)
        eps = p.sb("eps", [128, 1], F32)
        Bc = p.buf('c')
        p.op('dve', lambda e: e.memset(ones[:], 1.0), writes=[Bc])
        p.op('dve', lambda e: e.memset(eps[:], EPS), writes=[Bc])
        cw = p.sb("cw_sb", [128, 9, 5], F32)
        gp = p.sb("gp_sb", [128, 12], F32)
        nw = p.sb("nw_sb", [128, 384], F32)
        perm = p.sb("perm_sb", [128, 128], F32)
        Bpar = p.buf('par')
        po = [p.dma('sp', lambda e: e.dma_start(out=cw[:].rearrange("p a b -> p (a b)"), in_=cwd), writes=[Bpar]),
              p.dma('sp', lambda e: e.dma_start(out=gp[:], in_=gpar), writes=[Bpar]),
              p.dma('sp', lambda e: e.dma_start(out=nw[:], in_=nwd), writes=[Bpar]),
              p.dma('sp', lambda e: e.dma_start(out=perm[:], in_=permd), writes=[Bpar])]
        Prog.group(po)
        p.op('act', lambda e: e.activation(out=gp[:, 0:6], in_=gp[:, 0:6], func=AF.Exp), reads=[Bpar], writes=[Bpar])
        p.op('dve', lambda e: e.tensor_scalar_mul(out=gp[:, 0:6], in0=gp[:, 0:6], scalar1=-1.0), reads=[Bpar], writes=[Bpar])
        wzb_sb = p.sb("wzb_sb", [128, KC, 396], BF16)
        Bwz = p.buf('wz')
        p.dma('pool', lambda e: e.dma_start(out=wzb_sb[:], in_=wzb.rearrange("(k p) n -> p k n", p=128)), writes=[Bwz])
        NW = 2
        w_sb = [p.sb("w%d" % i, [128, KC, 128], BF16) for i in range(NW)]
        Bw = [p.buf('w') for _ in range(NW)]
        wcnt = [0]

        def load_w(col):
            s = wcnt[0] % NW
            wcnt[0] += 1
            src = wqkv[:, col * 128:(col + 1) * 128].rearrange("(k p) n -> p k n", p=128)
            p.dma('pool', lambda e, s=s, src=src: e.dma_start(out=w_sb[s][:], in_=src), writes=[Bw[s]])
            return s

        bg_sb = p.sb("bg_sb", [128, 18, 12], F32)
        Bbg = p.buf('bg')
        zt = [p.sb("zt%d" % i, [128, 384], F32) for i in range(2)]
        Bzt = [p.buf('zt') for _ in range(2)]
        tm6 = [p.sb("tm6_%d" % i, [128, 6], F32) for i in range(2)]
        Btm6 = [p.buf('tm6') for _ in range(2)]
        Bout = p.buf('out')
        outs = []
        tstarts = [128 * t for t in range(18)]
        for t_i, ts in enumerate(tstarts):
            pt, pB = pp.next()
            for kc in range(KC):
                p.op('pe', lambda e, pt=pt, kc=kc, ts=ts: e.matmul(pt[:, :396], lhsT=h_sb[:, kc, ts:ts + 128],
                                                                   rhs=wzb_sb[:, kc, :], start=(kc == 0), stop=(kc == KC - 1)),
                     reads=[Bwz] + h_bufs_for(hB, kc, ts, 128), writes=[pB])
            z_t, Bz = zt[t_i % 2], Bzt[t_i % 2]
            t6, Bt6 = tm6[t_i % 2], Btm6[t_i % 2]
            p.op('act', lambda e, pt=pt, z_t=z_t: e.activation(out=z_t[:], in_=pt[:, 0:384], func=AF.Silu), reads=[pB], writes=[Bz])
            p.op('dve', lambda e, z_t=z_t: e.tensor_tensor(out=z_t[:], in0=z_t[:], in1=nw[:], op=ALU.mult),
                 reads=[Bz, Bpar], writes=[Bz])
            outs.append(p.dma('sp', lambda e, z_t=z_t, ts=ts: e.dma_start(out=zs[ts:ts + 128, :], in_=z_t[:]),
                              reads=[Bz], writes=[p.buf('oo')], sembuf=Bz))
            p.op('act', lambda e, pt=pt, t_i=t_i: e.activation(out=bg_sb[:, t_i, 0:6], in_=pt[:, 384:390], func=AF.Sigmoid),
                 reads=[pB], writes=[Bbg])
            p.op('dve', lambda e, pt=pt, t6=t6: e.tensor_tensor(out=t6[:], in0=pt[:, 390:396], in1=gp[:, 6:12], op=ALU.add),
                 reads=[pB, Bpar], writes=[Bt6])
            p.op('act', lambda e, t6=t6: e.activation(out=t6[:], in_=t6[:], func=AF.Exp), reads=[Bt6], writes=[Bt6])
            p.op('act', lambda e, t6=t6: e.activation(out=t6[:], in_=t6[:], func=AF.Ln, bias=ones[:, 0:1], scale=1.0),
                 reads=[Bt6, Bc], writes=[Bt6])
            p.op('dve', lambda e, t6=t6, t_i=t_i: e.tensor_tensor(out=bg_sb[:, t_i, 6:12], in0=t6[:], in1=gp[:, 0:6], op=ALU.mult),
                 reads=[Bt6, Bpar], writes=[Bbg])
        outs.append(p.dma('sp', lambda e: e.dma_start(out=bg.rearrange("(t p) c -> p t c", p=128), in_=bg_sb[:]),
                          reads=[Bbg], writes=[p.buf('oo')], sembuf=Bout))
        PW = 2312
        pre = p.sb("pre", [128, 3, PW], F32)
        Bpre = [[p.buf('pre') for _ in TG512] for _ in range(3)]
        Bpad = p.buf('pad')
        for j in range(3):
            for (a, b) in ((0, 2), (2050, 2054), (2310, 2312)):
                p.op('dve', lambda e, j=j, a=a, b=b: e.memset(pre[:, j, a:b], 0.0), writes=[Bpad])
        stg = p.sb("stg", [128, 3, MT], F32)
        Bstg = [[p.buf('stg') for _ in TG512] for _ in range(3)]
        acc = [p.sb("acc%d" % i, [128, 512], F32) for i in range(2)]
        Bacc = [p.buf('acc') for _ in range(2)]
        post = [p.sb("post%d" % i, [128, 512], F32) for i in range(2)]
        Bpost = [p.buf('post') for _ in range(2)]
        sq = [p.sb("sq%d" % i, [128, 512], F32) for i in range(2)]
        Bsq = [p.buf('sq') for _ in range(2)]
        rn = [p.sb("rn%d" % i, [128, 512], F32) for i in range(2)]
        Brn = [p.buf('rn') for _ in range(2)]
        xn = [p.sb("xn%d" % i, [128, 512], F32) for i in range(2)]
        Bxn = [p.buf('xn') for _ in range(2)]
        cs = [p.sb("cs%d" % i, [128, 2, 512], F32) for i in range(2)]
        Bcs = [p.buf('cs') for _ in range(2)]
        off = lambda kind, s0: (2 + s0) if kind == 0 else (2054 + s0 - 2048)
        it = 0
        rc = 0
        for i in range(3):
            for j in range(3):
                s = load_w(3 * i + j)
                for tg, (s0, n, kind) in enumerate(TG512):
                    pt, pB = proj_fm(p, c, lambda kc, s=s: w_sb[s][:, kc, :], Bw[s], h_sb, hB, s0, n)
                    o0 = off(kind, s0)
                    p.op('act', lambda e, pt=pt, j=j, o0=o0, n=n: e.copy(out=pre[:, j, o0:o0 + n], in_=pt[:, :n]),
                         reads=[pB], writes=[Bpre[j][tg]])
            for j in range(3):
                for tg, (s0, n, kind) in enumerate(TG512):
                    o0 = off(kind, s0)
                    a_t, Ba = acc[it % 2], Bacc[it % 2]
                    p_t, Bp = post[it % 2], Bpost[it % 2]
                    s_t, Bs = sq[it % 2], Bsq[it % 2]
                    r_t, Br = rn[it % 2], Brn[it % 2]
                    x_t, Bx = xn[it % 2], Bxn[it % 2]
                    it += 1
                    nb = [Bpre[j][x] for x in range(max(0, tg - 1), min(len(TG512), tg + 2))] + [Bpad, Bpar]
                    p.op('dve', lambda e, a_t=a_t, i=i, j=j, o0=o0, n=n: e.tensor_scalar_mul(
                        out=a_t[:, :n], in0=pre[:, j, o0 - 2:o0 - 2 + n], scalar1=cw[:, 3 * i + j, 0:1]), reads=nb, writes=[Ba])
                    for tap in range(1, 5):
                        p.op('dve', lambda e, a_t=a_t, i=i, j=j, o0=o0, n=n, tap=tap: e.scalar_tensor_tensor(
                            out=a_t[:, :n], in0=pre[:, j, o0 - 2 + tap:o0 - 2 + tap + n], scalar=cw[:, 3 * i + j, tap:tap + 1],
                            in1=a_t[:, :n], op0=ALU.mult, op1=ALU.add), reads=nb + [Ba], writes=[Ba])
                    if j == 2:
                        p.op('act', lambda e, a_t=a_t, s0=s0, n=n: e.activation(out=stg[:, 2, s0:s0 + n], in_=a_t[:, :n], func=AF.Silu),
                             reads=[Ba], writes=[Bstg[2][tg]])
                        continue
                    p.op('act', lambda e, a_t=a_t, p_t=p_t, n=n: e.activation(out=p_t[:, :n], in_=a_t[:, :n], func=AF.Silu),
                         reads=[Ba], writes=[Bp])
                    p.op('act', lambda e, p_t=p_t, s_t=s_t, n=n: e.activation(out=s_t[:, :n], in_=p_t[:, :n], func=AF.Square),
                         reads=[Bp], writes=[Bs])
                    ps, psB = pp.next()
                    p.op('pe', lambda e, ps=ps, s_t=s_t, n=n: e.matmul(ps[:, :n], lhsT=ones[:], rhs=s_t[:, :n], start=True, stop=True),
                         reads=[Bc, Bs], writes=[psB])
                    p.op('act', lambda e, ps=ps, r_t=r_t, n=n: e.activation(out=r_t[:, :n], in_=ps[:, :n], func=AF.Sqrt,
                                                                            bias=eps[:, 0:1], scale=1.0),
                         reads=[psB, Bc], writes=[Br])
                    p.op('dve', lambda e, r_t=r_t, n=n: e.reciprocal(out=r_t[:, :n], in_=r_t[:, :n]), reads=[Br], writes=[Br])
                    dst = stg[:, j, s0:s0 + n] if kind == 1 else x_t[:, :n]
                    dB = Bstg[j][tg] if kind == 1 else Bx
                    p.op('dve', lambda e, p_t=p_t, r_t=r_t, n=n, j=j, dst=dst: e.scalar_tensor_tensor(
                        out=dst, in0=p_t[:, :n], scalar=(SCALE if j == 0 else 1.0), in1=r_t[:, :n], op0=ALU.mult, op1=ALU.mult),
                        reads=[Bp, Br], writes=[dB])
                    if kind == 1:
                        continue
                    c_t, Bct = cs[rc % 2], Bcs[rc % 2]
                    rc += 1
                    Prog.group([
                        p.dma('sp', lambda e, c_t=c_t, s0=s0, n=n: e.dma_start(out=c_t[:, 0, :n], in_=cosd[:, s0:s0 + n]), writes=[Bct]),
                        p.dma('sp', lambda e, c_t=c_t, s0=s0, n=n: e.dma_start(out=c_t[:, 1, :n], in_=sind[:, s0:s0 + n]), writes=[Bct])])
                    ps2, ps2B = pp.next()
                    p.op('pe', lambda e, ps2=ps2, x_t=x_t, n=n: e.matmul(ps2[:, :n], lhsT=perm[:], rhs=x_t[:, :n], start=True, stop=True),
                         reads=[Bpar, Bx], writes=[ps2B])
                    p.op('dve', lambda e, ps2=ps2, c_t=c_t, s_t=s_t, n=n: e.tensor_tensor(out=s_t[:, :n], in0=ps2[:, :n], in1=c_t[:, 1, :n],
                                                                                          op=ALU.mult),
                         reads=[ps2B, Bct, Bs], writes=[Bs])
                    p.op('dve', lambda e, x_t=x_t, c_t=c_t, n=n: e.tensor_tensor(out=x_t[:, :n], in0=x_t[:, :n], in1=c_t[:, 0, :n], op=ALU.mult),
                         reads=[Bx, Bct], writes=[Bx])
                    p.op('dve', lambda e, x_t=x_t, s_t=s_t, j=j, s0=s0, n=n: e.tensor_tensor(out=stg[:, j, s0:s0 + n], in0=x_t[:, :n],
                                                                                             in1=s_t[:, :n], op=ALU.add),
                         reads=[Bx, Bs], writes=[Bstg[j][tg]])
            for j in range(3):
                r0 = (3 * i + j) * 128
                outs.append(p.dma('sp', lambda e, j=j, r0=r0: e.dma_start(out=qkvT[r0:r0 + 128, :], in_=stg[:, j, :]),
                                  reads=Bstg[j], writes=[p.buf('oo')], sembuf=Bstg[j][0]))
        p.emit(final_waits=outs)
    return nc


def gdn_masks():
    s = np.arange(128)[:, None]
    t = np.arange(128)[None, :]
    same = (s // 64) == (t // 64)
    triF = (same & (s <= t)).astype(np.float32)
    triB = (same & (s >= t)).astype(np.float32)
    blk = same.astype(np.float32)
    half0 = np.broadcast_to((s < 64), (128, 128)).astype(np.float32)
    half1 = np.broadcast_to((s >= 64), (128, 128)).astype(np.float32)
    ident = np.eye(128, dtype=np.float32)
    negF = (triF - 1.0) * 30000.0
    negB = (triB - 1.0) * 30000.0
    return np.ascontiguousarray(np.concatenate([triF, triB, blk, half0, half1, negF, negB, triF - ident, triB - ident, ident],
                                               axis=1).astype(np.float32))


def build_mg2():
    nc = bass.Bass("TRN2", target_bir_lowering=False)
    qkvT = nc.dram_tensor("qkvT", [9 * 128, MT], F32, kind="ExternalInput").ap()
    ktm = nc.dram_tensor("ktm", [MT, 384], F32, kind="ExternalInput").ap()
    vtm = nc.dram_tensor("vtm", [MT, 384], F32, kind="ExternalInput").ap()
    zsd = nc.dram_tensor("zs", [MT, 384], F32, kind="ExternalInput").ap()
    bgd = nc.dram_tensor("bg", [MT, 12], F32, kind="ExternalInput").ap()
    cstd = nc.dram_tensor("cst", [128, 1280], F32, kind="ExternalInput").ap()
    ogT = nc.dram_tensor("ogT", [384, MT], BF16, kind="ExternalOutput").ap()
    oraw = nc.dram_tensor("oraw", [MT, 384], F32, kind="ExternalOutput").ap()
    NT = 18
    with ExitStack() as st:
        p = Prog(nc, st)
        pp = PsPool(p, 8)
        cst = p.sb("cst_sb", [128, 10, 128], F32)
        Bc = p.buf('cst')
        p.dma('sp', lambda e: e.dma_start(out=cst[:].rearrange("p a b -> p (a b)"), in_=cstd), writes=[Bc])
        TRI = [cst[:, 0, :], cst[:, 1, :]]
        BLK = cst[:, 2, :]
        HALF = [cst[:, 3, :], cst[:, 4, :]]
        NEGM = [cst[:, 5, :], cst[:, 6, :]]
        STRICT = [cst[:, 7, :], cst[:, 8, :]]
        IDENT = cst[:, 9, :]
        ones = p.sb("ones", [128, 128], F32)
        negones = p.sb("negones", [128, 128], F32)
        eps = p.sb("eps", [128, 1], F32)
        identb = p.sb("identb", [128, 128], BF16)
        ident4 = p.sb("ident4", [128, 4, 128], F32)
        negm4 = p.sb("negm4", [128, 4, 128], F32)
        strict4 = p.sb("strict4", [128, 4, 128], F32)
        Bk = p.buf('k')
        p.op('dve', lambda e: e.memset(ones[:], 1.0), writes=[Bk])
        p.op('dve', lambda e: e.memset(negones[:], -1.0), writes=[Bk])
        p.op('dve', lambda e: e.memset(eps[:], EPS), writes=[Bk])
        p.op('dve', lambda e: e.tensor_copy(out=identb[:], in_=IDENT), reads=[Bc], writes=[Bk])
        for q in range(4):
            p.op('dve', lambda e, q=q: e.tensor_copy(out=ident4[:, q, :], in_=IDENT), reads=[Bc], writes=[Bk])
            p.op('dve', lambda e, q=q: e.tensor_copy(out=negm4[:, q, :], in_=NEGM[q % 2]), reads=[Bc], writes=[Bk])
            p.op('dve', lambda e, q=q: e.tensor_copy(out=strict4[:, q, :], in_=STRICT[q % 2]), reads=[Bc], writes=[Bk])
        qT = p.sb("qT", [128, 3, MT], BF16)
        kT = p.sb("kT", [128, 3, MT], BF16)
        BqT = [p.buf('qT') for _ in range(3)]
        BkT = [p.buf('kT') for _ in range(3)]
        for hd in range(3):
            p.dma('pool', lambda e, hd=hd: e.dma_start(out=qT[:, hd, :], in_=qkvT[(3 * hd) * 128:(3 * hd + 1) * 128, :]),
                  writes=[BqT[hd]])
            p.dma('pool', lambda e, hd=hd: e.dma_start(out=kT[:, hd, :], in_=qkvT[(3 * hd + 1) * 128:(3 * hd + 2) * 128, :]),
                  writes=[BkT[hd]])
        bg = p.sb("bg_sb", [128, NT, 12], F32)
        Bbg = p.buf('bg')
        p.dma('sp', lambda e: e.dma_start(out=bg[:], in_=bgd.rearrange("(t p) c -> p t c", p=128)), writes=[Bbg])
        gc = p.sb("gc", [128, NT, 6], F32)
        eg = p.sb("eg", [128, NT, 6], F32)
        ekt = p.sb("ekt", [128, NT, 6], F32)
        bgc = p.sb("bgc", [128, NT, 6], F32)
        gtot = p.sb("gtot", [128, NT, 2, 6], F32)
        Bsc = p.buf('sc')
        pA, pAB = pp.next()
        pBk, pBB = pp.next()
        pC, pCB = pp.next()
        for t in range(NT):
            for d in range(2):
                p.op('pe', lambda e, t=t, d=d: e.matmul(pA[:, t * 6 + 3 * d:t * 6 + 3 * d + 3], lhsT=TRI[d],
                                                        rhs=bg[:, t, 6 + 3 * d:9 + 3 * d], start=True, stop=True),
                     reads=[Bc, Bbg], writes=[pAB])
            p.op('pe', lambda e, t=t: e.matmul(pBk[:, t * 6:t * 6 + 6], lhsT=BLK, rhs=bg[:, t, 6:12], start=True, stop=True),
                 reads=[Bc, Bbg], writes=[pBB])
            for hf in range(2):
                p.op('pe', lambda e, t=t, hf=hf: e.matmul(pC[:, (t * 2 + hf) * 6:(t * 2 + hf) * 6 + 6], lhsT=HALF[hf],
                                                          rhs=bg[:, t, 6:12], start=True, stop=True),
                     reads=[Bc, Bbg], writes=[pCB])
        fl = lambda a: a[:].rearrange("p t c -> p (t c)")
        p.op('dve', lambda e: e.tensor_copy(out=fl(gc), in_=pA[:, :NT * 6]), reads=[pAB], writes=[Bsc])
        p.op('act', lambda e: e.activation(out=fl(eg), in_=pA[:, :NT * 6], func=AF.Exp), reads=[pAB], writes=[Bsc])
        p.op('dve', lambda e: e.tensor_tensor(out=fl(ekt), in0=pBk[:, :NT * 6], in1=fl(gc), op=ALU.subtract),
             reads=[pBB, Bsc], writes=[Bsc])
        p.op('act', lambda e: e.activation(out=fl(ekt), in_=fl(ekt), func=AF.Exp), reads=[Bsc], writes=[Bsc])
        p.op('dve', lambda e: e.tensor_tensor(out=bgc[:], in0=bg[:, :, 0:6], in1=eg[:], op=ALU.mult), reads=[Bbg, Bsc], writes=[Bsc])
        p.op('act', lambda e: e.activation(out=gtot[:].rearrange("p t h c -> p (t h c)"), in_=pC[:, :NT * 12], func=AF.Exp),
             reads=[pCB], writes=[Bsc])
        attn = p.sb("attn", [128, 2 * NT, 128], BF16)
        ktl = p.sb("ktl", [128, 2 * NT, 128], BF16)
        u_sb = p.sb("u_sb", [128, 2 * NT, 128], F32)
        wT = p.sb("wT", [128, 2 * NT, 128], BF16)
        Bit = [p.buf('item') for _ in range(2 * NT)]
        o_d = [p.sb("o_d%d" % d, [128, NT, 128], F32) for d in range(2)]
        Bod = [[p.buf('od') for _ in range(NT)] for _ in range(2)]
        KK = p.sb("KK", [128, 2, 128], F32)
        QK = p.sb("QK", [128, 2, 128], F32)
        BKK = p.buf('KK')
        Ug4 = p.sb("Ug4", [128, 4, 128], F32)
        BUg = p.buf('Ug')
        tmp4 = p.sb("tmp4", [128, 4, 128], F32)
        Btmp = p.buf('tmp4')
        dec4 = p.sb("dec4", [128, 4, 128], F32)
        Bdec = p.buf('dec4')
        sdec4 = p.sb("sdec4", [128, 4, 128], F32)
        Bsdec = p.buf('sdec4')
        dgb4 = p.sb("dgb4", [128, 4, 128], F32)
        Bdgb = p.buf('dgb4')
        Mt = [p.sb("M4_%d" % i, [128, 4, 128], F32) for i in range(2)]
        Nt = [p.sb("N4_%d" % i, [128, 4, 128], F32) for i in range(2)]
        BM = [p.buf('M4') for _ in range(2)]
        BN = [p.buf('N4') for _ in range(2)]
        R4 = p.sb("R4", [128, 4, 128], F32)
        BR = p.buf('R4')
        Rb4 = p.sb("Rb4", [128, 4, 128], BF16)
        BRb = p.buf('Rb4')
        kvt = [p.sb("kvt%d" % i, [128, 2, 2, 128], F32) for i in range(2)]
        Bkvt = [p.buf('kvt') for _ in range(2)]
        vb4 = p.sb("vb4", [128, 4, 128], BF16)
        kbg4 = p.sb("kbg4", [128, 4, 128], BF16)
        Bvk = p.buf('vk4')
        S32 = [p.sb("S32_%d" % d, [128, 128], F32) for d in range(2)]
        Sbf = [p.sb("Sbf_%d" % d, [128, 128], BF16) for d in range(2)]
        BS32 = [p.buf('S32') for _ in range(2)]
        BSbf = [p.buf('Sbf') for _ in range(2)]
        vn = [[p.sb("vn%d_%d" % (d, i), [128, 128], BF16) for i in range(2)] for d in range(2)]
        Bvn = [[p.buf('vn') for _ in range(2)] for _ in range(2)]
        av = [[p.sb("av%d_%d" % (d, i), [128, 128], F32) for i in range(2)] for d in range(2)]
        Bav = [[p.buf('av') for _ in range(2)] for _ in range(2)]
        osum = p.sb("osum", [128, NT, 128], F32)
        osq = p.sb("osq", [128, NT, 128], F32)
        Bos = p.buf('osum')
        ssq = p.sb("ssq", [128, NT], F32)
        zt = [p.sb("zt%d" % i, [128, 128], F32) for i in range(2)]
        Bzt = [p.buf('zt') for _ in range(2)]
        ogt = [p.sb("ogt%d" % i, [128, 128], BF16) for i in range(2)]
        Bogt = [p.buf('ogt') for _ in range(2)]
        ogT_sb = p.sb("ogT_sb", [128, MT], BF16)
        BogT = p.buf('ogT')
        Bout = p.buf('out')
        outs = []
        kvc = 0

        order = [[(16, 0), (16, 1), (17, 0), (17, 1)] + [(t, hf) for t in range(16) for hf in range(2)],
                 [(17, 1), (17, 0), (16, 1), (16, 0)] + [(t, hf) for t in range(15, -1, -1) for hf in (1, 0)]]

        for hd in range(3):
            for t0 in range(0, NT, 2):
                items = [(t0 + q // 2, q % 2) for q in range(4)]
                pK, pKB = pp.next()
                for tt in range(2):
                    tk = (t0 + tt) * 128
                    p.op('pe', lambda e, tt=tt, tk=tk, hd=hd, pK=pK: e.matmul(
                        pK[:, tt * 128:(tt + 1) * 128], lhsT=kT[:, hd, tk:tk + 128], rhs=kT[:, hd, tk:tk + 128], start=True, stop=True),
                        reads=[BkT[hd]], writes=[pKB])
                    p.op('pe', lambda e, tt=tt, tk=tk, hd=hd, pK=pK: e.matmul(
                        pK[:, 256 + tt * 128:256 + (tt + 1) * 128], lhsT=kT[:, hd, tk:tk + 128], rhs=qT[:, hd, tk:tk + 128],
                        start=True, stop=True), reads=[BkT[hd], BqT[hd]], writes=[pKB])
                p.op('act', lambda e, pK=pK: e.copy(out=KK[:].rearrange("p a b -> p (a b)"), in_=pK[:, 0:256]), reads=[pKB], writes=[BKK])
                p.op('act', lambda e, pK=pK: e.copy(out=QK[:].rearrange("p a b -> p (a b)"), in_=pK[:, 256:512]), reads=[pKB], writes=[BKK])
                for q, (t, d) in enumerate(items):
                    col = d * 3 + hd
                    p.op('dve', lambda e, q=q, t=t, d=d, col=col: e.tensor_scalar_mul(
                        out=Ug4[:, q, :], in0=TRI[d], scalar1=bg[:, t, 6 + col:7 + col]), reads=[Bc, Bbg], writes=[BUg])
                    p.op('dve', lambda e, q=q, t=t, col=col: e.tensor_scalar_mul(
                        out=dgb4[:, q, :], in0=IDENT, scalar1=bg[:, t, col:col + 1]), reads=[Bc, Bbg], writes=[Bdgb])
                pD, pDB = pp.next()
                pRw, pRwB = pp.next()
                for q in range(4):
                    p.op('pe', lambda e, q=q, pD=pD: e.matmul(pD[:, q * 128:(q + 1) * 128], lhsT=ones[:], rhs=Ug4[:, q, :],
                                                              start=True, stop=False), reads=[Bk, BUg], writes=[pDB])
                    p.op('pe', lambda e, q=q, pD=pD: e.matmul(pD[:, q * 128:(q + 1) * 128], lhsT=Ug4[:, q, :], rhs=negones[:],
                                                              start=False, stop=True), reads=[Bk, BUg], writes=[pDB])
                    p.op('pe', lambda e, q=q, pRw=pRw: e.matmul(pRw[:, q * 128:(q + 1) * 128], lhsT=ones[:], rhs=dgb4[:, q, :],
                                                                start=True, stop=True), reads=[Bk, Bdgb], writes=[pRwB])
                f4 = lambda a: a[:].rearrange("p a b -> p (a b)")
                p.op('dve', lambda e, pD=pD: e.tensor_tensor(out=f4(tmp4), in0=pD[:, :512], in1=f4(negm4), op=ALU.add),
                     reads=[pDB, Bk], writes=[Btmp])
                p.op('act', lambda e: e.activation(out=f4(dec4), in_=f4(tmp4), func=AF.Exp), reads=[Btmp], writes=[Bdec])
                p.op('dve', lambda e: e.tensor_tensor(out=f4(sdec4), in0=f4(dec4), in1=f4(strict4), op=ALU.mult),
                     reads=[Bdec, Bk], writes=[Bsdec])
                d4v = lambda a, d: a[:].rearrange("p (t d) b -> p t d b", d=2)[:, :, d, :]
                for d in range(2):
                    p.op('dve', lambda e, d=d, t0=t0: e.tensor_tensor(
                        out=attn[:, 2 * t0:2 * t0 + 4, :].rearrange("p (t d) b -> p t d b", d=2)[:, :, d, :],
                        in0=QK[:], in1=d4v(dec4, d), op=ALU.mult),
                        reads=[BKK, Bdec], writes=[Bit[2 * t0 + d], Bit[2 * t0 + 2 + d]])
                    p.op('dve', lambda e, d=d: e.tensor_tensor(out=d4v(tmp4, d), in0=KK[:], in1=d4v(sdec4, d), op=ALU.mult),
                         reads=[BKK, Bsdec, Btmp], writes=[Btmp])
                M_c, N_c, BM_c, BN_c = Mt[0], Nt[0], BM[0], BN[0]
                M_n, N_n, BM_n, BN_n = Mt[1], Nt[1], BM[1], BN[1]
                p.op('dve', lambda e, pRw=pRw, M_c=M_c: e.scalar_tensor_tensor(out=f4(M_c), in0=f4(tmp4), scalar=-1.0, in1=pRw[:, :512],
                                                                                op0=ALU.mult, op1=ALU.mult),
                     reads=[Btmp, pRwB], writes=[BM_c])
                pN, pNB = pp.next()
                for q in range(4):
                    p.op('pe', lambda e, q=q, pN=pN, M_c=M_c: e.matmul(pN[:, q * 128:(q + 1) * 128], lhsT=M_c[:, q, :], rhs=IDENT,
                                                                       start=True, stop=True), reads=[BM_c, Bc], writes=[pNB])
                p.op('act', lambda e, pN=pN, N_c=N_c: e.copy(out=f4(N_c), in_=pN[:, :512]), reads=[pNB], writes=[BN_c])
                p.op('dve', lambda e, M_c=M_c: e.tensor_tensor(out=f4(R4), in0=f4(M_c), in1=f4(ident4), op=ALU.add),
                     reads=[BM_c, Bk], writes=[BR])
                for lev in range(5):
                    pNn, pNnB = pp.next()
                    for q in range(4):
                        p.op('pe', lambda e, q=q, pNn=pNn, M_c=M_c, N_c=N_c: e.matmul(
                            pNn[:, q * 128:(q + 1) * 128], lhsT=M_c[:, q, :], rhs=N_c[:, q, :], start=True, stop=True),
                            reads=[BM_c, BN_c], writes=[pNnB])
                    if lev < 4:
                        pMn, pMnB = pp.next()
                        for q in range(4):
                            p.op('pe', lambda e, q=q, pMn=pMn, M_c=M_c, N_c=N_c: e.matmul(
                                pMn[:, q * 128:(q + 1) * 128], lhsT=N_c[:, q, :], rhs=M_c[:, q, :], start=True, stop=True),
                                reads=[BM_c, BN_c], writes=[pMnB])
                    p.op('act', lambda e, pNn=pNn, N_n=N_n: e.copy(out=f4(N_n), in_=pNn[:, :512]), reads=[pNnB], writes=[BN_n])
                    if lev < 4:
                        p.op('dve', lambda e, pMn=pMn, M_n=M_n: e.tensor_copy(out=f4(M_n), in_=pMn[:, :512]), reads=[pMnB], writes=[BM_n])
                    pR, pRB = pp.next()
                    for q in range(4):
                        p.op('pe', lambda e, q=q, pR=pR, N_n=N_n: e.matmul(
                            pR[:, q * 128:(q + 1) * 128], lhsT=N_n[:, q, :], rhs=R4[:, q, :], start=True, stop=True),
                            reads=[BN_n, BR], writes=[pRB])
                    p.op('dve', lambda e, pR=pR: e.tensor_tensor(out=f4(R4), in0=f4(R4), in1=pR[:, :512], op=ALU.add),
                         reads=[BR, pRB], writes=[BR])
                    M_c, M_n, BM_c, BM_n = M_n, M_c, BM_n, BM_c
                    N_c, N_n, BN_c, BN_n = N_n, N_c, BN_n, BN_c
                p.op('act', lambda e: e.copy(out=f4(Rb4), in_=f4(R4)), reads=[BR], writes=[BRb])
                kv_t, Bkv = kvt[kvc % 2], Bkvt[kvc % 2]
                kvc += 1
                dl = []
                for tt in range(2):
                    r0 = (t0 + tt) * 128
                    dl.append(p.dma('sp', lambda e, tt=tt, r0=r0, hd=hd, kv_t=kv_t: e.dma_start(
                        out=kv_t[:, tt, 0, :], in_=ktm[r0:r0 + 128, hd * 128:(hd + 1) * 128]), writes=[Bkv]))
                    dl.append(p.dma('sp', lambda e, tt=tt, r0=r0, hd=hd, kv_t=kv_t: e.dma_start(
                        out=kv_t[:, tt, 1, :], in_=vtm[r0:r0 + 128, hd * 128:(hd + 1) * 128]), writes=[Bkv]))
                Prog.group(dl)
                for q, (t, d) in enumerate(items):
                    col = d * 3 + hd
                    tt = t - t0
                    p.op('pool', lambda e, q=q, t=t, col=col, tt=tt, kv_t=kv_t: e.tensor_scalar_mul(
                        out=vb4[:, q, :], in0=kv_t[:, tt, 1, :], scalar1=bg[:, t, col:col + 1]), reads=[Bkv, Bbg], writes=[Bvk])
                    p.op('pool', lambda e, q=q, t=t, col=col, tt=tt, kv_t=kv_t: e.tensor_scalar_mul(
                        out=kbg4[:, q, :], in0=kv_t[:, tt, 0, :], scalar1=bgc[:, t, col:col + 1]), reads=[Bkv, Bsc], writes=[Bvk])
                    p.op('pool', lambda e, t=t, d=d, col=col, tt=tt, kv_t=kv_t: e.tensor_scalar_mul(
                        out=ktl[:, 2 * t + d, :], in0=kv_t[:, tt, 0, :], scalar1=ekt[:, t, col:col + 1]), reads=[Bkv, Bsc],
                        writes=[Bit[2 * t + d]])
                pU, pUB = pp.next()
                pW, pWB = pp.next()
                for q in range(4):
                    p.op('pe', lambda e, q=q, pU=pU: e.matmul(pU[:, q * 128:(q + 1) * 128], lhsT=Rb4[:, q, :], rhs=vb4[:, q, :],
                                                              start=True, stop=True), reads=[BRb, Bvk], writes=[pUB])
                    p.op('pe', lambda e, q=q, pW=pW: e.matmul(pW[:, q * 128:(q + 1) * 128], lhsT=kbg4[:, q, :], rhs=Rb4[:, q, :],
                                                              start=True, stop=True), reads=[BRb, Bvk], writes=[pWB])
                p.op('act', lambda e, pU=pU, t0=t0: e.copy(out=u_sb[:, 2 * t0:2 * t0 + 4, :].rearrange("p a b -> p (a b)"), in_=pU[:, :512]),
                     reads=[pUB], writes=[Bit[2 * t0 + q] for q in range(4)])
                p.op('dve', lambda e, pW=pW, t0=t0: e.tensor_copy(out=wT[:, 2 * t0:2 * t0 + 4, :].rearrange("p a b -> p (a b)"), in_=pW[:, :512]),
                     reads=[pWB], writes=[Bit[2 * t0 + q] for q in range(4)])
            for d in range(2):
                p.op('dve', lambda e, d=d: e.memset(S32[d][:], 0.0), writes=[BS32[d]])
                p.op('dve', lambda e, d=d: e.memset(Sbf[d][:], 0.0), writes=[BSbf[d]])
            for step in range(36):
                for d in range(2):
                    t, hf = order[d][step]
                    it = 2 * t + d
                    col = d * 3 + hd
                    r0 = 64 * hf
                    tk = t * 128
                    vn_t, Bv = vn[d][step % 2], Bvn[d][step % 2]
                    av_t, Ba = av[d][step % 2], Bav[d][step % 2]
                    p1, p1B = pp.next()
                    p.op('pe', lambda e, p1=p1, it=it, d=d: e.matmul(p1[:, 0:128], lhsT=wT[:, it, :], rhs=Sbf[d][:], start=True, stop=True),
                         reads=[Bit[it], BSbf[d]], writes=[p1B])
                    p.op('pe', lambda e, p1=p1, tk=tk, hd=hd, d=d: e.matmul(p1[:, 128:256], lhsT=qT[:, hd, tk:tk + 128], rhs=Sbf[d][:],
                                                                              start=True, stop=True),
                         reads=[BqT[hd], BSbf[d]], writes=[p1B])
                    p.op('dve', lambda e, p1=p1, it=it, r0=r0, vn_t=vn_t: e.tensor_tensor(
                        out=vn_t[r0:r0 + 64, :], in0=u_sb[r0:r0 + 64, it, :], in1=p1[r0:r0 + 64, 0:128], op=ALU.subtract),
                        reads=[Bit[it], p1B], writes=[Bv])
                    p2, p2B = pp.next()
                    p.op('pe', lambda e, p2=p2, it=it, r0=r0, vn_t=vn_t: e.matmul(p2[:, 0:128], lhsT=ktl[r0:r0 + 64, it, :], rhs=vn_t[r0:r0 + 64, :],
                                                                                   start=True, stop=True),
                         reads=[Bit[it], Bv], writes=[p2B])
                    p.op('pe', lambda e, p2=p2, it=it, r0=r0, vn_t=vn_t: e.matmul(p2[:, 128:256], lhsT=attn[r0:r0 + 64, it, :], rhs=vn_t[r0:r0 + 64, :],
                                                                                   start=True, stop=True),
                         reads=[Bit[it], Bv], writes=[p2B])
                    p.op('dve', lambda e, p2=p2, d=d, t=t, hf=hf, col=col: e.scalar_tensor_tensor(
                        out=S32[d][:], in0=S32[d][:], scalar=gtot[:, t, hf, col:col + 1], in1=p2[:, 0:128], op0=ALU.mult, op1=ALU.add),
                        reads=[BS32[d], Bsc, p2B], writes=[BS32[d]])
                    p.op('act', lambda e, d=d: e.copy(out=Sbf[d][:], in_=S32[d][:]), reads=[BS32[d]], writes=[BSbf[d]])
                    p.op('act', lambda e, p2=p2, r0=r0, av_t=av_t: e.copy(out=av_t[r0:r0 + 64, :], in_=p2[r0:r0 + 64, 128:256]),
                         reads=[p2B], writes=[Ba])
                    p.op('dve', lambda e, p1=p1, d=d, t=t, r0=r0, col=col, av_t=av_t: e.scalar_tensor_tensor(
                        out=o_d[d][r0:r0 + 64, t, :], in0=p1[r0:r0 + 64, 128:256], scalar=eg[r0:r0 + 64, t, col:col + 1],
                        in1=av_t[r0:r0 + 64, :], op0=ALU.mult, op1=ALU.add),
                        reads=[p1B, Bsc, Ba], writes=[Bod[d][t]])
            allod = [Bod[d][t] for d in range(2) for t in range(NT)]
            fo = lambda a: a[:].rearrange("p t b -> p (t b)")
            p.op('dve', lambda e: e.tensor_tensor(out=fo(osum), in0=fo(o_d[0]), in1=fo(o_d[1]), op=ALU.add), reads=allod, writes=[Bos])
            outs.append(p.dma('sp', lambda e, hd=hd: e.dma_start(
                out=oraw[:, hd * 128:(hd + 1) * 128].rearrange("(t p) b -> p t b", p=128), in_=osum[:]),
                reads=[Bos], writes=[p.buf('oo')], sembuf=Bout))
            p.op('dve', lambda e: e.tensor_tensor(out=fo(osq), in0=fo(osum), in1=fo(osum), op=ALU.mult), reads=[Bos], writes=[Bos])
            p.op('dve', lambda e: e.reduce_sum(out=ssq[:], in_=osq[:], axis=AX.X), reads=[Bos], writes=[Bos])
            p.op('act', lambda e: e.activation(out=ssq[:], in_=ssq[:], func=AF.Sqrt, bias=eps[:, 0:1], scale=1.0 / 128),
                 reads=[Bos, Bk], writes=[Bos])
            p.op('dve', lambda e: e.reciprocal(out=ssq[:], in_=ssq[:]), reads=[Bos], writes=[Bos])
            for t in range(NT):
                z_t, Bz = zt[t % 2], Bzt[t % 2]
                g_t, Bg = ogt[t % 2], Bogt[t % 2]
                p.dma('sp', lambda e, t=t, hd=hd, z_t=z_t: e.dma_start(out=z_t[:], in_=zsd[t * 128:(t + 1) * 128, hd * 128:(hd + 1) * 128]),
                      writes=[Bz])
                p.op('dve', lambda e, t=t, z_t=z_t, g_t=g_t: e.scalar_tensor_tensor(
                    out=g_t[:], in0=osum[:, t, :], scalar=ssq[:, t:t + 1], in1=z_t[:], op0=ALU.mult, op1=ALU.mult),
                    reads=[Bos, Bz], writes=[Bg])
                pT, pTB = pp.next()
                p.op('pe', lambda e, pT=pT, g_t=g_t: e.matmul(pT[:, 0:128], lhsT=g_t[:], rhs=identb[:], start=True, stop=True),
                     reads=[Bg, Bk], writes=[pTB])
                p.op('act', lambda e, pT=pT, t=t: e.copy(out=ogT_sb[:, t * 128:(t + 1) * 128], in_=pT[:, 0:128]), reads=[pTB], writes=[BogT])
            outs.append(p.dma('sp', lambda e, hd=hd: e.dma_start(out=ogT[hd * 128:(hd + 1) * 128, :], in_=ogT_sb[:]),
                              reads=[BogT], writes=[p.buf('oo')], sembuf=Bout))
        p.emit(final_waits=outs)
    return nc


C_ = np.ascontiguousarray


def ada_tok(ada_l, b, vs):
    a = ada_l.reshape(9, KC, 128, 5)
    a_in = np.stack([a[vs, :, :, b], a[vs, :, :, 4]], axis=0)
    return C_(a_in.transpose(3, 0, 1, 2)).reshape(128, -1)


def ln_par(g, b):
    return C_(np.stack([fm(g), fm(b)], axis=1).reshape(128, -1))


def kernel(x, c, ctx, c_ctx, w_ada, b_ada, ln_g, ln_b, ffn1_w_gu, ffn1_w_down, w_in, na_rpb, cv_conv_w,
           gd_conv_w, gd_a_log, gd_dt_bias, gd_norm_w, w_out, ffn2_w_gu, ffn2_w_down):
    f32 = lambda a: np.asarray(a, dtype=np.float32)
    x, c, ctx, c_ctx = f32(x), f32(c), f32(ctx), f32(c_ctx)
    ada = run_ada(c, c_ctx, f32(w_ada), f32(b_ada))
    cosT, sinT, perm = gdn_consts()
    cst = gdn_masks()
    xT = []
    for i in range(8):
        b, hh = i // 2, i % 2
        xT.append(C_(np.concatenate([x[b, hh * 1024:(hh + 1) * 1024].T, ctx[b, hh * 128:(hh + 1) * 128].T], axis=1)))
    na_cols, cv_cols, gd_cols, zb_cols = [], [], [], []
    for hh in range(2):
        na_cols.append(np.concatenate([np.arange(j * 768 + (3 * hh + hd) * 128, j * 768 + (3 * hh + hd + 1) * 128)
                                       for j in range(3) for hd in range(3)]))
        cv_cols.append(np.concatenate([np.arange(2304 + j * 512 + (2 * hh + gi) * 128, 2304 + j * 512 + (2 * hh + gi + 1) * 128)
                                       for j in range(3) for gi in range(2)]))
        gd_cols.append(np.concatenate([np.arange(3840 + j * 768 + (3 * hh + hd) * 128, 3840 + j * 768 + (3 * hh + hd + 1) * 128)
                                       for hd in range(3) for j in range(3)]))
        zc = np.arange(6144 + 3 * hh * 128, 6144 + (3 * hh + 3) * 128)
        bc = np.array([6912 + d * 6 + 3 * hh + hd for d in range(2) for hd in range(3)])
        zb_cols.append(np.concatenate([zc, bc, bc + 12]))
    for l in range(DEPTH):
        wi = f32(w_in[l])
        wgu, wdn = f32(ffn1_w_gu[l]), f32(ffn1_w_down[l])
        lnp = ln_par(f32(ln_g[l, 0]), f32(ln_b[l, 0]))
        maps = [{"xT": xT[i], "ada": ada_tok(ada[l], i // 2, [0, 1, 2]), "lnp": lnp, "wgu": wgu, "wdn": wdn,
                 "adah": ada_tok(ada[l], i // 2, [3, 4])} for i in range(8)]
        res = run('pf0h', maps)
        x1T = [res[i]['yT'] for i in range(8)]
        hfull = []
        for b in range(4):
            h0, h1 = res[2 * b]['hT'], res[2 * b + 1]['hT']
            hfull.append(C_(np.concatenate([h0[:, :1024], h1[:, :1024], h0[:, 1024:], h1[:, 1024:]], axis=1)))
        wna = [C_(wi[:, na_cols[hh]]) for hh in range(2)]
        tabs = [na_tables(f32(na_rpb[l])[3 * hh:3 * hh + 3]) for hh in range(2)]
        r_ma = run('ma', [{"hT": hfull[i // 2], "wna": wna[i % 2], "btab": tabs[i % 2][0], "mask": tabs[i % 2][1]} for i in range(8)])
        wcv = [C_(wi[:, cv_cols[hh]]) for hh in range(2)]
        cvw = [C_(f32(cv_conv_w[l])[:, (2 * hh) * 128:(2 * hh + 2) * 128].reshape(3, 2, 128).transpose(2, 1, 0).reshape(128, 6))
               for hh in range(2)]
        r_mc = run('mc', [{"hT": hfull[i // 2], "wcv": wcv[i % 2], "cvw": cvw[i % 2]} for i in range(8)])
        gcw = f32(gd_conv_w[l])
        wqkv = [C_(wi[:, gd_cols[hh]]) for hh in range(2)]
        wzb = [C_(wi[:, zb_cols[hh]]) for hh in range(2)]
        cw = [C_(np.stack([gcw[:, j * 768 + (3 * hh + hd) * 128:j * 768 + (3 * hh + hd + 1) * 128] for hd in range(3) for j in range(3)],
                          axis=0).transpose(2, 0, 1)).reshape(128, 45) for hh in range(2)]
        gsel = lambda a, hh: np.array([a[d, 3 * hh + hd] for d in range(2) for hd in range(3)], np.float32)
        gpar = [C_(np.broadcast_to(np.concatenate([gsel(f32(gd_a_log[l]), hh), gsel(f32(gd_dt_bias[l]), hh)])[None, :], (128, 12)))
                for hh in range(2)]
        nwr = C_(np.broadcast_to(np.tile(f32(gd_norm_w[l]), 3)[None, :], (128, 384)))
        r_g1 = run('mg1', [{"hT": hfull[i // 2], "wqkv": wqkv[i % 2], "wzb": wzb[i % 2], "cw": cw[i % 2], "gpar": gpar[i % 2],
                            "nw": nwr, "cosT": cosT, "sinT": sinT, "perm": perm} for i in range(8)])
        maps = []
        for i in range(8):
            q3 = r_g1[i]['qkvT'].reshape(3, 3, 128, MT)
            maps.append({"qkvT": r_g1[i]['qkvT'], "ktm": C_(q3[:, 1].transpose(2, 0, 1).reshape(MT, 384)),
                         "vtm": C_(q3[:, 2].transpose(2, 0, 1).reshape(MT, 384)), "zs": r_g1[i]['zs'], "bg": r_g1[i]['bg'], "cst": cst})
        r_g2 = run('mg2', maps)
        omT = []
        for i in range(8):
            b, hh = i // 2, i % 2
            om = np.concatenate([r_ma[2 * b]['ona'], r_ma[2 * b + 1]['ona'], r_mc[2 * b]['ocv'], r_mc[2 * b + 1]['ocv'],
                                 r_g2[2 * b]['ogT'], r_g2[2 * b + 1]['ogT']], axis=0)
            omT.append(C_(np.concatenate([om[:, hh * 1024:(hh + 1) * 1024], om[:, 2048 + hh * 128:2048 + (hh + 1) * 128]], axis=1)))
        wgu, wdn, wo = f32(ffn2_w_gu[l]), f32(ffn2_w_down[l]), f32(w_out[l])
        lnp = ln_par(f32(ln_g[l, 2]), f32(ln_b[l, 2]))
        lnpm = ln_par(f32(ln_g[l, 1]), f32(ln_b[l, 1]))
        maps = [{"xT": x1T[i], "ada": ada_tok(ada[l], i // 2, [6, 7, 8]), "lnp": lnp, "wgu": wgu, "wdn": wdn, "omT": omT[i], "wo": wo,
                 "adam": ada_tok(ada[l], i // 2, [5]), "lnpm": lnpm} for i in range(8)]
        res = run('pf1', maps)
        xT = [res[i]['yT'] for i in range(8)]
    out = np.empty((4, 2048, D), np.float32)
    for i in range(8):
        b, hh = i // 2, i % 2
        out[b, hh * 1024:(hh + 1) * 1024, :] = xT[i][:, :1024].T
    return out
```

```python
import numpy as np
from contextlib import ExitStack
import ml_dtypes
import concourse.bass as bass
import concourse.mybir as mybir
from concourse.bass_utils import run_bass_kernel_spmd

F32 = mybir.dt.float32
BF16 = mybir.dt.bfloat16
AF = mybir.ActivationFunctionType
ALU = mybir.AluOpType
AX = mybir.AxisListType

D = 2048
KC = 16
DFF = 5632
DEPTH = 4
ALPHA = (2 * DEPTH) ** 0.25
EPS = 1e-6
T = 1152
GROUPS = [(0, 512, 0), (512, 512, 0), (1024, 128, 1)]

ENGS = ('pe', 'act', 'dve', 'pool', 'sp')


class Buf:
    __slots__ = ('name', 'writers', 'readers', 'dsem', 'dcount')

    def __init__(self, name):
        self.name = name
        self.writers = []
        self.readers = []
        self.dsem = None
        self.dcount = 0


class Op:
    __slots__ = ('eng', 'fn', 'deps', 'marked', 'sem', 'val', 'is_dma', 'grp', 'idx')

    def __init__(self, eng, fn, deps, is_dma=False):
        self.eng = eng
        self.fn = fn
        self.deps = deps
        self.marked = False
        self.sem = None
        self.val = None
        self.is_dma = is_dma
        self.grp = None
        self.idx = 0


class Prog:
    def __init__(self, nc, stack):
        self.nc = nc
        self.stack = stack
        self.lists = {e: [] for e in ENGS}
        self.esem = {e: stack.enter_context(nc.semaphore('s_' + e)) for e in ENGS}
        self.nbuf = 0

    def buf(self, name=None):
        self.nbuf += 1
        return Buf((name or 'b') + str(self.nbuf))

    def sb(self, name, shape, dt):
        return self.stack.enter_context(self.nc.sbuf_tensor(name, list(shape), dt))

    def ps(self, name, shape, dt=F32):
        return self.stack.enter_context(self.nc.psum_tensor(name, list(shape), dt))

    def _deps(self, reads, writes):
        deps = []
        for b in reads:
            deps.extend(b.writers)
        for b in writes:
            deps.extend(b.readers)
            deps.extend(b.writers)
        last = None
        for d in deps:
            if d.eng == 'pe' and not d.is_dma and (last is None or d.idx > last.idx):
                last = d
        if last is not None:
            deps = [d for d in deps if not (d.eng == 'pe' and not d.is_dma) or d is last]
        return deps

    def _commit(self, op, reads, writes):
        for b in reads:
            b.readers.append(op)
        for b in writes:
            if b.readers:
                b.readers = []
                b.writers = [op]
            else:
                b.writers.append(op)

    def op(self, eng, fn, reads=(), writes=()):
        o = Op(eng, fn, self._deps(reads, writes))
        self._commit(o, reads, writes)
        self.lists[eng].append(o)
        o.idx = len(self.lists[eng])
        return o

    def dma(self, q, fn, reads=(), writes=(), sembuf=None):
        if sembuf is None:
            sembuf = writes[0]
        if sembuf.dsem is None:
            sembuf.dsem = self.stack.enter_context(self.nc.semaphore('d_' + sembuf.name))
        o = Op(q, fn, self._deps(reads, writes), is_dma=True)
        sembuf.dcount += 1
        o.sem = sembuf.dsem
        o.val = 16 * sembuf.dcount
        o.marked = True
        self._commit(o, reads, writes)
        self.lists[q].append(o)
        return o

    @staticmethod
    def group(ops):
        v = max(o.val for o in ops)
        for o in ops:
            o.val = v
            o.grp = ops[0]

    def emit(self, final_waits=()):
        nc = self.nc
        for e in ENGS:
            for o in self.lists[e]:
                for d in o.deps:
                    if d.eng == 'pe' and o.eng == 'pe' and not d.is_dma and not o.is_dma:
                        continue
                    d.marked = True
        for e in ENGS:
            c = 0
            for o in self.lists[e]:
                if o.is_dma:
                    continue
                if o.marked:
                    c += 1
                    o.sem = self.esem[e]
                    o.val = c
        lists = self.lists
        fw = {}
        for o in final_waits:
            k = id(o.sem)
            if k not in fw or fw[k][1] < o.val:
                fw[k] = (o.sem, o.val)

        def run(e, eng):
            wm = {}
            for o in lists[e]:
                need = {}
                for d in o.deps:
                    if d.eng == 'pe' and e == 'pe' and not d.is_dma and not o.is_dma:
                        continue
                    if o.grp is not None and d.grp is o.grp:
                        continue
                    k = id(d.sem)
                    if wm.get(k, 0) >= d.val:
                        continue
                    if k not in need or need[k][1] < d.val:
                        need[k] = (d.sem, d.val)
                for k, (s, v) in need.items():
                    eng.wait_ge(s, v)
                    wm[k] = v
                ins = o.fn(eng)
                if o.is_dma:
                    ins.then_inc(o.sem, 16)
                elif o.marked:
                    ins.then_inc(o.sem, 1)
            if e == 'sp':
                for k, (s, v) in fw.items():
                    eng.wait_ge(s, v)

        with nc.Block() as block:
            @block.tensor
            def _(eng):
                run('pe', eng)

            @block.scalar
            def _(eng):
                run('act', eng)

            @block.vector
            def _(eng):
                run('dve', eng)

            @block.gpsimd
            def _(eng):
                run('pool', eng)

            @block.sync
            def _(eng):
                run('sp', eng)


class PsPool:
    def __init__(self, p, n=8):
        self.t = [p.ps('psb%d' % i, [128, 512]) for i in range(n)]
        self.b = [p.buf('psb') for _ in range(n)]
        self.i = 0
        self.n = n

    def next(self):
        i = self.i
        self.i = (i + 1) % self.n
        return self.t[i], self.b[i]


def build_ada():
    nc = bass.Bass("TRN2", target_bir_lowering=False)
    cT = nc.dram_tensor("cT", [128, KC * 5], F32, kind="ExternalInput").ap()
    w = nc.dram_tensor("w", [DEPTH, D, 2304], F32, kind="ExternalInput").ap()
    b = nc.dram_tensor("b", [128, 72], F32, kind="ExternalInput").ap()
    out = nc.dram_tensor("out", [128, 72 * 5], F32, kind="ExternalOutput").ap()
    with ExitStack() as st:
        p = Prog(nc, st)
        c_sb = p.sb("c_sb", [128, KC, 5], F32)
        s_sb = p.sb("s_sb", [128, KC, 5], F32)
        b_sb = p.sb("b_sb", [128, 72], F32)
        o_sb = p.sb("o_sb", [128, 72, 5], F32)
        NWB = 3
        wb = [p.sb("wb%d" % i, [128, KC, 128], F32) for i in range(NWB)]
        ps = p.ps("ps", [128, 72, 5])
        Bc, Bs, Bb, Bo, Bps, Bout = [p.buf(n) for n in ('c', 's', 'b', 'o', 'ps', 'out')]
        Bw = [p.buf('w') for _ in range(NWB)]
        p.dma('sp', lambda e: e.dma_start(out=c_sb[:].rearrange("p k r -> p (k r)"), in_=cT), writes=[Bc])
        p.dma('sp', lambda e: e.dma_start(out=b_sb[:], in_=b), writes=[Bb])
        p.op('act', lambda e: e.activation(out=s_sb[:], in_=c_sb[:], func=AF.Silu), reads=[Bc], writes=[Bs])
        idx = 0
        for l in range(DEPTH):
            for m in range(18):
                src = w[l, :, m * 128:(m + 1) * 128].rearrange("(k p) n -> p k n", p=128)
                t = wb[idx % NWB]
                q = 'sp' if idx % 2 == 0 else 'act'
                p.dma(q, lambda e, t=t, src=src: e.dma_start(out=t[:], in_=src), writes=[Bw[idx % NWB]])
                for kc in range(KC):
                    p.op('pe', lambda e, t=t, kc=kc, idx=idx: e.matmul(
                        ps[:, idx, :], lhsT=t[:, kc, :], rhs=s_sb[:, kc, :], start=(kc == 0), stop=(kc == KC - 1)),
                        reads=[Bw[idx % NWB], Bs], writes=[Bps])
                idx += 1
        for r in range(5):
            p.op('dve', lambda e, r=r: e.tensor_tensor(out=o_sb[:, :, r], in0=ps[:, :, r], in1=b_sb[:], op=ALU.add),
                 reads=[Bps, Bb], writes=[Bo])
        o = p.dma('sp', lambda e: e.dma_start(out=out, in_=o_sb[:].rearrange("p m r -> p (m r)")), reads=[Bo],
                  writes=[Bout])
        p.emit(final_waits=[o])
    return nc


class TokCtx:
    pass


def ln_stats(p, c, src_sb, srcB):
    for g, (s0, n, kind) in enumerate(GROUPS):
        sl = slice(s0, s0 + n)
        p.op('dve', lambda e, sl=sl, n=n: e.tensor_tensor(out=c.acc1[:, :n], in0=src_sb[:, 0, sl], in1=src_sb[:, 1, sl],
                                                           op=ALU.add),
             reads=[srcB[0][g], srcB[1][g]], writes=[c.Bacc1])
        for k in range(2, KC):
            p.op('dve', lambda e, sl=sl, n=n, k=k: e.tensor_tensor(out=c.acc1[:, :n], in0=c.acc1[:, :n],
                                                                    in1=src_sb[:, k, sl], op=ALU.add),
                 reads=[srcB[k][g], c.Bacc1], writes=[c.Bacc1])
        p.op('act', lambda e, sl=sl, n=n: e.activation(out=c.acc2[:, :n], in_=src_sb[:, 0, sl], func=AF.Square),
             reads=[srcB[0][g]], writes=[c.Bacc2])
        for k in range(1, KC):
            sq, Bsq = c.sq[k % 2], c.Bsq[k % 2]
            p.op('act', lambda e, sl=sl, n=n, k=k, sq=sq: e.activation(out=sq[:, :n], in_=src_sb[:, k, sl],
                                                                       func=AF.Square),
                 reads=[srcB[k][g]], writes=[Bsq])
            p.op('dve', lambda e, n=n, sq=sq: e.tensor_tensor(out=c.acc2[:, :n], in0=c.acc2[:, :n], in1=sq[:, :n],
                                                              op=ALU.add),
                 reads=[Bsq, c.Bacc2], writes=[c.Bacc2])
        ps1, B1 = c.pp.next()
        ps2, B2 = c.pp.next()
        p.op('pe', lambda e, n=n, ps1=ps1: e.matmul(ps1[:, :n], lhsT=c.ones[:], rhs=c.acc1[:, :n], start=True, stop=True),
             reads=[c.Bones, c.Bacc1], writes=[B1])
        p.op('pe', lambda e, n=n, ps2=ps2: e.matmul(ps2[:, :n], lhsT=c.ones[:], rhs=c.acc2[:, :n], start=True, stop=True),
             reads=[c.Bones, c.Bacc2], writes=[B2])
        p.op('act', lambda e, n=n, ps1=ps1: e.mul(out=c.mean[:, :n], in_=ps1[:, :n], mul=1.0 / D),
             reads=[B1], writes=[c.Bmean])
        p.op('dve', lambda e, n=n: e.tensor_tensor(out=c.msq[:, :n], in0=c.mean[:, :n], in1=c.mean[:, :n], op=ALU.mult),
             reads=[c.Bmean], writes=[c.Bmsq])
        p.op('dve', lambda e, n=n, ps2=ps2: e.scalar_tensor_tensor(out=c.msq[:, :n], in0=ps2[:, :n], scalar=1.0 / D,
                                                                   in1=c.msq[:, :n], op0=ALU.mult, op1=ALU.subtract),
             reads=[B2, c.Bmsq], writes=[c.Bmsq])
        p.op('act', lambda e, n=n: e.activation(out=c.msq[:, :n], in_=c.msq[:, :n], func=AF.Sqrt, bias=c.eps[:, 0:1],
                                                scale=1.0),
             reads=[c.Bmsq, c.Bones], writes=[c.Bmsq])
        p.op('dve', lambda e, sl=sl, n=n: e.reciprocal(out=c.rstd[:, sl], in_=c.msq[:, :n]),
             reads=[c.Bmsq], writes=[c.Brstd[g]])
        p.op('dve', lambda e, sl=sl, n=n: e.scalar_tensor_tensor(out=c.nmr[:, sl], in0=c.mean[:, :n], scalar=-1.0,
                                                                  in1=c.rstd[:, sl], op0=ALU.mult, op1=ALU.mult),
             reads=[c.Bmean, c.Brstd[g]], writes=[c.Bnmr[g]])


def ln_apply(p, c, src_sb, srcB, dst_sb, dstB, scale_ap, bias_ap, parB):
    i = 0
    for g, (s0, n, kind) in enumerate(GROUPS):
        sl = slice(s0, s0 + n)
        for k in range(KC):
            t1, Bt1 = c.t1[i % 2], c.Bt1[i % 2]
            i += 1
            p.op('dve', lambda e, sl=sl, n=n, k=k, t1=t1: e.tensor_tensor(out=t1[:, :n], in0=src_sb[:, k, sl],
                                                                           in1=c.rstd[:, sl], op=ALU.mult),
                 reads=[srcB[k][g], c.Brstd[g]], writes=[Bt1])
            p.op('dve', lambda e, sl=sl, n=n, t1=t1: e.tensor_tensor(out=t1[:, :n], in0=t1[:, :n], in1=c.nmr[:, sl],
                                                                      op=ALU.add),
                 reads=[Bt1, c.Bnmr[g]], writes=[Bt1])
            p.op('act', lambda e, sl=sl, n=n, k=k, t1=t1, kind=kind: e.activation(
                out=dst_sb[:, k, sl], in_=t1[:, :n], func=AF.Identity, scale=scale_ap(kind, k), bias=bias_ap(kind, k)),
                reads=[Bt1, parB], writes=[dstB[k][g]])


def tok_common(p, nc):
    c = TokCtx()
    c.pp = PsPool(p, 8)
    c.ones = p.sb("ones", [128, 128], F32)
    c.eps = p.sb("eps", [128, 1], F32)
    c.Bones = p.buf('ones')
    p.op('dve', lambda e: e.memset(c.ones[:], 1.0), writes=[c.Bones])
    p.op('dve', lambda e: e.memset(c.eps[:], EPS), writes=[c.Bones])
    for nm in ('acc1', 'acc2', 'mean', 'msq'):
        setattr(c, nm, p.sb(nm, [128, 512], F32))
        setattr(c, 'B' + nm, p.buf(nm))
    c.sq = [p.sb("sq%d" % i, [128, 512], F32) for i in range(2)]
    c.Bsq = [p.buf('sq') for _ in range(2)]
    c.t1 = [p.sb("t1_%d" % i, [128, 512], F32) for i in range(2)]
    c.Bt1 = [p.buf('t1') for _ in range(2)]
    c.rstd = p.sb("rstd", [128, T], F32)
    c.nmr = p.sb("nmr", [128, T], F32)
    c.Brstd = [p.buf('rstd') for _ in GROUPS]
    c.Bnmr = [p.buf('nmr') for _ in GROUPS]
    return c


NPH = 4
PHC = 11


def build_pf(prefix, nph=NPH, dbg=None, mixh=False):
    nc = bass.Bass("TRN2", target_bir_lowering=False)
    xT = nc.dram_tensor("xT", [D, T], F32, kind="ExternalInput").ap()
    ada = nc.dram_tensor("ada", [128, 2 * 3 * KC], F32, kind="ExternalInput").ap()
    lnp = nc.dram_tensor("lnp", [128, 2 * KC], F32, kind="ExternalInput").ap()
    if dbg is None:
        wgu = nc.dram_tensor("wgu", [D, 2 * DFF], F32, kind="ExternalInput").ap()
        wdn = nc.dram_tensor("wdn", [DFF, D], F32, kind="ExternalInput").ap()
    if prefix:
        omT = nc.dram_tensor("omT", [D, T], BF16, kind="ExternalInput").ap()
        wo = nc.dram_tensor("wo", [D, D], F32, kind="ExternalInput").ap()
        adam = nc.dram_tensor("adam", [128, 2 * KC], F32, kind="ExternalInput").ap()
        lnpm = nc.dram_tensor("lnpm", [128, 2 * KC], F32, kind="ExternalInput").ap()
    yT = nc.dram_tensor("yT", [D, T], F32, kind="ExternalOutput").ap()
    if mixh:
        adah = nc.dram_tensor("adah", [128, 2 * 2 * KC], F32, kind="ExternalInput").ap()
        hTo = nc.dram_tensor("hT", [D, T], BF16, kind="ExternalOutput").ap()
    NG = len(GROUPS)
    with ExitStack() as st:
        p = Prog(nc, st)
        c = tok_common(p, nc)
        pp = c.pp
        x_sb = p.sb("x_sb", [128, KC, T], F32)
        h_sb = p.sb("h_sb", [128, KC, T], BF16)
        a_sb = p.sb("a_sb", [128, PHC, T], BF16)
        xB = [[p.buf('x') for _ in range(NG)] for _ in range(KC)]
        hB = [[p.buf('h') for _ in range(NG)] for _ in range(KC)]
        aB = [[p.buf('a') for _ in range(NG)] for _ in range(PHC)]
        ada_sb = p.sb("ada_sb", [128, 2, 3, KC], F32)
        sc1p = p.sb("sc1p", [128, 2, KC], F32)
        hgate = p.sb("hgate", [128, 2, KC], F32)
        lnp_sb = p.sb("lnp_sb", [128, 2, KC], F32)
        Bpar = p.buf('par')
        NWG = 3
        wg_sb = [p.sb("wg%d" % i, [128, 2, KC, 128], BF16) for i in range(NWG)]
        Bwg = [p.buf('wg') for _ in range(NWG)]
        NWD = 3
        wd_sb = [p.sb("wd%d" % i, [128, PHC, 128], BF16) for i in range(NWD)]
        Bwd = [p.buf('wd') for _ in range(NWD)]
        sg_sb = [p.sb("sg%d" % i, [128, 512], F32) for i in range(2)]
        Bsg = [p.buf('sg') for _ in range(2)]
        Bout = p.buf('out')

        lo = []
        for q4 in range(4):
            src = xT[q4 * 512:(q4 + 1) * 512, :].rearrange("(k p) t -> p k t", p=128)
            lo.append(p.dma('sp', lambda e, q4=q4, src=src: e.dma_start(out=x_sb[:, 4 * q4:4 * q4 + 4, :], in_=src),
                            writes=[xB[k][g] for k in range(4 * q4, 4 * q4 + 4) for g in range(NG)], sembuf=xB[4 * q4][0]))
        po = [p.dma('sp', lambda e: e.dma_start(out=ada_sb[:].rearrange("p a b k -> p (a b k)"), in_=ada), writes=[Bpar]),
              p.dma('sp', lambda e: e.dma_start(out=lnp_sb[:].rearrange("p a k -> p (a k)"), in_=lnp), writes=[Bpar])]
        if prefix:
            adam_sb = p.sb("adam_sb", [128, 2, KC], F32)
            lnpm_sb = p.sb("lnpm_sb", [128, 2, KC], F32)
            po.append(p.dma('sp', lambda e: e.dma_start(out=adam_sb[:].rearrange("p a k -> p (a k)"), in_=adam),
                            writes=[Bpar]))
            po.append(p.dma('sp', lambda e: e.dma_start(out=lnpm_sb[:].rearrange("p a k -> p (a k)"), in_=lnpm),
                            writes=[Bpar]))
            for q4 in range(4):
                src = omT[q4 * 512:(q4 + 1) * 512, :].rearrange("(k p) t -> p k t", p=128)
                p.dma('sp', lambda e, q4=q4, src=src: e.dma_start(out=h_sb[:, 4 * q4:4 * q4 + 4, :], in_=src),
                      writes=[hB[k][g] for k in range(4 * q4, 4 * q4 + 4) for g in range(NG)], sembuf=hB[4 * q4][0])
        Prog.group(po)
        p.op('dve', lambda e: e.tensor_scalar_add(out=sc1p[:], in0=ada_sb[:, :, 1, :], scalar1=1.0),
             reads=[Bpar], writes=[Bpar])
        p.op('dve', lambda e: e.tensor_scalar_mul(out=hgate[:], in0=ada_sb[:, :, 2, :], scalar1=0.5),
             reads=[Bpar], writes=[Bpar])

        wq = []
        if prefix:
            for m in range(KC):
                wq.append(('wo', m))
        for ph in range(nph if dbg is None else 0):
            for j in range(PHC):
                wq.append(('gu', ph * PHC + j))
            for m in range(KC):
                wq.append(('dn', (ph, m)))
        cnt = {'g': 0, 'd': 0}
        slot = {}
        occ = {}
        cur = [0]

        def can_issue(i):
            kind, a = wq[i]
            if kind in ('wo', 'gu'):
                key = ('g', cnt['g'] % NWG)
            else:
                key = ('d', cnt['d'] % NWD)
            return key not in occ or occ[key] < cur[0]

        def issue(i):
            kind, a = wq[i]
            if kind == 'wo':
                s = cnt['g'] % NWG
                cnt['g'] += 1
                src = wo[:, a * 128:(a + 1) * 128].rearrange("(k p) n -> p k n", p=128)
                p.dma('pool', lambda e, s=s, src=src: e.dma_start(out=wg_sb[s][:, 0, :, :], in_=src), writes=[Bwg[s]])
            elif kind == 'gu':
                s = cnt['g'] % NWG
                cnt['g'] += 1
                sg_ = wgu[:, a * 128:(a + 1) * 128].rearrange("(k p) n -> p k n", p=128)
                su_ = wgu[:, DFF + a * 128:DFF + (a + 1) * 128].rearrange("(k p) n -> p k n", p=128)
                o1 = p.dma('pool', lambda e, s=s, src=sg_: e.dma_start(out=wg_sb[s][:, 0, :, :], in_=src), writes=[Bwg[s]])
                o2 = p.dma('pool', lambda e, s=s, src=su_: e.dma_start(out=wg_sb[s][:, 1, :, :], in_=src), writes=[Bwg[s]])
                Prog.group([o1, o2])
            else:
                ph, m = a
                s = cnt['d'] % NWD
                cnt['d'] += 1
                src = wdn[ph * PHC * 128:(ph + 1) * PHC * 128, m * 128:(m + 1) * 128].rearrange("(k p) n -> p k n", p=128)
                p.dma('pool', lambda e, s=s, src=src: e.dma_start(out=wd_sb[s][:], in_=src), writes=[Bwd[s]])
            slot[i] = s
            occ[('g' if kind in ('wo', 'gu') else 'd', s)] = i

        nxt = [0]

        def prefetch(upto):
            cur[0] = upto - 2
            while nxt[0] < len(wq) and nxt[0] <= upto and can_issue(nxt[0]):
                issue(nxt[0])
                nxt[0] += 1

        wi = 0

        def scale_x_alpha():
            for g, (s0, n, kind) in enumerate(GROUPS):
                for k in range(KC):
                    p.op('act', lambda e, k=k, s0=s0, n=n: e.mul(out=x_sb[:, k, s0:s0 + n], in_=x_sb[:, k, s0:s0 + n],
                                                                 mul=ALPHA),
                         reads=[xB[k][g]], writes=[xB[k][g]])

        if prefix:
            scale_x_alpha()
            for m in range(KC):
                prefetch(wi + 2)
                s = slot[wi]
                wi += 1
                for g, (s0, n, kind) in enumerate(GROUPS):
                    pt, pB = pp.next()
                    for kc in range(KC):
                        p.op('pe', lambda e, pt=pt, s=s, kc=kc, s0=s0, n=n: e.matmul(
                            pt[:, :n], lhsT=wg_sb[s][:, 0, kc, :], rhs=h_sb[:, kc, s0:s0 + n],
                            start=(kc == 0), stop=(kc == KC - 1)),
                            reads=[Bwg[s], hB[kc][g]], writes=[pB])
                    p.op('dve', lambda e, pt=pt, m=m, s0=s0, n=n, kind=kind: e.scalar_tensor_tensor(
                        out=x_sb[:, m, s0:s0 + n], in0=pt[:, :n], scalar=adam_sb[:, kind, m:m + 1],
                        in1=x_sb[:, m, s0:s0 + n], op0=ALU.mult, op1=ALU.add),
                        reads=[pB, Bpar, xB[m][g]], writes=[xB[m][g]])
            ln_stats(p, c, x_sb, xB)
            ln_apply(p, c, x_sb, xB, x_sb, xB, lambda kind, k: lnpm_sb[:, 0, k:k + 1],
                     lambda kind, k: lnpm_sb[:, 1, k:k + 1], Bpar)

        if dbg is None:
            ln_stats(p, c, x_sb, xB)
            ln_apply(p, c, x_sb, xB, h_sb, hB, lambda kind, k: sc1p[:, kind, k:k + 1],
                     lambda kind, k: ada_sb[:, kind, 0, k:k + 1], Bpar)
            scale_x_alpha()
        ev = 0
        for ph in range(nph if dbg is None else 0):
            for j in range(PHC):
                prefetch(wi + 2)
                s = slot[wi]
                wi += 1
                for g, (s0, n, kind) in enumerate(GROUPS):
                    pg, pgB = pp.next()
                    pu, puB = pp.next()
                    for kc in range(KC):
                        p.op('pe', lambda e, pg=pg, s=s, kc=kc, s0=s0, n=n: e.matmul(
                            pg[:, :n], lhsT=wg_sb[s][:, 0, kc, :], rhs=h_sb[:, kc, s0:s0 + n],
                            start=(kc == 0), stop=(kc == KC - 1)),
                            reads=[Bwg[s], hB[kc][g]], writes=[pgB])
                    for kc in range(KC):
                        p.op('pe', lambda e, pu=pu, s=s, kc=kc, s0=s0, n=n: e.matmul(
                            pu[:, :n], lhsT=wg_sb[s][:, 1, kc, :], rhs=h_sb[:, kc, s0:s0 + n],
                            start=(kc == 0), stop=(kc == KC - 1)),
                            reads=[Bwg[s], hB[kc][g]], writes=[puB])
                    sgt, sgB = sg_sb[ev % 2], Bsg[ev % 2]
                    ev += 1
                    p.op('act', lambda e, pg=pg, sgt=sgt, n=n: e.activation(out=sgt[:, :n], in_=pg[:, :n], func=AF.Silu),
                         reads=[pgB], writes=[sgB])
                    p.op('dve', lambda e, pu=pu, sgt=sgt, j=j, s0=s0, n=n: e.tensor_tensor(
                        out=a_sb[:, j, s0:s0 + n], in0=sgt[:, :n], in1=pu[:, :n], op=ALU.mult),
                        reads=[sgB, puB], writes=[aB[j][g]])
            for m in range(KC):
                prefetch(wi + 2)
                s = slot[wi]
                wi += 1
                for g, (s0, n, kind) in enumerate(GROUPS):
                    pt, pB = pp.next()
                    for j in range(PHC):
                        p.op('pe', lambda e, pt=pt, s=s, j=j, s0=s0, n=n: e.matmul(
                            pt[:, :n], lhsT=wd_sb[s][:, j, :], rhs=a_sb[:, j, s0:s0 + n],
                            start=(j == 0), stop=(j == PHC - 1)),
                            reads=[Bwd[s], aB[j][g]], writes=[pB])
                    p.op('dve', lambda e, pt=pt, m=m, s0=s0, n=n, kind=kind: e.scalar_tensor_tensor(
                        out=x_sb[:, m, s0:s0 + n], in0=pt[:, :n], scalar=hgate[:, kind, m:m + 1],
                        in1=x_sb[:, m, s0:s0 + n], op0=ALU.mult, op1=ALU.add),
                        reads=[pB, Bpar, xB[m][g]], writes=[xB[m][g]])
        if dbg != 'copy':
            ln_stats(p, c, x_sb, xB)
            ln_apply(p, c, x_sb, xB, x_sb, xB, lambda kind, k: lnp_sb[:, 0, k:k + 1],
                     lambda kind, k: lnp_sb[:, 1, k:k + 1], Bpar)
        outs = []
        if mixh:
            adah_sb = p.sb("adah_sb", [128, 2, 2, KC], F32)
            Bah = p.buf('adah')
            p.dma('sp', lambda e: e.dma_start(out=adah_sb[:].rearrange("p a b k -> p (a b k)"), in_=adah), writes=[Bah])
            p.op('dve', lambda e: e.tensor_scalar_add(out=adah_sb[:, :, 1, :], in0=adah_sb[:, :, 1, :], scalar1=1.0),
                 reads=[Bah], writes=[Bah])
            ln_stats(p, c, x_sb, xB)
            ln_apply(p, c, x_sb, xB, h_sb, hB, lambda kind, k: adah_sb[:, kind, 1, k:k + 1],
                     lambda kind, k: adah_sb[:, kind, 0, k:k + 1], Bah)
            for q4 in range(4):
                dst = hTo[q4 * 512:(q4 + 1) * 512, :].rearrange("(k p) t -> p k t", p=128)
                outs.append(p.dma('sp', lambda e, q4=q4, dst=dst: e.dma_start(out=dst, in_=h_sb[:, 4 * q4:4 * q4 + 4, :]),
                                  reads=[hB[k][g] for k in range(4 * q4, 4 * q4 + 4) for g in range(NG)], writes=[p.buf('o')],
                                  sembuf=Bout))
        for q4 in range(4):
            dst = yT[q4 * 512:(q4 + 1) * 512, :].rearrange("(k p) t -> p k t", p=128)
            outs.append(p.dma('sp', lambda e, q4=q4, dst=dst: e.dma_start(out=dst, in_=x_sb[:, 4 * q4:4 * q4 + 4, :]),
                              reads=[xB[k][g] for k in range(4 * q4, 4 * q4 + 4) for g in range(NG)], writes=[p.buf('o')],
                              sembuf=Bout))
        p.emit(final_waits=outs)
    return nc


_progs = {}


def get_prog(name):
    if name not in _progs:
        builders = {'ada': build_ada, 'pf0': lambda: build_pf(False), 'pf0h': lambda: build_pf(False, mixh=True),
                    'pf1': lambda: build_pf(True), 'ma': lambda: build_ma(), 'mc': lambda: build_mc(),
                    'mg1': lambda: build_mg1(), 'mg2': lambda: build_mg2()}
        _progs[name] = builders[name]()
    return _progs[name]


def run(name, in_maps):
    res = run_bass_kernel_spmd(get_prog(name), in_maps, core_ids=list(range(8)))
    return res.results


def fm(v):
    v = np.asarray(v)
    lead = v.shape[:-1]
    a = v.reshape(lead + (KC, 128))
    a = np.moveaxis(a, -1, 0)
    return np.ascontiguousarray(a)


def run_ada(c, c_ctx, w_ada, b_ada):
    cc = np.concatenate([c, c_ctx[None, :]], axis=0)
    cT = np.ascontiguousarray(cc.T.reshape(KC, 128, 5).transpose(1, 0, 2)).reshape(128, KC * 5)
    in_maps = []
    for i in range(8):
        wsl = np.ascontiguousarray(w_ada[:, :, i * 2304:(i + 1) * 2304])
        bsl = b_ada[:, i * 2304:(i + 1) * 2304].reshape(DEPTH, 18, 128).transpose(2, 0, 1).reshape(128, 72)
        in_maps.append({"cT": cT, "w": wsl, "b": np.ascontiguousarray(bsl)})
    res = run('ada', in_maps)
    ada = np.empty((DEPTH, 9 * D, 5), np.float32)
    for i in range(8):
        o = res[i]["out"].reshape(128, DEPTH, 18, 5)
        ada[:, i * 2304:(i + 1) * 2304, :] = o.transpose(1, 2, 0, 3).reshape(DEPTH, 2304, 5)
    return ada


MT = 2304
MG256 = [(g * 256, 256, 0) for g in range(8)] + [(2048, 256, 1)]
TG512 = [(0, 512, 0), (512, 512, 0), (1024, 512, 0), (1536, 512, 0), (2048, 256, 1)]


def m_common(p, nc):
    c = TokCtx()
    c.pp = PsPool(p, 8)
    c.ones = p.sb("ones", [128, 128], F32)
    c.eps = p.sb("eps", [128, 1], F32)
    c.Bones = p.buf('ones')
    p.op('dve', lambda e: e.memset(c.ones[:], 1.0), writes=[c.Bones])
    p.op('dve', lambda e: e.memset(c.eps[:], EPS), writes=[c.Bones])
    for nm in ('acc1', 'acc2', 'mean', 'msq'):
        setattr(c, nm, p.sb(nm, [128, 256], F32))
        setattr(c, 'B' + nm, p.buf(nm))
    c.sq = [p.sb("sq%d" % i, [128, 256], F32) for i in range(2)]
    c.Bsq = [p.buf('sq') for _ in range(2)]
    c.t1 = [p.sb("t1_%d" % i, [128, 256], F32) for i in range(2)]
    c.Bt1 = [p.buf('t1') for _ in range(2)]
    c.rs = [p.sb("rs%d" % i, [128, 256], F32) for i in range(2)]
    c.nm = [p.sb("nm%d" % i, [128, 256], F32) for i in range(2)]
    c.Brs = [p.buf('rs') for _ in range(2)]
    c.Bnm = [p.buf('nm') for _ in range(2)]
    return c


def frontend(p, c, xT, adam, h_sb, hB, nxs=2):
    am = p.sb("am_sb", [128, 2, 2, KC], F32)
    Bam = p.buf('am')
    p.dma('sp', lambda e: e.dma_start(out=am[:].rearrange("p a b k -> p (a b k)"), in_=adam), writes=[Bam])
    p.op('dve', lambda e: e.tensor_scalar_add(out=am[:, :, 1, :], in0=am[:, :, 1, :], scalar1=1.0),
         reads=[Bam], writes=[Bam])
    xs = [p.sb("xs%d" % i, [128, KC, 256], F32) for i in range(nxs)]
    Bxs = [p.buf('xs') for _ in range(nxs)]
    ti = 0
    for g, (s0, n, kind) in enumerate(MG256):
        x_t, Bx = xs[g % nxs], Bxs[g % nxs]
        src = xT[:, s0:s0 + n].rearrange("(k p) t -> p k t", p=128)
        p.dma('sp', lambda e, x_t=x_t, src=src: e.dma_start(out=x_t[:], in_=src), writes=[Bx])
        p.op('dve', lambda e, x_t=x_t: e.tensor_tensor(out=c.acc1[:], in0=x_t[:, 0, :], in1=x_t[:, 1, :], op=ALU.add),
             reads=[Bx], writes=[c.Bacc1])
        for k in range(2, KC):
            p.op('dve', lambda e, x_t=x_t, k=k: e.tensor_tensor(out=c.acc1[:], in0=c.acc1[:], in1=x_t[:, k, :],
                                                                  op=ALU.add),
                 reads=[Bx, c.Bacc1], writes=[c.Bacc1])
        p.op('act', lambda e, x_t=x_t: e.activation(out=c.acc2[:], in_=x_t[:, 0, :], func=AF.Square),
             reads=[Bx], writes=[c.Bacc2])
        for k in range(1, KC):
            sq, Bsq = c.sq[k % 2], c.Bsq[k % 2]
            p.op('act', lambda e, x_t=x_t, k=k, sq=sq: e.activation(out=sq[:], in_=x_t[:, k, :], func=AF.Square),
                 reads=[Bx], writes=[Bsq])
            p.op('dve', lambda e, sq=sq: e.tensor_tensor(out=c.acc2[:], in0=c.acc2[:], in1=sq[:], op=ALU.add),
                 reads=[Bsq, c.Bacc2], writes=[c.Bacc2])
        ps1, B1 = c.pp.next()
        ps2, B2 = c.pp.next()
        p.op('pe', lambda e, ps1=ps1: e.matmul(ps1[:, :n], lhsT=c.ones[:], rhs=c.acc1[:], start=True, stop=True),
             reads=[c.Bones, c.Bacc1], writes=[B1])
        p.op('pe', lambda e, ps2=ps2: e.matmul(ps2[:, :n], lhsT=c.ones[:], rhs=c.acc2[:], start=True, stop=True),
             reads=[c.Bones, c.Bacc2], writes=[B2])
        rs, nm, Brs, Bnm = c.rs[g % 2], c.nm[g % 2], c.Brs[g % 2], c.Bnm[g % 2]
        p.op('act', lambda e, ps1=ps1: e.mul(out=c.mean[:], in_=ps1[:, :n], mul=1.0 / D), reads=[B1], writes=[c.Bmean])
        p.op('dve', lambda e: e.tensor_tensor(out=c.msq[:], in0=c.mean[:], in1=c.mean[:], op=ALU.mult),
             reads=[c.Bmean], writes=[c.Bmsq])
        p.op('dve', lambda e, ps2=ps2: e.scalar_tensor_tensor(out=c.msq[:], in0=ps2[:, :n], scalar=1.0 / D,
                                                               in1=c.msq[:], op0=ALU.mult, op1=ALU.subtract),
             reads=[B2, c.Bmsq], writes=[c.Bmsq])
        p.op('act', lambda e: e.activation(out=c.msq[:], in_=c.msq[:], func=AF.Sqrt, bias=c.eps[:, 0:1], scale=1.0),
             reads=[c.Bmsq, c.Bones], writes=[c.Bmsq])
        p.op('dve', lambda e, rs=rs: e.reciprocal(out=rs[:], in_=c.msq[:]), reads=[c.Bmsq], writes=[Brs])
        p.op('dve', lambda e, rs=rs, nm=nm: e.scalar_tensor_tensor(out=nm[:], in0=c.mean[:], scalar=-1.0, in1=rs[:],
                                                                   op0=ALU.mult, op1=ALU.mult),
             reads=[c.Bmean, Brs], writes=[Bnm])
        for k in range(KC):
            t1, Bt1 = c.t1[ti % 2], c.Bt1[ti % 2]
            ti += 1
            p.op('dve', lambda e, x_t=x_t, k=k, t1=t1, rs=rs: e.tensor_tensor(out=t1[:], in0=x_t[:, k, :], in1=rs[:],
                                                                               op=ALU.mult),
                 reads=[Bx, Brs], writes=[Bt1])
            p.op('dve', lambda e, t1=t1, nm=nm: e.tensor_tensor(out=t1[:], in0=t1[:], in1=nm[:], op=ALU.add),
                 reads=[Bt1, Bnm], writes=[Bt1])
            p.op('act', lambda e, k=k, t1=t1, kind=kind, s0=s0: e.activation(
                out=h_sb[:, k, s0:s0 + 256], in_=t1[:], func=AF.Identity, scale=am[:, kind, 1, k:k + 1],
                bias=am[:, kind, 0, k:k + 1]),
                reads=[Bt1, Bam], writes=[hB[k][g]])


def h_bufs_for(hB, k, s0, n):
    return [hB[k][g] for g in range(s0 // 256, (s0 + n + 255) // 256)]


def proj_fm(p, c, w_sb, Bw, h_sb, hB, s0, n):
    pt, pB = c.pp.next()
    for kc in range(KC):
        p.op('pe', lambda e, pt=pt, kc=kc: e.matmul(pt[:, :n], lhsT=w_sb(kc), rhs=h_sb[:, kc, s0:s0 + n],
                                                    start=(kc == 0), stop=(kc == KC - 1)),
             reads=[Bw] + h_bufs_for(hB, kc, s0, n), writes=[pB])
    return pt, pB


def build_mc():
    nc = bass.Bass("TRN2", target_bir_lowering=False)
    hT = nc.dram_tensor("hT", [D, MT], BF16, kind="ExternalInput").ap()
    wcv = nc.dram_tensor("wcv", [D, 768], F32, kind="ExternalInput").ap()
    cvw = nc.dram_tensor("cvw", [128, 6], F32, kind="ExternalInput").ap()
    ocv = nc.dram_tensor("ocv", [256, MT], BF16, kind="ExternalOutput").ap()
    with ExitStack() as st:
        p = Prog(nc, st)
        c = TokCtx()
        c.pp = PsPool(p, 8)
        h_sb = p.sb("h_sb", [128, KC, MT], BF16)
        hB = [[p.buf('h') for _ in MG256] for _ in range(KC)]
        w_sb = p.sb("w_sb", [128, KC, 768], BF16)
        Bw = p.buf('w')
        wo = []
        for j in range(6):
            src = wcv[:, j * 128:(j + 1) * 128].rearrange("(k p) n -> p k n", p=128)
            wo.append(p.dma('pool', lambda e, j=j, src=src: e.dma_start(out=w_sb[:, :, j * 128:(j + 1) * 128], in_=src),
                            writes=[Bw]))
        Prog.group(wo)
        cw = p.sb("cw", [128, 2, 3], F32)
        Bcw = p.buf('cw')
        p.dma('sp', lambda e: e.dma_start(out=cw[:].rearrange("p a b -> p (a b)"), in_=cvw), writes=[Bcw])
        load_h(p, hT, h_sb, hB)
        CUW = 2308
        cu = p.sb("cu", [128, 2, CUW], F32)
        Bcu = [[p.buf('cu') for _ in TG512] for _ in range(2)]
        Bpad = p.buf('pad')
        for gi in range(2):
            for col in (0, 2049, 2307):
                p.op('dve', lambda e, gi=gi, col=col: e.memset(cu[:, gi, col:col + (2 if col == 2049 else 1)], 0.0),
                     writes=[Bpad])
        o_sb = p.sb("o_sb", [128, 2, MT], BF16)
        Bo = p.buf('o')
        tmp = [p.sb("tmp%d" % i, [128, 512], F32) for i in range(2)]
        Btmp = [p.buf('tmp') for _ in range(2)]
        col0 = lambda kind: 1 if kind == 0 else 2051 - 2048
        it = 0
        for gi in range(2):
            for tg, (s0, n, kind) in enumerate(TG512):
                pc, pcB = proj_fm(p, c, lambda kc, gi=gi: w_sb[:, kc, (2 + gi) * 128:(3 + gi) * 128], Bw, h_sb, hB, s0, n)
                pu, puB = proj_fm(p, c, lambda kc, gi=gi: w_sb[:, kc, (4 + gi) * 128:(5 + gi) * 128], Bw, h_sb, hB, s0, n)
                t, Bt = tmp[it % 2], Btmp[it % 2]
                it += 1
                p.op('act', lambda e, t=t, pc=pc, n=n: e.copy(out=t[:, :n], in_=pc[:, :n]), reads=[pcB], writes=[Bt])
                o0 = s0 + col0(kind)
                p.op('dve', lambda e, t=t, pu=pu, n=n, gi=gi, o0=o0: e.tensor_tensor(
                    out=cu[:, gi, o0:o0 + n], in0=t[:, :n], in1=pu[:, :n], op=ALU.mult),
                    reads=[Bt, puB], writes=[Bcu[gi][tg]])
        for gi in range(2):
            for tg, (s0, n, kind) in enumerate(TG512):
                pb, pbB = proj_fm(p, c, lambda kc, gi=gi: w_sb[:, kc, gi * 128:(gi + 1) * 128], Bw, h_sb, hB, s0, n)
                t, Bt = tmp[it % 2], Btmp[it % 2]
                it += 1
                o0 = s0 + col0(kind)
                nb = [Bcu[gi][x] for x in range(max(0, tg - 1), min(len(TG512), tg + 2))] + [Bpad, Bcw]
                p.op('dve', lambda e, t=t, n=n, gi=gi, o0=o0: e.tensor_scalar_mul(
                    out=t[:, :n], in0=cu[:, gi, o0 - 1:o0 - 1 + n], scalar1=cw[:, gi, 0:1]), reads=nb, writes=[Bt])
                for tap in (1, 2):
                    p.op('dve', lambda e, t=t, n=n, gi=gi, o0=o0, tap=tap: e.scalar_tensor_tensor(
                        out=t[:, :n], in0=cu[:, gi, o0 - 1 + tap:o0 - 1 + tap + n], scalar=cw[:, gi, tap:tap + 1],
                        in1=t[:, :n], op0=ALU.mult, op1=ALU.add), reads=nb + [Bt], writes=[Bt])
                p.op('dve', lambda e, t=t, pb=pb, n=n, gi=gi, s0=s0: e.tensor_tensor(
                    out=o_sb[:, gi, s0:s0 + n], in0=t[:, :n], in1=pb[:, :n], op=ALU.mult),
                    reads=[Bt, pbB], writes=[Bo])
        outs = []
        for gi in range(2):
            outs.append(p.dma('sp', lambda e, gi=gi: e.dma_start(out=ocv[gi * 128:(gi + 1) * 128, :], in_=o_sb[:, gi, :]),
                              reads=[Bo], writes=[p.buf('oo')], sembuf=Bo))
        p.emit(final_waits=outs)
    return nc


def na_tables(rpb3):
    kl = np.arange(128) // 64
    kc = np.arange(128) % 64
    m = np.arange(16)
    qc = np.arange(64)
    ri = m[None, :] + kl[:, None] - 1
    ci = kc[:, None] - qc[None, :] + 15
    row_ok = (ri >= 0) & (ri <= 14)
    cs = np.clip(qc - 8, 0, 48)
    col_ok = (kc[:, None] >= cs[None, :]) & (kc[:, None] < cs[None, :] + 16)
    bt = rpb3[:, np.clip(ri, 0, 14)[:, :, None], np.clip(ci, 0, 30)[:, None, :]]
    bt = np.ascontiguousarray(bt.transpose(1, 0, 2, 3)).reshape(128, 3 * 1024).astype(np.float32)
    mask = (row_ok[:, :, None] & col_ok[:, None, :]).astype(np.float32).reshape(128, 1024)
    return bt, np.ascontiguousarray(mask)


def build_ma():
    nc = bass.Bass("TRN2", target_bir_lowering=False)
    hT = nc.dram_tensor("hT", [D, MT], BF16, kind="ExternalInput").ap()
    wna = nc.dram_tensor("wna", [D, 1152], F32, kind="ExternalInput").ap()
    btab = nc.dram_tensor("btab", [128, 3 * 1024], F32, kind="ExternalInput").ap()
    mask = nc.dram_tensor("mask", [128, 1024], F32, kind="ExternalInput").ap()
    ona = nc.dram_tensor("ona", [384, MT], BF16, kind="ExternalOutput").ap()
    SCALE = 128 ** -0.5
    with ExitStack() as st:
        p = Prog(nc, st)
        c = TokCtx()
        c.pp = PsPool(p, 8)
        pp = c.pp
        h_sb = p.sb("h_sb", [128, KC, MT], BF16)
        hB = [[p.buf('h') for _ in MG256] for _ in range(KC)]
        G_sb = p.sb("G_sb", [128, 3, 16, 64], F32)
        mk_sb = p.sb("mk_sb", [128, 16, 64], F32)
        BG = p.buf('G')
        Bmk = p.buf('mk')
        p.dma('sp', lambda e: e.dma_start(out=G_sb[:].rearrange("p a m q -> p (a m q)"), in_=btab), writes=[BG])
        p.dma('sp', lambda e: e.dma_start(out=mk_sb[:].rearrange("p m q -> p (m q)"), in_=mask), writes=[Bmk])
        for i in range(3):
            p.op('act', lambda e, i=i: e.activation(out=G_sb[:, i], in_=G_sb[:, i], func=AF.Exp), reads=[BG], writes=[BG])
            p.op('dve', lambda e, i=i: e.tensor_tensor(out=G_sb[:, i], in0=G_sb[:, i], in1=mk_sb[:], op=ALU.mult),
                 reads=[BG, Bmk], writes=[BG])
        onesb = p.sb("onesb", [128, 128], BF16)
        Bob = p.buf('onesb')
        p.op('dve', lambda e: e.memset(onesb[:], 1.0), writes=[Bob])
        wv_sb = p.sb("wv_sb", [128, KC, 384], BF16)
        Bwv = p.buf('wv')
        wo = []
        for j in range(3):
            src = wna[:, (6 + j) * 128:(7 + j) * 128].rearrange("(k p) n -> p k n", p=128)
            wo.append(p.dma('pool', lambda e, j=j, src=src: e.dma_start(out=wv_sb[:, :, j * 128:(j + 1) * 128], in_=src),
                            writes=[Bwv]))
        Prog.group(wo)
        NW = 2
        wq_sb = [p.sb("wq%d" % i, [128, KC, 128], BF16) for i in range(NW)]
        Bwq = [p.buf('wq') for _ in range(NW)]

        def load_wq(j):
            src = wna[:, j * 128:(j + 1) * 128].rearrange("(k p) n -> p k n", p=128)
            p.dma('pool', lambda e, j=j, src=src: e.dma_start(out=wq_sb[j % NW][:], in_=src), writes=[Bwq[j % NW]])

        load_wq(0)
        load_h(p, hT, h_sb, hB)
        qT = p.sb("qT", [128, 3, MT], BF16)
        kT = p.sb("kT", [128, 3, MT], BF16)
        BqT = [[p.buf('qT') for _ in TG512] for _ in range(3)]
        BkT = [[p.buf('kT') for _ in TG512] for _ in range(3)]
        for j in range(6):
            if j + 1 < 6:
                load_wq(j + 1)
            for tg, (s0, n, kind) in enumerate(TG512):
                pt, pB = proj_fm(p, c, lambda kc, j=j: wq_sb[j % NW][:, kc, :], Bwq[j % NW], h_sb, hB, s0, n)
                if j < 3:
                    p.op('act', lambda e, pt=pt, j=j, s0=s0, n=n: e.mul(out=qT[:, j, s0:s0 + n], in_=pt[:, :n], mul=SCALE),
                         reads=[pB], writes=[BqT[j][tg]])
                else:
                    p.op('dve', lambda e, pt=pt, j=j, s0=s0, n=n: e.tensor_copy(out=kT[:, j - 3, s0:s0 + n], in_=pt[:, :n]),
                         reads=[pB], writes=[BkT[j - 3][tg]])
        vstarts = [128 * j for j in range(16)] + [64 + 128 * j for j in range(15)] + [2048, 2176]
        v_sb = p.sb("v_sb", [128, 33, 384], BF16)
        Bv = [p.buf('v') for _ in vstarts]
        for t_i, ts in enumerate(vstarts):
            pt, pB = pp.next()
            for kc in range(KC):
                p.op('pe', lambda e, pt=pt, kc=kc, ts=ts: e.matmul(pt[:, :384], lhsT=h_sb[:, kc, ts:ts + 128],
                                                                   rhs=wv_sb[:, kc, :], start=(kc == 0), stop=(kc == KC - 1)),
                     reads=[Bwv] + h_bufs_for(hB, kc, ts, 128), writes=[pB])
            eng = 'act' if t_i % 2 == 0 else 'dve'
            if eng == 'act':
                p.op('act', lambda e, pt=pt, t_i=t_i: e.copy(out=v_sb[:, t_i, :], in_=pt[:, :384]), reads=[pB], writes=[Bv[t_i]])
            else:
                p.op('dve', lambda e, pt=pt, t_i=t_i: e.tensor_copy(out=v_sb[:, t_i, :], in_=pt[:, :384]), reads=[pB],
                     writes=[Bv[t_i]])
        o_sbs = [p.sb("o_sb%d" % i, [128, MT], BF16) for i in range(2)]
        Bos = [p.buf('o') for _ in range(2)]
        Bout = p.buf('out')
        outs = []
        ex = [p.sb("ex%d" % i, [128, 256], F32) for i in range(2)]
        Bex = [p.buf('ex') for _ in range(2)]
        pr = [p.sb("pr%d" % i, [128, 512], BF16) for i in range(2)]
        Bpr = [p.buf('pr') for _ in range(2)]
        rd = [p.sb("rd%d" % i, [128, 256], F32) for i in range(2)]
        Brd = [p.buf('rd') for _ in range(2)]
        it = 0
        tgof = lambda tok: min(tok // 512, 4)
        for i in range(3):
            o_sb, Bo = o_sbs[i % 2], Bos[i % 2]
            for qr in range(32):
                start = min(max(qr - 4, 0), 24)
                q0 = qr * 64
                ps, psB = pp.next()
                ktoks = [64 * start + 128 * ti for ti in range(4)] + [2048, 2176]
                for ti, kt in enumerate(ktoks):
                    p.op('pe', lambda e, ps=ps, ti=ti, kt=kt, i=i, q0=q0: e.matmul(
                        ps[:, ti * 64:(ti + 1) * 64], lhsT=kT[:, i, kt:kt + 128], rhs=qT[:, i, q0:q0 + 64],
                        start=True, stop=True),
                        reads=[BkT[i][tgof(kt)], BkT[i][tgof(kt + 127)], BqT[i][tgof(q0)]], writes=[psB])
                e_t, Be = ex[it % 2], Bex[it % 2]
                p_t, Bp = pr[it % 2], Bpr[it % 2]
                r_t, Br = rd[it % 2], Brd[it % 2]
                it += 1
                p.op('act', lambda e, ps=ps, e_t=e_t: e.activation(out=e_t[:, :256], in_=ps[:, :256], func=AF.Exp),
                     reads=[psB], writes=[Be])
                p.op('act', lambda e, ps=ps, p_t=p_t: e.activation(out=p_t[:, 256:384], in_=ps[:, 256:384], func=AF.Exp),
                     reads=[psB], writes=[Bp])
                m0 = start - qr + 8
                p.op('dve', lambda e, e_t=e_t, p_t=p_t, i=i, m0=m0: e.tensor_tensor(
                    out=p_t[:, 0:256].rearrange("p (t q) -> p t q", q=64),
                    in0=e_t[:, :256].rearrange("p (t q) -> p t q", q=64),
                    in1=G_sb[:, i, m0:m0 + 7:2, :], op=ALU.mult), reads=[Be, BG], writes=[Bp])
                if start % 2 == 0:
                    vt = [start // 2 + ti for ti in range(4)]
                else:
                    vt = [16 + (start - 1) // 2 + ti for ti in range(4)]
                vt += [31, 32]
                po, poB = pp.next()
                pd, pdB = pp.next()
                for ti, v_i in enumerate(vt):
                    p.op('pe', lambda e, po=po, ti=ti, v_i=v_i, i=i, p_t=p_t: e.matmul(
                        po[:, :64], lhsT=v_sb[:, v_i, i * 128:(i + 1) * 128], rhs=p_t[:, ti * 64:(ti + 1) * 64],
                        start=(ti == 0), stop=(ti == 5)), reads=[Bv[v_i], Bp], writes=[poB])
                for ti in range(6):
                    p.op('pe', lambda e, pd=pd, ti=ti, p_t=p_t: e.matmul(
                        pd[:, :64], lhsT=onesb[:], rhs=p_t[:, ti * 64:(ti + 1) * 64],
                        start=(ti == 0), stop=(ti == 5)), reads=[Bob, Bp], writes=[pdB])
                p.op('dve', lambda e, pd=pd, r_t=r_t: e.reciprocal(out=r_t[:, :64], in_=pd[:, :64]), reads=[pdB], writes=[Br])
                p.op('dve', lambda e, po=po, r_t=r_t, i=i, q0=q0, o_sb=o_sb: e.tensor_tensor(
                    out=o_sb[:, q0:q0 + 64], in0=po[:, :64], in1=r_t[:, :64], op=ALU.mult),
                    reads=[poB, Br], writes=[Bo])
            ps, psB = pp.next()
            for ci in range(2):
                p.op('pe', lambda e, ps=ps, ci=ci, i=i: e.matmul(
                    ps[:, ci * 256:(ci + 1) * 256], lhsT=kT[:, i, 2048 + 128 * ci:2048 + 128 * (ci + 1)],
                    rhs=qT[:, i, 2048:2304], start=True, stop=True), reads=[BkT[i][4], BqT[i][4]], writes=[psB])
            p_t, Bp = pr[it % 2], Bpr[it % 2]
            r_t, Br = rd[it % 2], Brd[it % 2]
            it += 1
            p.op('act', lambda e, ps=ps, p_t=p_t: e.activation(out=p_t[:, :512], in_=ps[:, :512], func=AF.Exp),
                 reads=[psB], writes=[Bp])
            po, poB = pp.next()
            pd, pdB = pp.next()
            for ci in range(2):
                p.op('pe', lambda e, po=po, ci=ci, i=i, p_t=p_t: e.matmul(
                    po[:, :256], lhsT=v_sb[:, 31 + ci, i * 128:(i + 1) * 128], rhs=p_t[:, ci * 256:(ci + 1) * 256],
                    start=(ci == 0), stop=(ci == 1)), reads=[Bv[31 + ci], Bp], writes=[poB])
            for ci in range(2):
                p.op('pe', lambda e, pd=pd, ci=ci, p_t=p_t: e.matmul(
                    pd[:, :256], lhsT=onesb[:], rhs=p_t[:, ci * 256:(ci + 1) * 256],
                    start=(ci == 0), stop=(ci == 1)), reads=[Bob, Bp], writes=[pdB])
            p.op('dve', lambda e, pd=pd, r_t=r_t: e.reciprocal(out=r_t[:, :256], in_=pd[:, :256]), reads=[pdB], writes=[Br])
            p.op('dve', lambda e, po=po, r_t=r_t, i=i, o_sb=o_sb: e.tensor_tensor(
                out=o_sb[:, 2048:2304], in0=po[:, :256], in1=r_t[:, :256], op=ALU.mult),
                reads=[poB, Br], writes=[Bo])
            outs.append(p.dma('sp', lambda e, i=i, o_sb=o_sb: e.dma_start(out=ona[i * 128:(i + 1) * 128, :], in_=o_sb[:]),
                              reads=[Bo], writes=[p.buf('oo')], sembuf=Bout))
        p.emit(final_waits=outs)
    return nc


def load_h(p, hT, h_sb, hB):
    for q4 in range(4):
        src = hT[q4 * 512:(q4 + 1) * 512, :].rearrange("(k p) t -> p k t", p=128)
        p.dma('sp', lambda e, q4=q4, src=src: e.dma_start(out=h_sb[:, 4 * q4:4 * q4 + 4, :], in_=src),
              writes=[hB[k][g] for k in range(4 * q4, 4 * q4 + 4) for g in range(len(MG256))], sembuf=hB[4 * q4][0])


def gdn_consts():
    t = np.arange(2048)
    row = (t // 64).astype(np.float32)
    col = (t % 64).astype(np.float32)
    inv = (10000.0 ** (-np.arange(32, dtype=np.float32) / 32)).astype(np.float32)
    d = np.arange(128)
    ang = np.where((d < 64)[:, None], row[None, :] * inv[d % 32][:, None], col[None, :] * inv[d % 32][:, None]).astype(np.float32)
    cosT = np.cos(ang).astype(np.float32)
    sinT = np.sin(ang).astype(np.float32)
    perm = np.zeros((128, 128), np.float32)
    for dp in range(128):
        if dp % 64 < 32:
            perm[dp + 32, dp] = -1.0
        else:
            perm[dp - 32, dp] = 1.0
    return cosT, sinT, perm


def build_mg1():
    nc = bass.Bass("TRN2", target_bir_lowering=False)
    hT = nc.dram_tensor("hT", [D, MT], BF16, kind="ExternalInput").ap()
    wqkv = nc.dram_tensor("wqkv", [D, 1152], F32, kind="ExternalInput").ap()
    wzb = nc.dram_tensor("wzb", [D, 396], F32, kind="ExternalInput").ap()
    cwd = nc.dram_tensor("cw", [128, 45], F32, kind="ExternalInput").ap()
    gpar = nc.dram_tensor("gpar", [128, 12], F32, kind="ExternalInput").ap()
    nwd = nc.dram_tensor("nw", [128, 384], F32, kind="ExternalInput").ap()
    cosd = nc.dram_tensor("cosT", [128, 2048], F32, kind="ExternalInput").ap()
    sind = nc.dram_tensor("sinT", [128, 2048], F32, kind="ExternalInput").ap()
    permd = nc.dram_tensor("perm", [128, 128], F32, kind="ExternalInput").ap()
    qkvT = nc.dram_tensor("qkvT", [9 * 128, MT], F32, kind="ExternalOutput").ap()
    zs = nc.dram_tensor("zs", [MT, 384], F32, kind="ExternalOutput").ap()
    bg = nc.dram_tensor("bg", [MT, 12], F32, kind="ExternalOutput").ap()
    SCALE = 128 ** -0.5
    with ExitStack() as st:
        p = Prog(nc, st)
        pp = PsPool(p, 8)
        c = TokCtx()
        c.pp = pp
        h_sb = p.sb("h_sb", [128, KC, MT], BF16)
        hB = [[p.buf('h') for _ in MG256] for _ in range(KC)]
        load_h(p, hT, h_sb, hB)
        ones = p.sb("ones", [128, 128], F32)
        eps = p.sb("eps", [128, 1], F32)
        Bc = p.buf('c')
        p.op('dve', lambda e: e.memset(ones[:], 1.0), writes=[Bc])
        p.op('dve', lambda e: e.memset(eps[:], EPS), writes=[Bc])
        cw = p.sb("cw_sb", [128, 9, 5], F32)
        gp = p.sb("gp_sb", [128, 12], F32)
        nw = p.sb("nw_sb", [128, 384], F32)
        perm = p.sb("perm_sb", [128, 128], F32)
        Bpar = p.buf('par')
        po = [p.dma('sp', lambda e: e.dma_start(out=cw[:].rearrange("p a b -> p (a b)"), in_=cwd), writes=[Bpar]),
              p.dma('sp', lambda e: e.dma_start(out=gp[:], in_=gpar), writes=[Bpar]),
              p.dma('sp', lambda e: e.dma_start(out=nw[:], in_=nwd), writes=[Bpar]),
              p.dma('sp', lambda e: e.dma_start(out=perm[:], in_=permd), writes=[Bpar])]
        Prog.group(po)
        p.op('act', lambda e: e.activation(out=gp[:, 0:6], in_=gp[:, 0:6], func=AF.Exp), reads=[Bpar], writes=[Bpar])
        p.op('dve', lambda e: e.tensor_scalar_mul(out=gp[:, 0:6], in0=gp[:, 0:6], scalar1=-1.0), reads=[Bpar], writes=[Bpar])
        wzb_sb = p.sb("wzb_sb", [128, KC, 396], BF16)
        Bwz = p.buf('wz')
        p.dma('pool', lambda e: e.dma_start(out=wzb_sb[:], in_=wzb.rearrange("(k p) n -> p k n", p=128)), writes=[Bwz])
        NW = 2
        w_sb = [p.sb("w%d" % i, [128, KC, 128], BF16) for i in range(NW)]
        Bw = [p.buf('w') for _ in range(NW)]
        wcnt = [0]

        def load_w(col):
            s = wcnt[0] % NW
            wcnt[0] += 1
            src = wqkv[:, col * 128:(col + 1) * 128].rearrange("(k p) n -> p k n", p=128)
            p.dma('pool', lambda e, s=s, src=src: e.dma_start(out=w_sb[s][:], in_=src), writes=[Bw[s]])
            return s

        bg_sb = p.sb("bg_sb", [128, 18, 12], F32)
        Bbg = p.buf('bg')
        zt = [p.sb("zt%d" % i, [128, 384], F32) for i in range(2)]
        Bzt = [p.buf('zt') for _ in range(2)]
        tm6 = [p.sb("tm6_%d" % i, [128, 6], F32) for i in range(2)]
        Btm6 = [p.buf('tm6') for _ in range(2)]
        Bout = p.buf('out')
        outs = []
        tstarts = [128 * t for t in range(18)]
        for t_i, ts in enumerate(tstarts):
            pt, pB = pp.next()
            for kc in range(KC):
                p.op('pe', lambda e, pt=pt, kc=kc, ts=ts: e.matmul(pt[:, :396], lhsT=h_sb[:, kc, ts:ts + 128],
                                                                   rhs=wzb_sb[:, kc, :], start=(kc == 0), stop=(kc == KC - 1)),
                     reads=[Bwz] + h_bufs_for(hB, kc, ts, 128), writes=[pB])
            z_t, Bz = zt[t_i % 2], Bzt[t_i % 2]
            t6, Bt6 = tm6[t_i % 2], Btm6[t_i % 2]
            p.op('act', lambda e, pt=pt, z_t=z_t: e.activation(out=z_t[:], in_=pt[:, 0:384], func=AF.Silu), reads=[pB], writes=[Bz])
            p.op('dve', lambda e, z_t=z_t: e.tensor_tensor(out=z_t[:], in0=z_t[:], in1=nw[:], op=ALU.mult),
                 reads=[Bz, Bpar], writes=[Bz])
            outs.append(p.dma('sp', lambda e, z_t=z_t, ts=ts: e.dma_start(out=zs[ts:ts + 128, :], in_=z_t[:]),
                              reads=[Bz], writes=[p.buf('oo')], sembuf=Bz))
            p.op('act', lambda e, pt=pt, t_i=t_i: e.activation(out=bg_sb[:, t_i, 0:6], in_=pt[:, 384:390], func=AF.Sigmoid),
                 reads=[pB], writes=[Bbg])
            p.op('dve', lambda e, pt=pt, t6=t6: e.tensor_tensor(out=t6[:], in0=pt[:, 390:396], in1=gp[:, 6:12], op=ALU.add),
                 reads=[pB, Bpar], writes=[Bt6])
            p.op('act', lambda e, t6=t6: e.activation(out=t6[:], in_=t6[:], func=AF.Exp), reads=[Bt6], writes=[Bt6])
            p.op('act', lambda e, t6=t6: e.activation(out=t6[:], in_=t6[:], func=AF.Ln, bias=ones[:, 0:1], scale=1.0),
                 reads=[Bt6, Bc], writes=[Bt6])
            p.op('dve', lambda e, t6=t6, t_i=t_i: e.tensor_tensor(out=bg_sb[:, t_i, 6:12], in0=t6[:], in1=gp[:, 0:6], op=ALU.mult),
                 reads=[Bt6, Bpar], writes=[Bbg])
        outs.append(p.dma('sp', lambda e: e.dma_start(out=bg.rearrange("(t p) c -> p t c", p=128), in_=bg_sb[:]),
                          reads=[Bbg], writes=[p.buf('oo')], sembuf=Bout))
        PW = 2312
        pre = p.sb("pre", [128, 3, PW], F32)
        Bpre = [[p.buf('pre') for _ in TG512] for _ in range(3)]
        Bpad = p.buf('pad')
        for j in range(3):
            for (a, b) in ((0, 2), (2050, 2054), (2310, 2312)):
                p.op('dve', lambda e, j=j, a=a, b=b: e.memset(pre[:, j, a:b], 0.0), writes=[Bpad])
        stg = p.sb("stg", [128, 3, MT], F32)
        Bstg = [[p.buf('stg') for _ in TG512] for _ in range(3)]
        acc = [p.sb("acc%d" % i, [128, 512], F32) for i in range(2)]
        Bacc = [p.buf('acc') for _ in range(2)]
        post = [p.sb("post%d" % i, [128, 512], F32) for i in range(2)]
        Bpost = [p.buf('post') for _ in range(2)]
        sq = [p.sb("sq%d" % i, [128, 512], F32) for i in range(2)]
        Bsq = [p.buf('sq') for _ in range(2)]
        rn = [p.sb("rn%d" % i, [128, 512], F32) for i in range(2)]
        Brn = [p.buf('rn') for _ in range(2)]
        xn = [p.sb("xn%d" % i, [128, 512], F32) for i in range(2)]
        Bxn = [p.buf('xn') for _ in range(2)]
        cs = [p.sb("cs%d" % i, [128, 2, 512], F32) for i in range(2)]
        Bcs = [p.buf('cs') for _ in range(2)]
        off = lambda kind, s0: (2 + s0) if kind == 0 else (2054 + s0 - 2048)
        it = 0
        rc = 0
        for i in range(3):
            for j in range(3):
                s = load_w(3 * i + j)
                for tg, (s0, n, kind) in enumerate(TG512):
                    pt, pB = proj_fm(p, c, lambda kc, s=s: w_sb[s][:, kc, :], Bw[s], h_sb, hB, s0, n)
                    o0 = off(kind, s0)
                    p.op('act', lambda e, pt=pt, j=j, o0=o0, n=n: e.copy(out=pre[:, j, o0:o0 + n], in_=pt[:, :n]),
                         reads=[pB], writes=[Bpre[j][tg]])
            for j in range(3):
                for tg, (s0, n, kind) in enumerate(TG512):
                    o0 = off(kind, s0)
                    a_t, Ba = acc[it % 2], Bacc[it % 2]
                    p_t, Bp = post[it % 2], Bpost[it % 2]
                    s_t, Bs = sq[it % 2], Bsq[it % 2]
                    r_t, Br = rn[it % 2], Brn[it % 2]
                    x_t, Bx = xn[it % 2], Bxn[it % 2]
                    it += 1
                    nb = [Bpre[j][x] for x in range(max(0, tg - 1), min(len(TG512), tg + 2))] + [Bpad, Bpar]
                    p.op('dve', lambda e, a_t=a_t, i=i, j=j, o0=o0, n=n: e.tensor_scalar_mul(
                        out=a_t[:, :n], in0=pre[:, j, o0 - 2:o0 - 2 + n], scalar1=cw[:, 3 * i + j, 0:1]), reads=nb, writes=[Ba])
                    for tap in range(1, 5):
                        p.op('dve', lambda e, a_t=a_t, i=i, j=j, o0=o0, n=n, tap=tap: e.scalar_tensor_tensor(
                            out=a_t[:, :n], in0=pre[:, j, o0 - 2 + tap:o0 - 2 + tap + n], scalar=cw[:, 3 * i + j, tap:tap + 1],
                            in1=a_t[:, :n], op0=ALU.mult, op1=ALU.add), reads=nb + [Ba], writes=[Ba])
                    if j == 2:
                        p.op('act', lambda e, a_t=a_t, s0=s0, n=n: e.activation(out=stg[:, 2, s0:s0 + n], in_=a_t[:, :n], func=AF.Silu),
                             reads=[Ba], writes=[Bstg[2][tg]])
                        continue
                    p.op('act', lambda e, a_t=a_t, p_t=p_t, n=n: e.activation(out=p_t[:, :n], in_=a_t[:, :n], func=AF.Silu),
                         reads=[Ba], writes=[Bp])
                    p.op('act', lambda e, p_t=p_t, s_t=s_t, n=n: e.activation(out=s_t[:, :n], in_=p_t[:, :n], func=AF.Square),
                         reads=[Bp], writes=[Bs])
                    ps, psB = pp.next()
                    p.op('pe', lambda e, ps=ps, s_t=s_t, n=n: e.matmul(ps[:, :n], lhsT=ones[:], rhs=s_t[:, :n], start=True, stop=True),
                         reads=[Bc, Bs], writes=[psB])
                    p.op('act', lambda e, ps=ps, r_t=r_t, n=n: e.activation(out=r_t[:, :n], in_=ps[:, :n], func=AF.Sqrt,
                                                                            bias=eps[:, 0:1], scale=1.0),
                         reads=[psB, Bc], writes=[Br])
                    p.op('dve', lambda e, r_t=r_t, n=n: e.reciprocal(out=r_t[:, :n], in_=r_t[:, :n]), reads=[Br], writes=[Br])
                    dst = stg[:, j, s0:s0 + n] if kind == 1 else x_t[:, :n]
                    dB = Bstg[j][tg] if kind == 1 else Bx
                    p.op('dve', lambda e, p_t=p_t, r_t=r_t, n=n, j=j, dst=dst: e.scalar_tensor_tensor(
                        out=dst, in0=p_t[:, :n], scalar=(SCALE if j == 0 else 1.0), in1=r_t[:, :n], op0=ALU.mult, op1=ALU.mult),
                        reads=[Bp, Br], writes=[dB])
                    if kind == 1:
                        continue
                    c_t, Bct = cs[rc % 2], Bcs[rc % 2]
                    rc += 1
                    Prog.group([
                        p.dma('sp', lambda e, c_t=c_t, s0=s0, n=n: e.dma_start(out=c_t[:, 0, :n], in_=cosd[:, s0:s0 + n]), writes=[Bct]),
                        p.dma('sp', lambda e, c_t=c_t, s0=s0, n=n: e.dma_start(out=c_t[:, 1, :n], in_=sind[:, s0:s0 + n]), writes=[Bct])])
                    ps2, ps2B = pp.next()
                    p.op('pe', lambda e, ps2=ps2, x_t=x_t, n=n: e.matmul(ps2[:, :n], lhsT=perm[:], rhs=x_t[:, :n], start=True, stop=True),
                         reads=[Bpar, Bx], writes=[ps2B])
                    p.op('dve', lambda e, ps2=ps2, c_t=c_t, s_t=s_t, n=n: e.tensor_tensor(out=s_t[:, :n], in0=ps2[:, :n], in1=c_t[:, 1, :n],
                                                                                          op=ALU.mult),
                         reads=[ps2B, Bct, Bs], writes=[Bs])
                    p.op('dve', lambda e, x_t=x_t, c_t=c_t, n=n: e.tensor_tensor(out=x_t[:, :n], in0=x_t[:, :n], in1=c_t[:, 0, :n], op=ALU.mult),
                         reads=[Bx, Bct], writes=[Bx])
                    p.op('dve', lambda e, x_t=x_t, s_t=s_t, j=j, s0=s0, n=n: e.tensor_tensor(out=stg[:, j, s0:s0 + n], in0=x_t[:, :n],
                                                                                             in1=s_t[:, :n], op=ALU.add),
                         reads=[Bx, Bs], writes=[Bstg[j][tg]])
            for j in range(3):
                r0 = (3 * i + j) * 128
                outs.append(p.dma('sp', lambda e, j=j, r0=r0: e.dma_start(out=qkvT[r0:r0 + 128, :], in_=stg[:, j, :]),
                                  reads=Bstg[j], writes=[p.buf('oo')], sembuf=Bstg[j][0]))
        p.emit(final_waits=outs)
    return nc


def gdn_masks():
    s = np.arange(128)[:, None]
    t = np.arange(128)[None, :]
    same = (s // 64) == (t // 64)
    triF = (same & (s <= t)).astype(np.float32)
    triB = (same & (s >= t)).astype(np.float32)
    blk = same.astype(np.float32)
    half0 = np.broadcast_to((s < 64), (128, 128)).astype(np.float32)
    half1 = np.broadcast_to((s >= 64), (128, 128)).astype(np.float32)
    ident = np.eye(128, dtype=np.float32)
    negF = (triF - 1.0) * 30000.0
    negB = (triB - 1.0) * 30000.0
    return np.ascontiguousarray(np.concatenate([triF, triB, blk, half0, half1, negF, negB, triF - ident, triB - ident, ident],
                                               axis=1).astype(np.float32))


def build_mg2():
    nc = bass.Bass("TRN2", target_bir_lowering=False)
    qkvT = nc.dram_tensor("qkvT", [9 * 128, MT], F32, kind="ExternalInput").ap()
    ktm = nc.dram_tensor("ktm", [MT, 384], F32, kind="ExternalInput").ap()
    vtm = nc.dram_tensor("vtm", [MT, 384], F32, kind="ExternalInput").ap()
    zsd = nc.dram_tensor("zs", [MT, 384], F32, kind="ExternalInput").ap()
    bgd = nc.dram_tensor("bg", [MT, 12], F32, kind="ExternalInput").ap()
    cstd = nc.dram_tensor("cst", [128, 1280], F32, kind="ExternalInput").ap()
    ogT = nc.dram_tensor("ogT", [384, MT], BF16, kind="ExternalOutput").ap()
    oraw = nc.dram_tensor("oraw", [MT, 384], F32, kind="ExternalOutput").ap()
    NT = 18
    with ExitStack() as st:
        p = Prog(nc, st)
        pp = PsPool(p, 8)
        cst = p.sb("cst_sb", [128, 10, 128], F32)
        Bc = p.buf('cst')
        p.dma('sp', lambda e: e.dma_start(out=cst[:].rearrange("p a b -> p (a b)"), in_=cstd), writes=[Bc])
        TRI = [cst[:, 0, :], cst[:, 1, :]]
        BLK = cst[:, 2, :]
        HALF = [cst[:, 3, :], cst[:, 4, :]]
        NEGM = [cst[:, 5, :], cst[:, 6, :]]
        STRICT = [cst[:, 7, :], cst[:, 8, :]]
        IDENT = cst[:, 9, :]
        ones = p.sb("ones", [128, 128], F32)
        negones = p.sb("negones", [128, 128], F32)
        eps = p.sb("eps", [128, 1], F32)
        identb = p.sb("identb", [128, 128], BF16)
        ident4 = p.sb("ident4", [128, 4, 128], F32)
        negm4 = p.sb("negm4", [128, 4, 128], F32)
        strict4 = p.sb("strict4", [128, 4, 128], F32)
        Bk = p.buf('k')
        p.op('dve', lambda e: e.memset(ones[:], 1.0), writes=[Bk])
        p.op('dve', lambda e: e.memset(negones[:], -1.0), writes=[Bk])
        p.op('dve', lambda e: e.memset(eps[:], EPS), writes=[Bk])
        p.op('dve', lambda e: e.tensor_copy(out=identb[:], in_=IDENT), reads=[Bc], writes=[Bk])
        for q in range(4):
            p.op('dve', lambda e, q=q: e.tensor_copy(out=ident4[:, q, :], in_=IDENT), reads=[Bc], writes=[Bk])
            p.op('dve', lambda e, q=q: e.tensor_copy(out=negm4[:, q, :], in_=NEGM[q % 2]), reads=[Bc], writes=[Bk])
            p.op('dve', lambda e, q=q: e.tensor_copy(out=strict4[:, q, :], in_=STRICT[q % 2]), reads=[Bc], writes=[Bk])
        qT = p.sb("qT", [128, 3, MT], BF16)
        kT = p.sb("kT", [128, 3, MT], BF16)
        BqT = [p.buf('qT') for _ in range(3)]
        BkT = [p.buf('kT') for _ in range(3)]
        for hd in range(3):
            p.dma('pool', lambda e, hd=hd: e.dma_start(out=qT[:, hd, :], in_=qkvT[(3 * hd) * 128:(3 * hd + 1) * 128, :]),
                  writes=[BqT[hd]])
            p.dma('pool', lambda e, hd=hd: e.dma_start(out=kT[:, hd, :], in_=qkvT[(3 * hd + 1) * 128:(3 * hd + 2) * 128, :]),
                  writes=[BkT[hd]])
        bg = p.sb("bg_sb", [128, NT, 12], F32)
        Bbg = p.buf('bg')
        p.dma('sp', lambda e: e.dma_start(out=bg[:], in_=bgd.rearrange("(t p) c -> p t c", p=128)), writes=[Bbg])
        gc = p.sb("gc", [128, NT, 6], F32)
        eg = p.sb("eg", [128, NT, 6], F32)
        ekt = p.sb("ekt", [128, NT, 6], F32)
        bgc = p.sb("bgc", [128, NT, 6], F32)
        gtot = p.sb("gtot", [128, NT, 2, 6], F32)
        Bsc = p.buf('sc')
        pA, pAB = pp.next()
        pBk, pBB = pp.next()
        pC, pCB = pp.next()
        for t in range(NT):
            for d in range(2):
                p.op('pe', lambda e, t=t, d=d: e.matmul(pA[:, t * 6 + 3 * d:t * 6 + 3 * d + 3], lhsT=TRI[d],
                                                        rhs=bg[:, t, 6 + 3 * d:9 + 3 * d], start=True, stop=True),
                     reads=[Bc, Bbg], writes=[pAB])
            p.op('pe', lambda e, t=t: e.matmul(pBk[:, t * 6:t * 6 + 6], lhsT=BLK, rhs=bg[:, t, 6:12], start=True, stop=True),
                 reads=[Bc, Bbg], writes=[pBB])
            for hf in range(2):
                p.op('pe', lambda e, t=t, hf=hf: e.matmul(pC[:, (t * 2 + hf) * 6:(t * 2 + hf) * 6 + 6], lhsT=HALF[hf],
                                                          rhs=bg[:, t, 6:12], start=True, stop=True),
                     reads=[Bc, Bbg], writes=[pCB])
        fl = lambda a: a[:].rearrange("p t c -> p (t c)")
        p.op('dve', lambda e: e.tensor_copy(out=fl(gc), in_=pA[:, :NT * 6]), reads=[pAB], writes=[Bsc])
        p.op('act', lambda e: e.activation(out=fl(eg), in_=pA[:, :NT * 6], func=AF.Exp), reads=[pAB], writes=[Bsc])
        p.op('dve', lambda e: e.tensor_tensor(out=fl(ekt), in0=pBk[:, :NT * 6], in1=fl(gc), op=ALU.subtract),
             reads=[pBB, Bsc], writes=[Bsc])
        p.op('act', lambda e: e.activation(out=fl(ekt), in_=fl(ekt), func=AF.Exp), reads=[Bsc], writes=[Bsc])
        p.op('dve', lambda e: e.tensor_tensor(out=bgc[:], in0=bg[:, :, 0:6], in1=eg[:], op=ALU.mult), reads=[Bbg, Bsc], writes=[Bsc])
        p.op('act', lambda e: e.activation(out=gtot[:].rearrange("p t h c -> p (t h c)"), in_=pC[:, :NT * 12], func=AF.Exp),
             reads=[pCB], writes=[Bsc])
        attn = p.sb("attn", [128, 2 * NT, 128], BF16)
        ktl = p.sb("ktl", [128, 2 * NT, 128], BF16)
        u_sb = p.sb("u_sb", [128, 2 * NT, 128], F32)
        wT = p.sb("wT", [128, 2 * NT, 128], BF16)
        Bit = [p.buf('item') for _ in range(2 * NT)]
        o_d = [p.sb("o_d%d" % d, [128, NT, 128], F32) for d in range(2)]
        Bod = [[p.buf('od') for _ in range(NT)] for _ in range(2)]
        KK = p.sb("KK", [128, 2, 128], F32)
        QK = p.sb("QK", [128, 2, 128], F32)
        BKK = p.buf('KK')
        Ug4 = p.sb("Ug4", [128, 4, 128], F32)
        BUg = p.buf('Ug')
        tmp4 = p.sb("tmp4", [128, 4, 128], F32)
        Btmp = p.buf('tmp4')
        dec4 = p.sb("dec4", [128, 4, 128], F32)
        Bdec = p.buf('dec4')
        sdec4 = p.sb("sdec4", [128, 4, 128], F32)
        Bsdec = p.buf('sdec4')
        dgb4 = p.sb("dgb4", [128, 4, 128], F32)
        Bdgb = p.buf('dgb4')
        Mt = [p.sb("M4_%d" % i, [128, 4, 128], F32) for i in range(2)]
        Nt = [p.sb("N4_%d" % i, [128, 4, 128], F32) for i in range(2)]
        BM = [p.buf('M4') for _ in range(2)]
        BN = [p.buf('N4') for _ in range(2)]
        R4 = p.sb("R4", [128, 4, 128], F32)
        BR = p.buf('R4')
        Rb4 = p.sb("Rb4", [128, 4, 128], BF16)
        BRb = p.buf('Rb4')
        kvt = [p.sb("kvt%d" % i, [128, 2, 2, 128], F32) for i in range(2)]
        Bkvt = [p.buf('kvt') for _ in range(2)]
        vb4 = p.sb("vb4", [128, 4, 128], BF16)
        kbg4 = p.sb("kbg4", [128, 4, 128], BF16)
        Bvk = p.buf('vk4')
        S32 = [p.sb("S32_%d" % d, [128, 128], F32) for d in range(2)]
        Sbf = [p.sb("Sbf_%d" % d, [128, 128], BF16) for d in range(2)]
        BS32 = [p.buf('S32') for _ in range(2)]
        BSbf = [p.buf('Sbf') for _ in range(2)]
        vn = [[p.sb("vn%d_%d" % (d, i), [128, 128], BF16) for i in range(2)] for d in range(2)]
        Bvn = [[p.buf('vn') for _ in range(2)] for _ in range(2)]
        av = [[p.sb("av%d_%d" % (d, i), [128, 128], F32) for i in range(2)] for d in range(2)]
        Bav = [[p.buf('av') for _ in range(2)] for _ in range(2)]
        osum = p.sb("osum", [128, NT, 128], F32)
        osq = p.sb("osq", [128, NT, 128], F32)
        Bos = p.buf('osum')
        ssq = p.sb("ssq", [128, NT], F32)
        zt = [p.sb("zt%d" % i, [128, 128], F32) for i in range(2)]
        Bzt = [p.buf('zt') for _ in range(2)]
        ogt = [p.sb("ogt%d" % i, [128, 128], BF16) for i in range(2)]
        Bogt = [p.buf('ogt') for _ in range(2)]
        ogT_sb = p.sb("ogT_sb", [128, MT], BF16)
        BogT = p.buf('ogT')
        Bout = p.buf('out')
        outs = []
        kvc = 0

        order = [[(16, 0), (16, 1), (17, 0), (17, 1)] + [(t, hf) for t in range(16) for hf in range(2)],
                 [(17, 1), (17, 0), (16, 1), (16, 0)] + [(t, hf) for t in range(15, -1, -1) for hf in (1, 0)]]

        for hd in range(3):
            for t0 in range(0, NT, 2):
                items = [(t0 + q // 2, q % 2) for q in range(4)]
                pK, pKB = pp.next()
                for tt in range(2):
                    tk = (t0 + tt) * 128
                    p.op('pe', lambda e, tt=tt, tk=tk, hd=hd, pK=pK: e.matmul(
                        pK[:, tt * 128:(tt + 1) * 128], lhsT=kT[:, hd, tk:tk + 128], rhs=kT[:, hd, tk:tk + 128], start=True, stop=True),
                        reads=[BkT[hd]], writes=[pKB])
                    p.op('pe', lambda e, tt=tt, tk=tk, hd=hd, pK=pK: e.matmul(
                        pK[:, 256 + tt * 128:256 + (tt + 1) * 128], lhsT=kT[:, hd, tk:tk + 128], rhs=qT[:, hd, tk:tk + 128],
                        start=True, stop=True), reads=[BkT[hd], BqT[hd]], writes=[pKB])
                p.op('act', lambda e, pK=pK: e.copy(out=KK[:].rearrange("p a b -> p (a b)"), in_=pK[:, 0:256]), reads=[pKB], writes=[BKK])
                p.op('act', lambda e, pK=pK: e.copy(out=QK[:].rearrange("p a b -> p (a b)"), in_=pK[:, 256:512]), reads=[pKB], writes=[BKK])
                for q, (t, d) in enumerate(items):
                    col = d * 3 + hd
                    p.op('dve', lambda e, q=q, t=t, d=d, col=col: e.tensor_scalar_mul(
                        out=Ug4[:, q, :], in0=TRI[d], scalar1=bg[:, t, 6 + col:7 + col]), reads=[Bc, Bbg], writes=[BUg])
                    p.op('dve', lambda e, q=q, t=t, col=col: e.tensor_scalar_mul(
                        out=dgb4[:, q, :], in0=IDENT, scalar1=bg[:, t, col:col + 1]), reads=[Bc, Bbg], writes=[Bdgb])
                pD, pDB = pp.next()
                pRw, pRwB = pp.next()
                for q in range(4):
                    p.op('pe', lambda e, q=q, pD=pD: e.matmul(pD[:, q * 128:(q + 1) * 128], lhsT=ones[:], rhs=Ug4[:, q, :],
                                                              start=True, stop=False), reads=[Bk, BUg], writes=[pDB])
                    p.op('pe', lambda e, q=q, pD=pD: e.matmul(pD[:, q * 128:(q + 1) * 128], lhsT=Ug4[:, q, :], rhs=negones[:],
                                                              start=False, stop=True), reads=[Bk, BUg], writes=[pDB])
                    p.op('pe', lambda e, q=q, pRw=pRw: e.matmul(pRw[:, q * 128:(q + 1) * 128], lhsT=ones[:], rhs=dgb4[:, q, :],
                                                                start=True, stop=True), reads=[Bk, Bdgb], writes=[pRwB])
                f4 = lambda a: a[:].rearrange("p a b -> p (a b)")
                p.op('dve', lambda e, pD=pD: e.tensor_tensor(out=f4(tmp4), in0=pD[:, :512], in1=f4(negm4), op=ALU.add),
                     reads=[pDB, Bk], writes=[Btmp])
                p.op('act', lambda e: e.activation(out=f4(dec4), in_=f4(tmp4), func=AF.Exp), reads=[Btmp], writes=[Bdec])
                p.op('dve', lambda e: e.tensor_tensor(out=f4(sdec4), in0=f4(dec4), in1=f4(strict4), op=ALU.mult),
                     reads=[Bdec, Bk], writes=[Bsdec])
                d4v = lambda a, d: a[:].rearrange("p (t d) b -> p t d b", d=2)[:, :, d, :]
                for d in range(2):
                    p.op('dve', lambda e, d=d, t0=t0: e.tensor_tensor(
                        out=attn[:, 2 * t0:2 * t0 + 4, :].rearrange("p (t d) b -> p t d b", d=2)[:, :, d, :],
                        in0=QK[:], in1=d4v(dec4, d), op=ALU.mult),
                        reads=[BKK, Bdec], writes=[Bit[2 * t0 + d], Bit[2 * t0 + 2 + d]])
                    p.op('dve', lambda e, d=d: e.tensor_tensor(out=d4v(tmp4, d), in0=KK[:], in1=d4v(sdec4, d), op=ALU.mult),
                         reads=[BKK, Bsdec, Btmp], writes=[Btmp])
                M_c, N_c, BM_c, BN_c = Mt[0], Nt[0], BM[0], BN[0]
                M_n, N_n, BM_n, BN_n = Mt[1], Nt[1], BM[1], BN[1]
                p.op('dve', lambda e, pRw=pRw, M_c=M_c: e.scalar_tensor_tensor(out=f4(M_c), in0=f4(tmp4), scalar=-1.0, in1=pRw[:, :512],
                                                                                op0=ALU.mult, op1=ALU.mult),
                     reads=[Btmp, pRwB], writes=[BM_c])
                pN, pNB = pp.next()
                for q in range(4):
                    p.op('pe', lambda e, q=q, pN=pN, M_c=M_c: e.matmul(pN[:, q * 128:(q + 1) * 128], lhsT=M_c[:, q, :], rhs=IDENT,
                                                                       start=True, stop=True), reads=[BM_c, Bc], writes=[pNB])
                p.op('act', lambda e, pN=pN, N_c=N_c: e.copy(out=f4(N_c), in_=pN[:, :512]), reads=[pNB], writes=[BN_c])
                p.op('dve', lambda e, M_c=M_c: e.tensor_tensor(out=f4(R4), in0=f4(M_c), in1=f4(ident4), op=ALU.add),
                     reads=[BM_c, Bk], writes=[BR])
                for lev in range(5):
                    pNn, pNnB = pp.next()
                    for q in range(4):
                        p.op('pe', lambda e, q=q, pNn=pNn, M_c=M_c, N_c=N_c: e.matmul(
                            pNn[:, q * 128:(q + 1) * 128], lhsT=M_c[:, q, :], rhs=N_c[:, q, :], start=True, stop=True),
                            reads=[BM_c, BN_c], writes=[pNnB])
                    if lev < 4:
                        pMn, pMnB = pp.next()
                        for q in range(4):
                            p.op('pe', lambda e, q=q, pMn=pMn, M_c=M_c, N_c=N_c: e.matmul(
                                pMn[:, q * 128:(q + 1) * 128], lhsT=N_c[:, q, :], rhs=M_c[:, q, :], start=True, stop=True),
                                reads=[BM_c, BN_c], writes=[pMnB])
                    p.op('act', lambda e, pNn=pNn, N_n=N_n: e.copy(out=f4(N_n), in_=pNn[:, :512]), reads=[pNnB], writes=[BN_n])
                    if lev < 4:
                        p.op('dve', lambda e, pMn=pMn, M_n=M_n: e.tensor_copy(out=f4(M_n), in_=pMn[:, :512]), reads=[pMnB], writes=[BM_n])
                    pR, pRB = pp.next()
                    for q in range(4):
                        p.op('pe', lambda e, q=q, pR=pR, N_n=N_n: e.matmul(
                            pR[:, q * 128:(q + 1) * 128], lhsT=N_n[:, q, :], rhs=R4[:, q, :], start=True, stop=True),
                            reads=[BN_n, BR], writes=[pRB])
                    p.op('dve', lambda e, pR=pR: e.tensor_tensor(out=f4(R4), in0=f4(R4), in1=pR[:, :512], op=ALU.add),
                         reads=[BR, pRB], writes=[BR])
                    M_c, M_n, BM_c, BM_n = M_n, M_c, BM_n, BM_c
                    N_c, N_n, BN_c, BN_n = N_n, N_c, BN_n, BN_c
                p.op('act', lambda e: e.copy(out=f4(Rb4), in_=f4(R4)), reads=[BR], writes=[BRb])
                kv_t, Bkv = kvt[kvc % 2], Bkvt[kvc % 2]
                kvc += 1
                dl = []
                for tt in range(2):
                    r0 = (t0 + tt) * 128
                    dl.append(p.dma('sp', lambda e, tt=tt, r0=r0, hd=hd, kv_t=kv_t: e.dma_start(
                        out=kv_t[:, tt, 0, :], in_=ktm[r0:r0 + 128, hd * 128:(hd + 1) * 128]), writes=[Bkv]))
                    dl.append(p.dma('sp', lambda e, tt=tt, r0=r0, hd=hd, kv_t=kv_t: e.dma_start(
                        out=kv_t[:, tt, 1, :], in_=vtm[r0:r0 + 128, hd * 128:(hd + 1) * 128]), writes=[Bkv]))
                Prog.group(dl)
                for q, (t, d) in enumerate(items):
                    col = d * 3 + hd
                    tt = t - t0
                    p.op('pool', lambda e, q=q, t=t, col=col, tt=tt, kv_t=kv_t: e.tensor_scalar_mul(
                        out=vb4[:, q, :], in0=kv_t[:, tt, 1, :], scalar1=bg[:, t, col:col + 1]), reads=[Bkv, Bbg], writes=[Bvk])
                    p.op('pool', lambda e, q=q, t=t, col=col, tt=tt, kv_t=kv_t: e.tensor_scalar_mul(
                        out=kbg4[:, q, :], in0=kv_t[:, tt, 0, :], scalar1=bgc[:, t, col:col + 1]), reads=[Bkv, Bsc], writes=[Bvk])
                    p.op('pool', lambda e, t=t, d=d, col=col, tt=tt, kv_t=kv_t: e.tensor_scalar_mul(
                        out=ktl[:, 2 * t + d, :], in0=kv_t[:, tt, 0, :], scalar1=ekt[:, t, col:col + 1]), reads=[Bkv, Bsc],
                        writes=[Bit[2 * t + d]])
                pU, pUB = pp.next()
                pW, pWB = pp.next()
                for q in range(4):
                    p.op('pe', lambda e, q=q, pU=pU: e.matmul(pU[:, q * 128:(q + 1) * 128], lhsT=Rb4[:, q, :], rhs=vb4[:, q, :],
                                                              start=True, stop=True), reads=[BRb, Bvk], writes=[pUB])
                    p.op('pe', lambda e, q=q, pW=pW: e.matmul(pW[:, q * 128:(q + 1) * 128], lhsT=kbg4[:, q, :], rhs=Rb4[:, q, :],
                                                              start=True, stop=True), reads=[BRb, Bvk], writes=[pWB])
                p.op('act', lambda e, pU=pU, t0=t0: e.copy(out=u_sb[:, 2 * t0:2 * t0 + 4, :].rearrange("p a b -> p (a b)"), in_=pU[:, :512]),
                     reads=[pUB], writes=[Bit[2 * t0 + q] for q in range(4)])
                p.op('dve', lambda e, pW=pW, t0=t0: e.tensor_copy(out=wT[:, 2 * t0:2 * t0 + 4, :].rearrange("p a b -> p (a b)"), in_=pW[:, :512]),
                     reads=[pWB], writes=[Bit[2 * t0 + q] for q in range(4)])
            for d in range(2):
                p.op('dve', lambda e, d=d: e.memset(S32[d][:], 0.0), writes=[BS32[d]])
                p.op('dve', lambda e, d=d: e.memset(Sbf[d][:], 0.0), writes=[BSbf[d]])
            for step in range(36):
                for d in range(2):
                    t, hf = order[d][step]
                    it = 2 * t + d
                    col = d * 3 + hd
                    r0 = 64 * hf
                    tk = t * 128
                    vn_t, Bv = vn[d][step % 2], Bvn[d][step % 2]
                    av_t, Ba = av[d][step % 2], Bav[d][step % 2]
                    p1, p1B = pp.next()
                    p.op('pe', lambda e, p1=p1, it=it, d=d: e.matmul(p1[:, 0:128], lhsT=wT[:, it, :], rhs=Sbf[d][:], start=True, stop=True),
                         reads=[Bit[it], BSbf[d]], writes=[p1B])
                    p.op('pe', lambda e, p1=p1, tk=tk, hd=hd, d=d: e.matmul(p1[:, 128:256], lhsT=qT[:, hd, tk:tk + 128], rhs=Sbf[d][:],
                                                                              start=True, stop=True),
                         reads=[BqT[hd], BSbf[d]], writes=[p1B])
                    p.op('dve', lambda e, p1=p1, it=it, r0=r0, vn_t=vn_t: e.tensor_tensor(
                        out=vn_t[r0:r0 + 64, :], in0=u_sb[r0:r0 + 64, it, :], in1=p1[r0:r0 + 64, 0:128], op=ALU.subtract),
                        reads=[Bit[it], p1B], writes=[Bv])
                    p2, p2B = pp.next()
                    p.op('pe', lambda e, p2=p2, it=it, r0=r0, vn_t=vn_t: e.matmul(p2[:, 0:128], lhsT=ktl[r0:r0 + 64, it, :], rhs=vn_t[r0:r0 + 64, :],
                                                                                   start=True, stop=True),
                         reads=[Bit[it], Bv], writes=[p2B])
                    p.op('pe', lambda e, p2=p2, it=it, r0=r0, vn_t=vn_t: e.matmul(p2[:, 128:256], lhsT=attn[r0:r0 + 64, it, :], rhs=vn_t[r0:r0 + 64, :],
                                                                                   start=True, stop=True),
                         reads=[Bit[it], Bv], writes=[p2B])
                    p.op('dve', lambda e, p2=p2, d=d, t=t, hf=hf, col=col: e.scalar_tensor_tensor(
                        out=S32[d][:], in0=S32[d][:], scalar=gtot[:, t, hf, col:col + 1], in1=p2[:, 0:128], op0=ALU.mult, op1=ALU.add),
                        reads=[BS32[d], Bsc, p2B], writes=[BS32[d]])
                    p.op('act', lambda e, d=d: e.copy(out=Sbf[d][:], in_=S32[d][:]), reads=[BS32[d]], writes=[BSbf[d]])
                    p.op('act', lambda e, p2=p2, r0=r0, av_t=av_t: e.copy(out=av_t[r0:r0 + 64, :], in_=p2[r0:r0 + 64, 128:256]),
                         reads=[p2B], writes=[Ba])
                    p.op('dve', lambda e, p1=p1, d=d, t=t, r0=r0, col=col, av_t=av_t: e.scalar_tensor_tensor(
                        out=o_d[d][r0:r0 + 64, t, :], in0=p1[r0:r0 + 64, 128:256], scalar=eg[r0:r0 + 64, t, col:col + 1],
                        in1=av_t[r0:r0 + 64, :], op0=ALU.mult, op1=ALU.add),
                        reads=[p1B, Bsc, Ba], writes=[Bod[d][t]])
            allod = [Bod[d][t] for d in range(2) for t in range(NT)]
            fo = lambda a: a[:].rearrange("p t b -> p (t b)")
            p.op('dve', lambda e: e.tensor_tensor(out=fo(osum), in0=fo(o_d[0]), in1=fo(o_d[1]), op=ALU.add), reads=allod, writes=[Bos])
            outs.append(p.dma('sp', lambda e, hd=hd: e.dma_start(
                out=oraw[:, hd * 128:(hd + 1) * 128].rearrange("(t p) b -> p t b", p=128), in_=osum[:]),
                reads=[Bos], writes=[p.buf('oo')], sembuf=Bout))
            p.op('dve', lambda e: e.tensor_tensor(out=fo(osq), in0=fo(osum), in1=fo(osum), op=ALU.mult), reads=[Bos], writes=[Bos])
            p.op('dve', lambda e: e.reduce_sum(out=ssq[:], in_=osq[:], axis=AX.X), reads=[Bos], writes=[Bos])
            p.op('act', lambda e: e.activation(out=ssq[:], in_=ssq[:], func=AF.Sqrt, bias=eps[:, 0:1], scale=1.0 / 128),
                 reads=[Bos, Bk], writes=[Bos])
            p.op('dve', lambda e: e.reciprocal(out=ssq[:], in_=ssq[:]), reads=[Bos], writes=[Bos])
            for t in range(NT):
                z_t, Bz = zt[t % 2], Bzt[t % 2]
                g_t, Bg = ogt[t % 2], Bogt[t % 2]
                p.dma('sp', lambda e, t=t, hd=hd, z_t=z_t: e.dma_start(out=z_t[:], in_=zsd[t * 128:(t + 1) * 128, hd * 128:(hd + 1) * 128]),
                      writes=[Bz])
                p.op('dve', lambda e, t=t, z_t=z_t, g_t=g_t: e.scalar_tensor_tensor(
                    out=g_t[:], in0=osum[:, t, :], scalar=ssq[:, t:t + 1], in1=z_t[:], op0=ALU.mult, op1=ALU.mult),
                    reads=[Bos, Bz], writes=[Bg])
                pT, pTB = pp.next()
                p.op('pe', lambda e, pT=pT, g_t=g_t: e.matmul(pT[:, 0:128], lhsT=g_t[:], rhs=identb[:], start=True, stop=True),
                     reads=[Bg, Bk], writes=[pTB])
                p.op('act', lambda e, pT=pT, t=t: e.copy(out=ogT_sb[:, t * 128:(t + 1) * 128], in_=pT[:, 0:128]), reads=[pTB], writes=[BogT])
            outs.append(p.dma('sp', lambda e, hd=hd: e.dma_start(out=ogT[hd * 128:(hd + 1) * 128, :], in_=ogT_sb[:]),
                              reads=[BogT], writes=[p.buf('oo')], sembuf=Bout))
        p.emit(final_waits=outs)
    return nc


C_ = np.ascontiguousarray


def ada_tok(ada_l, b, vs):
    a = ada_l.reshape(9, KC, 128, 5)
    a_in = np.stack([a[vs, :, :, b], a[vs, :, :, 4]], axis=0)
    return C_(a_in.transpose(3, 0, 1, 2)).reshape(128, -1)


def ln_par(g, b):
    return C_(np.stack([fm(g), fm(b)], axis=1).reshape(128, -1))


def kernel(x, c, ctx, c_ctx, w_ada, b_ada, ln_g, ln_b, ffn1_w_gu, ffn1_w_down, w_in, na_rpb, cv_conv_w,
           gd_conv_w, gd_a_log, gd_dt_bias, gd_norm_w, w_out, ffn2_w_gu, ffn2_w_down):
    f32 = lambda a: np.asarray(a, dtype=np.float32)
    x, c, ctx, c_ctx = f32(x), f32(c), f32(ctx), f32(c_ctx)
    ada = run_ada(c, c_ctx, f32(w_ada), f32(b_ada))
    cosT, sinT, perm = gdn_consts()
    cst = gdn_masks()
    xT = []
    for i in range(8):
        b, hh = i // 2, i % 2
        xT.append(C_(np.concatenate([x[b, hh * 1024:(hh + 1) * 1024].T, ctx[b, hh * 128:(hh + 1) * 128].T], axis=1)))
    na_cols, cv_cols, gd_cols, zb_cols = [], [], [], []
    for hh in range(2):
        na_cols.append(np.concatenate([np.arange(j * 768 + (3 * hh + hd) * 128, j * 768 + (3 * hh + hd + 1) * 128)
                                       for j in range(3) for hd in range(3)]))
        cv_cols.append(np.concatenate([np.arange(2304 + j * 512 + (2 * hh + gi) * 128, 2304 + j * 512 + (2 * hh + gi + 1) * 128)
                                       for j in range(3) for gi in range(2)]))
        gd_cols.append(np.concatenate([np.arange(3840 + j * 768 + (3 * hh + hd) * 128, 3840 + j * 768 + (3 * hh + hd + 1) * 128)
                                       for hd in range(3) for j in range(3)]))
        zc = np.arange(6144 + 3 * hh * 128, 6144 + (3 * hh + 3) * 128)
        bc = np.array([6912 + d * 6 + 3 * hh + hd for d in range(2) for hd in range(3)])
        zb_cols.append(np.concatenate([zc, bc, bc + 12]))
    for l in range(DEPTH):
        wi = f32(w_in[l])
        wgu, wdn = f32(ffn1_w_gu[l]), f32(ffn1_w_down[l])
        lnp = ln_par(f32(ln_g[l, 0]), f32(ln_b[l, 0]))
        maps = [{"xT": xT[i], "ada": ada_tok(ada[l], i // 2, [0, 1, 2]), "lnp": lnp, "wgu": wgu, "wdn": wdn,
                 "adah": ada_tok(ada[l], i // 2, [3, 4])} for i in range(8)]
        res = run('pf0h', maps)
        x1T = [res[i]['yT'] for i in range(8)]
        hfull = []
        for b in range(4):
            h0, h1 = res[2 * b]['hT'], res[2 * b + 1]['hT']
            hfull.append(C_(np.concatenate([h0[:, :1024], h1[:, :1024], h0[:, 1024:], h1[:, 1024:]], axis=1)))
        wna = [C_(wi[:, na_cols[hh]]) for hh in range(2)]
        tabs = [na_tables(f32(na_rpb[l])[3 * hh:3 * hh + 3]) for hh in range(2)]
        r_ma = run('ma', [{"hT": hfull[i // 2], "wna": wna[i % 2], "btab": tabs[i % 2][0], "mask": tabs[i % 2][1]} for i in range(8)])
        wcv = [C_(wi[:, cv_cols[hh]]) for hh in range(2)]
        cvw = [C_(f32(cv_conv_w[l])[:, (2 * hh) * 128:(2 * hh + 2) * 128].reshape(3, 2, 128).transpose(2, 1, 0).reshape(128, 6))
               for hh in range(2)]
        r_mc = run('mc', [{"hT": hfull[i // 2], "wcv": wcv[i % 2], "cvw": cvw[i % 2]} for i in range(8)])
        gcw = f32(gd_conv_w[l])
        wqkv = [C_(wi[:, gd_cols[hh]]) for hh in range(2)]
        wzb = [C_(wi[:, zb_cols[hh]]) for hh in range(2)]
        cw = [C_(np.stack([gcw[:, j * 768 + (3 * hh + hd) * 128:j * 768 + (3 * hh + hd + 1) * 128] for hd in range(3) for j in range(3)],
                          axis=0).transpose(2, 0, 1)).reshape(128, 45) for hh in range(2)]
        gsel = lambda a, hh: np.array([a[d, 3 * hh + hd] for d in range(2) for hd in range(3)], np.float32)
        gpar = [C_(np.broadcast_to(np.concatenate([gsel(f32(gd_a_log[l]), hh), gsel(f32(gd_dt_bias[l]), hh)])[None, :], (128, 12)))
                for hh in range(2)]
        nwr = C_(np.broadcast_to(np.tile(f32(gd_norm_w[l]), 3)[None, :], (128, 384)))
        r_g1 = run('mg1', [{"hT": hfull[i // 2], "wqkv": wqkv[i % 2], "wzb": wzb[i % 2], "cw": cw[i % 2], "gpar": gpar[i % 2],
                            "nw": nwr, "cosT": cosT, "sinT": sinT, "perm": perm} for i in range(8)])
        maps = []
        for i in range(8):
            q3 = r_g1[i]['qkvT'].reshape(3, 3, 128, MT)
            maps.append({"qkvT": r_g1[i]['qkvT'], "ktm": C_(q3[:, 1].transpose(2, 0, 1).reshape(MT, 384)),
                         "vtm": C_(q3[:, 2].transpose(2, 0, 1).reshape(MT, 384)), "zs": r_g1[i]['zs'], "bg": r_g1[i]['bg'], "cst": cst})
        r_g2 = run('mg2', maps)
        omT = []
        for i in range(8):
            b, hh = i // 2, i % 2
            om = np.concatenate([r_ma[2 * b]['ona'], r_ma[2 * b + 1]['ona'], r_mc[2 * b]['ocv'], r_mc[2 * b + 1]['ocv'],
                                 r_g2[2 * b]['ogT'], r_g2[2 * b + 1]['ogT']], axis=0)
            omT.append(C_(np.concatenate([om[:, hh * 1024:(hh + 1) * 1024], om[:, 2048 + hh * 128:2048 + (hh + 1) * 128]], axis=1)))
        wgu, wdn, wo = f32(ffn2_w_gu[l]), f32(ffn2_w_down[l]), f32(w_out[l])
        lnp = ln_par(f32(ln_g[l, 2]), f32(ln_b[l, 2]))
        lnpm = ln_par(f32(ln_g[l, 1]), f32(ln_b[l, 1]))
        maps = [{"xT": x1T[i], "ada": ada_tok(ada[l], i // 2, [6, 7, 8]), "lnp": lnp, "wgu": wgu, "wdn": wdn, "omT": omT[i], "wo": wo,
                 "adam": ada_tok(ada[l], i // 2, [5]), "lnpm": lnpm} for i in range(8)]
        res = run('pf1', maps)
        xT = [res[i]['yT'] for i in range(8)]
    out = np.empty((4, 2048, D), np.float32)
    for i in range(8):
        b, hh = i // 2, i % 2
        out[b, hh * 1024:(hh + 1) * 1024, :] = xT[i][:, :1024].T
    return out
```
